# Optimizing a Trainium2 kernel written in Bass

```python
import jax, jax.numpy as jnp
from jax import lax
import numpy as np

D_MODEL = 4096
BATCH = 8
SEQ = 2048
DEPTH = 1

D_MIX = D_MODEL
HG_WIDTH = D_MIX // 2
HG_HEAD_DIM = 128
HG_HEADS = HG_WIDTH // HG_HEAD_DIM
HG_CHUNK = 64
LRU_WIDTH = D_MIX - HG_WIDTH
LRU_BLOCKS = 16
LRU_BLOCK_DIM = LRU_WIDTH // LRU_BLOCKS
LRU_CONV = 4
LRU_C = 8.0
D_FF = 256 * ((8 * D_MODEL // 3 + 255) // 256)
FFN_CONV = 3
EPS = 1e-6
IN_WIDTHS = (HG_WIDTH, HG_WIDTH, HG_WIDTH, HG_WIDTH, LRU_WIDTH, LRU_WIDTH)
IN_TOTAL = sum(IN_WIDTHS)
IN_SPLIT = tuple(int(v) for v in np.cumsum(IN_WIDTHS)[:-1])

kernel_name = 'hymba_hgrn2_rglru_convffn_block'


def rmsnorm(x, w):
    x32 = x.astype(jnp.float32)
    y = x32 * lax.rsqrt(jnp.mean(x32 * x32, axis=-1, keepdims=True) + EPS)
    return (y * w.astype(jnp.float32)).astype(x.dtype)


def causal_dwconv(x, w, b):
    width = w.shape[0]
    seq = x.shape[1]
    xp = jnp.pad(x, ((0, 0), (width - 1, 0), (0, 0)))
    y = b + xp[:, 0:seq, :] * w[0]
    for j in range(1, width):
        y = y + xp[:, j:j + seq, :] * w[j]
    return y


def hgrn2_chunk_scan(q, k, v, g):
    bsz, seq, nh, dk = q.shape
    dv = v.shape[-1]
    n = seq // HG_CHUNK

    def to_chunks(t):
        return t.reshape(bsz, n, HG_CHUNK, nh, t.shape[-1]).transpose(1, 0, 3, 2, 4)

    causal = jnp.tril(jnp.ones((HG_CHUNK, HG_CHUNK), dtype=bool))[:, :, None]

    def step(state, inp):
        qc, kc, vc, gc = inp
        b = jnp.cumsum(gc, axis=2)
        o_inter = jnp.einsum('bhtk,bhkv->bhtv', qc * jnp.exp(b), state)
        diff = b[:, :, :, None, :] - b[:, :, None, :, :]
        decay = jnp.exp(jnp.where(causal, diff, -jnp.inf))
        scores = jnp.einsum('bhtk,bhtsk,bhsk->bhts', qc, decay, kc)
        o = o_inter + jnp.einsum('bhts,bhsv->bhtv', scores, vc)
        b_last = b[:, :, -1:, :]
        state = (jnp.exp(b_last[:, :, 0, :])[..., None] * state
                 + jnp.einsum('bhsk,bhsv->bhkv', kc * jnp.exp(b_last - b), vc))
        return state, o

    init = jnp.zeros((bsz, nh, dk, dv), jnp.float32)
    _, o = lax.scan(step, init, (to_chunks(q), to_chunks(k), to_chunks(v), to_chunks(g)))
    return o.transpose(1, 0, 3, 2, 4).reshape(bsz, seq, nh, dv)


def hgrn2_group(q_raw, f_raw, i_raw, g_raw, lb, norm_w):
    bsz, seq, _ = q_raw.shape
    q = jax.nn.silu(q_raw.astype(jnp.float32))
    f = lb + (1.0 - lb) * jax.nn.sigmoid(f_raw.astype(jnp.float32))
    k = 1.0 - f
    logf = jnp.log(f)
    v = i_raw.astype(jnp.float32)
    shp = (bsz, seq, HG_HEADS, HG_HEAD_DIM)
    o = hgrn2_chunk_scan(q.reshape(shp), k.reshape(shp), v.reshape(shp), logf.reshape(shp))
    o = rmsnorm(o, norm_w.reshape(HG_HEADS, HG_HEAD_DIM))
    return o.reshape(bsz, seq, HG_WIDTH) * jax.nn.silu(g_raw.astype(jnp.float32))


def _lin_combine(c1, c2):
    a1, b1 = c1
    a2, b2 = c2
    return a1 * a2, a2 * b1 + b2


def rglru_group(x_raw, y_raw, conv_w, conv_b, wa, ba, wx, bx, lam):
    bsz, seq, _ = x_raw.shape
    xb = causal_dwconv(x_raw, conv_w, conv_b).astype(jnp.float32)
    xblk = xb.reshape(bsz, seq, LRU_BLOCKS, LRU_BLOCK_DIM)
    r = jax.nn.sigmoid(jnp.einsum('bsnd,nde->bsne', xblk, wa).reshape(bsz, seq, LRU_WIDTH) + ba)
    i = jax.nn.sigmoid(jnp.einsum('bsnd,nde->bsne', xblk, wx).reshape(bsz, seq, LRU_WIDTH) + bx)
    log_a = -LRU_C * r * jax.nn.softplus(-lam.astype(jnp.float32))
    a = jnp.exp(log_a)
    mult = jnp.sqrt(-jnp.expm1(2.0 * log_a))
    mult = mult.at[:, 0].set(1.0)
    u = xb * i * mult
    _, h = lax.associative_scan(_lin_combine, (a, u), axis=1)
    return h * jax.nn.gelu(y_raw.astype(jnp.float32))


def setup_inputs(seed: int = 0) -> dict:
    key = jax.random.key(seed)
    ks = jax.random.split(key, 20)
    f32 = jnp.float32
    nrm = lambda k, shp, s: (jax.random.normal(k, shp, f32) * s)
    x = jax.random.normal(ks[0], (BATCH, SEQ, D_MODEL), f32)
    ln1_w = 1.0 + nrm(ks[1], (DEPTH, D_MODEL), 0.02)
    w_in = nrm(ks[2], (DEPTH, D_MODEL, IN_TOTAL), D_MODEL ** -0.5)
    lb_gamma = nrm(ks[3], (DEPTH + 1, HG_WIDTH), 0.5)
    hg_norm_w = 1.0 + nrm(ks[4], (DEPTH, HG_WIDTH), 0.02)
    lru_conv_w = nrm(ks[5], (DEPTH, LRU_CONV, LRU_WIDTH), LRU_CONV ** -0.5)
    lru_conv_b = nrm(ks[6], (DEPTH, LRU_WIDTH), 0.02)
    lru_wa = nrm(ks[7], (DEPTH, LRU_BLOCKS, LRU_BLOCK_DIM, LRU_BLOCK_DIM), LRU_BLOCK_DIM ** -0.5)
    lru_ba = nrm(ks[8], (DEPTH, LRU_WIDTH), 0.1)
    lru_wx = nrm(ks[9], (DEPTH, LRU_BLOCKS, LRU_BLOCK_DIM, LRU_BLOCK_DIM), LRU_BLOCK_DIM ** -0.5)
    lru_bx = nrm(ks[10], (DEPTH, LRU_WIDTH), 0.1)
    a_c = jax.random.uniform(ks[11], (DEPTH, LRU_WIDTH), f32, 0.9, 0.999)
    a0 = a_c ** (1.0 / LRU_C)
    lru_lambda = jnp.log(a0) - jnp.log1p(-a0)
    lru_norm_w = 1.0 + nrm(ks[12], (DEPTH, LRU_WIDTH), 0.02)
    w_out = nrm(ks[13], (DEPTH, D_MIX, D_MODEL), D_MIX ** -0.5)
    ln2_w = 1.0 + nrm(ks[14], (DEPTH, D_MODEL), 0.02)
    ffn_w_up = nrm(ks[15], (DEPTH, D_MODEL, 2 * D_FF), D_MODEL ** -0.5)
    ffn_conv_w = nrm(ks[16], (DEPTH, FFN_CONV, 2 * D_FF), FFN_CONV ** -0.5)
    ffn_conv_b = nrm(ks[17], (DEPTH, 2 * D_FF), 0.02)
    ffn_w_down = nrm(ks[18], (DEPTH, D_FF, D_MODEL), D_FF ** -0.5)
    final_norm_w = 1.0 + nrm(ks[19], (D_MODEL,), 0.02)
    return {'x': x, 'ln1_w': ln1_w, 'w_in': w_in, 'lb_gamma': lb_gamma,
            'hg_norm_w': hg_norm_w, 'lru_conv_w': lru_conv_w, 'lru_conv_b': lru_conv_b,
            'lru_wa': lru_wa, 'lru_ba': lru_ba, 'lru_wx': lru_wx, 'lru_bx': lru_bx,
            'lru_lambda': lru_lambda, 'lru_norm_w': lru_norm_w, 'w_out': w_out,
            'ln2_w': ln2_w, 'ffn_w_up': ffn_w_up, 'ffn_conv_w': ffn_conv_w,
            'ffn_conv_b': ffn_conv_b, 'ffn_w_down': ffn_w_down, 'final_norm_w': final_norm_w}


def reference(x, ln1_w, w_in, lb_gamma, hg_norm_w, lru_conv_w, lru_conv_b, lru_wa, lru_ba,
              lru_wx, lru_bx, lru_lambda, lru_norm_w, w_out, ln2_w, ffn_w_up, ffn_conv_w,
              ffn_conv_b, ffn_w_down, final_norm_w):
    lb_all = jnp.cumsum(jax.nn.softmax(lb_gamma.astype(jnp.float32), axis=0), axis=0)
    h = x
    for l in range(DEPTH):
        hn = rmsnorm(h, ln1_w[l])
        proj = jnp.einsum('bsd,de->bse', hn, w_in[l])
        q_r, f_r, i_r, g_r, x_r, y_r = jnp.split(proj, IN_SPLIT, axis=-1)
        o_hg = hgrn2_group(q_r, f_r, i_r, g_r, lb_all[l], hg_norm_w[l])
        o_lru = rglru_group(x_r, y_r, lru_conv_w[l], lru_conv_b[l], lru_wa[l], lru_ba[l],
                            lru_wx[l], lru_bx[l], lru_lambda[l])
        o_lru = rmsnorm(o_lru, lru_norm_w[l])
        mix = jnp.concatenate([o_hg, o_lru], axis=-1).astype(h.dtype)
        h = h + jnp.einsum('bse,ed->bsd', mix, w_out[l])
        hn = rmsnorm(h, ln2_w[l])
        up = jnp.einsum('bsd,df->bsf', hn, ffn_w_up[l])
        up = causal_dwconv(up, ffn_conv_w[l], ffn_conv_b[l])
        gate, val = jnp.split(up, [D_FF], axis=-1)
        h = h + jnp.einsum('bsf,fd->bsd', jax.nn.silu(gate) * val, ffn_w_down[l])
    return rmsnorm(h, final_norm_w)
```

```python
import numpy as np
import ml_dtypes
from contextlib import ExitStack
import concourse.bass as bass
import concourse.mybir as mybir
from concourse.bass_utils import run_bass_kernel_spmd

F32 = mybir.dt.float32
BF16 = mybir.dt.bfloat16
AF = mybir.ActivationFunctionType
ALU = mybir.AluOpType

D = 4096
SEQ = 2048
T = 512
NT = SEQ // T
KC = D // 128
NH = 16
NL = 16
DFF = 11008
NFF = DFF // 128
GRP = 16
EPS = 1e-6
NB = 3
NG = 4

V_LN1, V_LN2, V_FIN, V_G0, V_G1, V_HGW = 0, 32, 64, 96, 112, 128
V_CW, V_CB, V_BA, V_BX, V_LAM, V_LRUW = 144, 208, 224, 240, 256, 272
V_FCW, V_FCB, NV = 288, 804, 976
DV_LB, DV_NOML, DV_LNOML, DV_C8, DV_C16, DV_SCR, DV_BAH, DV_BXH, DV_C8H, DV_S1, DV_B1, DV_LNH, NDV = 0, 16, 32, 48, 64, 80, 96, 112, 128, 144, 160, 176, 192

ENGS = ["pe", "act", "dve", "pool", "sp"]


class Prog:
    def __init__(self):
        self.ops = {e: [] for e in ENGS}
        self.cnt = {}
        self.seen = {e: {} for e in ENGS}
        self.lastw = {}
        self.readers = {}

    def op(self, eng, fn, reads=(), writes=(), signal=True, dma_sem=None, pre_r=(), pre_w=()):
        own = "S_" + eng
        deps = []
        for r in list(reads) + list(pre_r):
            if r in self.lastw:
                deps.append(self.lastw[r])
        for w in list(writes) + list(pre_w):
            if w in self.lastw:
                deps.append(self.lastw[w])
            deps += self.readers.get(w, [])
        waits = {}
        for (s, v) in deps:
            if eng == "pe" and s == own:
                continue
            if self.seen[eng].get(s, 0) < v:
                waits[s] = max(waits.get(s, 0), v)
        for s, v in waits.items():
            self.seen[eng][s] = v
        if dma_sem is not None:
            self.cnt[dma_sem] = self.cnt.get(dma_sem, 0) + 16
            tag = (dma_sem, self.cnt[dma_sem])
            inc = (dma_sem, 16)
        elif signal:
            self.cnt[own] = self.cnt.get(own, 0) + 1
            tag = (own, self.cnt[own])
            inc = (own, 1)
        else:
            tag = (own, self.cnt.get(own, 0) + 1)
            inc = None
        for r in reads:
            self.readers.setdefault(r, []).append(tag)
        for w in writes:
            self.lastw[w] = tag
            self.readers[w] = []
        self.ops[eng].append((waits, fn, inc))

    def final_waits(self, eng):
        waits = {}
        for s, v in self.cnt.items():
            if self.seen[eng].get(s, 0) < v:
                waits[s] = v
        self.ops[eng].append((waits, None, None))

    def emit(self, nc, sems):
        with nc.Block() as block:
            def run(engname, e):
                for (waits, fn, inc) in self.ops[engname]:
                    for s, v in waits.items():
                        e.wait_ge(sems[s], v)
                    if fn is None:
                        continue
                    ins = fn(e)
                    if inc is not None:
                        ins.then_inc(sems[inc[0]], inc[1])

            @block.tensor
            def _(e):
                run("pe", e)

            @block.scalar
            def _(e):
                run("act", e)

            @block.vector
            def _(e):
                run("dve", e)

            @block.gpsimd
            def _(e):
                run("pool", e)

            @block.sync
            def _(e):
                run("sp", e)


class Pipe:
    def __init__(self):
        self.blk = 0
        self.pending = []

    def defer(self, delay, fn):
        self.pending.append((self.blk + delay, fn))

    def flush(self):
        while True:
            due = [p for p in self.pending if p[0] <= self.blk]
            if not due:
                return
            p = due[0]
            self.pending.remove(p)
            p[1]()

    def tick(self):
        self.blk += 1
        self.flush()

    def drain(self):
        while self.pending:
            self.blk = max(self.blk, min(p[0] for p in self.pending))
            self.flush()


def build_program(ntiles=NT, maxblocks=None, small=False):
    nc = bass.Bass("TRN2", target_bir_lowering=False)
    x_d = nc.dram_tensor("x", [128, KC, SEQ], F32, kind="ExternalInput").ap()
    out_d = nc.dram_tensor("out", [128, KC, SEQ], F32, kind="ExternalOutput").ap()
    win_d = nc.dram_tensor("w_in", [96, 128, KC * 128], F32, kind="ExternalInput").ap()
    if not small:
        wout_d = nc.dram_tensor("w_out", [32, 128, KC * 128], F32, kind="ExternalInput").ap()
        wupg_d = nc.dram_tensor("w_upg", [NFF, 128, KC * 128], F32, kind="ExternalInput").ap()
        wupv_d = nc.dram_tensor("w_upv", [NFF, 128, KC * 128], F32, kind="ExternalInput").ap()
        wdn_d = nc.dram_tensor("w_dn", [32, 128, NFF * 128], F32, kind="ExternalInput").ap()
    wlru_d = nc.dram_tensor("w_lru", [NL, 128, 256], F32, kind="ExternalInput").ap()
    vecs_d = nc.dram_tensor("vecs", [128, NV], F32, kind="ExternalInput").ap()
    cstb_d = nc.dram_tensor("cstb", [128, 256], BF16, kind="ExternalInput").ap()
    cstf_d = nc.dram_tensor("cstf", [128, 1024], F32, kind="ExternalInput").ap()

    P = Prog()
    pipe = Pipe()
    with ExitStack() as st:
        def sb(name, shape, dt):
            return st.enter_context(nc.sbuf_tensor(name, shape, dt))

        def ps(name, shape, dt):
            return st.enter_context(nc.psum_tensor(name, shape, dt))

        h = sb("h", [128, KC, T], F32)
        hn = sb("hn", [128, KC, T], BF16)
        mix = sb("mix", [128, KC, T], BF16)
        wb = [sb(f"wb{i}", [128, KC, 128], BF16) for i in range(NB)]
        lw = [sb(f"lw{i}", [128, 2, 128], BF16) for i in range(2)]
        vecs = sb("vecs_sb", [128, NV], F32)
        dv = sb("dv", [128, NDV], F32)
        cstb = sb("cstb_sb", [128, 256], BF16)
        cstf = sb("cstf_sb", [128, 1024], F32)
        Sst = sb("Sst", [128, NH, 128], F32)
        lstate = sb("lstate", [128, NL], F32)
        lc = sb("lc", [128, NL, 3], F32)
        fc = sb("fc", [128, 2 * NFF, 2], F32)
        Ft = [sb(f"F{i}", [128, 516], F32) for i in range(8)]
        GS = [sb(f"GS{i}", [128, T], F32) for i in range(2)]
        Bt = [sb(f"B{i}", [128, T], BF16) for i in range(7)]
        smid = sb("smid", [128, 8, 128], BF16)
        SQ = [sb(f"SQ{i}", [128, T], BF16) for i in range(2)]
        PT = [sb(f"PT{i}", [128, 128], F32) for i in range(4)]
        rstd2 = sb("rstd2", [128, T], F32)

        G = [ps(f"G{i}", [128, T], F32) for i in range(NG)]
        pST = ps("pST", [128, T], F32)
        pOT = ps("pOT", [128, T], F32)
        pTR = ps("pTR", [128, 2, 4, 128], BF16)
        pP = ps("pP", [128, 4, 128], F32)

        ident = cstb[:, 0:128]
        ones = cstb[:, 128:256]
        maskm = cstf[:, 0:512]
        mask4 = cstf[:, 512:1024]

        def vcol(off):
            return vecs[:, off:off + 1]

        def dcol(off):
            return dv[:, off:off + 1]

        P.op("sp", lambda e: e.dma_start(out=vecs[:], in_=vecs_d), writes=["vecs"], dma_sem="D_m0")
        P.op("sp", lambda e: e.dma_start(out=cstb[:], in_=cstb_d), writes=["cstb"], dma_sem="D_m1")
        P.op("sp", lambda e: e.dma_start(out=cstf[:], in_=cstf_d), writes=["cstf"], dma_sem="D_m2")
        P.op("dve", lambda e: e.memset(Sst[:], 0.0), writes=[("S", i) for i in range(NH)])
        P.op("dve", lambda e: e.memset(lstate[:], 0.0), writes=["lstate"])
        P.op("dve", lambda e: e.memset(lc[:], 0.0), writes=["lc"])
        P.op("dve", lambda e: e.memset(fc[:], 0.0), writes=["fc"])
        P.op("dve", lambda e: e.memset(smid[:], 0.5), writes=[("smid", c) for c in range(8)])
        P.op("dve", lambda e: e.memset(Bt[3][:], 0.0), writes=["B3"])
        P.op("dve", lambda e: e.memset(Bt[6][:], 0.0), writes=["B6"])
        P.op("dve", lambda e: e.tensor_tensor(out=dv[:, DV_SCR:DV_SCR + 16], in0=vecs[:, V_G0:V_G0 + 16],
                                              in1=vecs[:, V_G1:V_G1 + 16], op=ALU.subtract),
             reads=["vecs"], writes=["dvs"])
        P.op("act", lambda e: e.activation(out=dv[:, DV_LB:DV_LB + 16], in_=dv[:, DV_SCR:DV_SCR + 16], func=AF.Sigmoid),
             reads=["dvs"], writes=["dv_lb"])
        P.op("dve", lambda e: e.tensor_scalar_add(out=dv[:, DV_NOML:DV_NOML + 16], in0=dv[:, DV_LB:DV_LB + 16], scalar1=-1.0),
             reads=["dv_lb"], writes=["dv_noml"])
        P.op("act", lambda e: e.activation(out=dv[:, DV_LNOML:DV_LNOML + 16], in_=dv[:, DV_LB:DV_LB + 16], func=AF.Ln,
                                           scale=-1.0, bias=1.0),
             reads=["dv_lb"], writes=["dv_lnoml"])
        P.op("act", lambda e: e.activation(out=dv[:, DV_SCR:DV_SCR + 16], in_=vecs[:, V_LAM:V_LAM + 16], func=AF.Exp, scale=-1.0),
             reads=["vecs", "dvs"], writes=["dvs"])
        P.op("act", lambda e: e.activation(out=dv[:, DV_SCR:DV_SCR + 16], in_=dv[:, DV_SCR:DV_SCR + 16], func=AF.Ln, bias=1.0),
             reads=["dvs"], writes=["dvs"])
        P.op("dve", lambda e: e.tensor_scalar_mul(out=dv[:, DV_C8:DV_C8 + 16], in0=dv[:, DV_SCR:DV_SCR + 16], scalar1=-8.0),
             reads=["dvs"], writes=["dv_c8"])
        P.op("dve", lambda e: e.tensor_scalar_mul(out=dv[:, DV_C16:DV_C16 + 16], in0=dv[:, DV_SCR:DV_SCR + 16], scalar1=-16.0),
             reads=["dvs"], writes=["dv_c16"])
        P.op("dve", lambda e: e.tensor_scalar_mul(out=dv[:, DV_C8H:DV_C8H + 16], in0=dv[:, DV_SCR:DV_SCR + 16], scalar1=-4.0),
             reads=["dvs"], writes=["dv_c8h"])
        P.op("dve", lambda e: e.tensor_scalar_mul(out=dv[:, DV_BAH:DV_BAH + 16], in0=vecs[:, V_BA:V_BA + 16], scalar1=0.5),
             reads=["vecs"], writes=["dv_bah"])
        P.op("dve", lambda e: e.tensor_scalar_mul(out=dv[:, DV_BXH:DV_BXH + 16], in0=vecs[:, V_BX:V_BX + 16], scalar1=0.5),
             reads=["vecs"], writes=["dv_bxh"])
        P.op("dve", lambda e: e.tensor_scalar_mul(out=dv[:, DV_S1:DV_S1 + 16], in0=dv[:, DV_NOML:DV_NOML + 16], scalar1=0.5),
             reads=["dv_noml"], writes=["dv_s1"])
        P.op("dve", lambda e: e.tensor_scalar_add(out=dv[:, DV_B1:DV_B1 + 16], in0=dv[:, DV_S1:DV_S1 + 16], scalar1=1.0),
             reads=["dv_s1"], writes=["dv_b1"])
        P.op("act", lambda e: e.activation(out=dv[:, DV_LNH:DV_LNH + 16], in_=dv[:, DV_LB:DV_LB + 16], func=AF.Ln,
                                           scale=-0.5, bias=0.5),
             reads=["dv_lb"], writes=["dv_lnh"])
        CONST = ["vecs", "cstb", "cstf", "dv_lb", "dv_noml", "dv_lnoml", "dv_c8", "dv_c16"]

        sqc = [0]

        def rmsnorm_to_hn(wcol):
            for c in range(KC):
                s = sqc[0] % 2
                sqc[0] += 1
                P.op("act", lambda e, c=c, s=s: e.activation(out=SQ[s][:], in_=h[:, c, :], func=AF.Square),
                     reads=[("h", c)], writes=[("SQ", s)])
                P.op("pe", lambda e, c=c, s=s: e.matmul(pST[:], lhsT=ones, rhs=SQ[s][:], start=(c == 0), stop=(c == KC - 1)),
                     reads=[("SQ", s), "cstb"], writes=["pST"], signal=True)
                pe_filler(2)
            P.op("act", lambda e: e.activation(out=Ft[2][:, 0:T], in_=pST[:], func=AF.Ln, scale=1.0 / D, bias=EPS),
                 reads=["pST"], writes=["F2"])
            P.op("act", lambda e: e.activation(out=Ft[4][:, 0:T], in_=Ft[2][:, 0:T], func=AF.Exp, scale=-0.5),
                 reads=["F2"], writes=["F4"])
            for c in range(KC):
                P.op("dve", lambda e, c=c: e.scalar_tensor_tensor(out=hn[:, c, :], in0=h[:, c, :], scalar=vcol(wcol + c),
                                                                 in1=Ft[4][:, 0:T], op0=ALU.mult, op1=ALU.mult),
                     reads=[("h", c), "F4", "vecs"], writes=[("hn", c)])

        blocks = []

        def issue_wdma(i):
            b = blocks[i]
            buf = i % NB
            if "dma" in b:
                P.op("pool", lambda e, b=b, buf=buf: b["dma"](e, wb[buf]), writes=[("wb", buf)], dma_sem=f"D_w{buf}")
                return
            nk = b["nk"]
            P.op("pool", lambda e, b=b, buf=buf, nk=nk: e.dma_start(
                out=wb[buf][:, 0:nk, :], in_=b["src"]), writes=[("wb", buf)], dma_sem=f"D_w{buf}")

        bankc = [0]

        def run_blocks():
            for i in range(min(NB - 1, len(blocks))):
                issue_wdma(i)
            for i, b in enumerate(blocks):
                if b.get("pre"):
                    b["pre"]()
                if i + NB - 1 < len(blocks):
                    issue_wdma(i + NB - 1)
                buf = i % NB
                subs = b.get("subs") or [dict(k0=0, nk=b["nk"], rhs=b["rhs"], post=b["post"], nosig=b.get("nosig", False))]
                for si, sub in enumerate(subs):
                    bank = bankc[0] % NG
                    bankc[0] += 1
                    nk, k0 = sub["nk"], sub["k0"]
                    for k in range(nk):
                        rap, rres = sub["rhs"](k)
                        pre_r, pre_w = [], []
                        if k == nk - 1:
                            pre_w.append(("G", bankc[0] % NG))
                            if si == len(subs) - 1 and i + 1 < len(blocks):
                                pre_r.append(("wb", (i + 1) % NB))
                        P.op("pe", lambda e, buf=buf, bank=bank, k=k, rap=rap, nk=nk, k0=k0: e.matmul(
                            G[bank][:], lhsT=wb[buf][:, k0 + k, :], rhs=rap, start=(k == 0), stop=(k == nk - 1)),
                            reads=[("wb", buf), rres], writes=[("G", bank)], signal=(k == nk - 1 and not sub.get("nosig")),
                            pre_r=pre_r, pre_w=pre_w)
                        if b.get("fill") and k < nk - 1:
                            pe_filler(1)
                    sub["post"](bank)
                pipe.tick()

        def pe_filler(n, bank_off=0):
            bank = (bankc[0] + bank_off) % NG
            for _ in range(n):
                P.op("pe", lambda e, bank=bank: e.matmul(G[bank][:], lhsT=ident, rhs=smid[:, 0:4, :].rearrange("p a b -> p (a b)"),
                                                         start=True, stop=True),
                     reads=["cstb"] + [("smid", c) for c in range(4)], writes=[("G", bank)], signal=False)

        def rhs_hn(k):
            return hn[:, k, :], ("hn", k)

        def rhs_mix(k):
            return mix[:, k, :], ("mix", k)

        for t in range(ntiles):
            t0 = t * T

            def tile_start(t=t, t0=t0):
                for g8 in range(8):
                    P.op("sp", lambda e, g8=g8: e.dma_start(out=h[:, 4 * g8:4 * g8 + 4, :], in_=x_d[:, 4 * g8:4 * g8 + 4, t0:t0 + T]),
                         writes=[("h", c) for c in range(4 * g8, 4 * g8 + 4)], dma_sem=f"D_h{g8}")
                rmsnorm_to_hn(V_LN1)

            lru_sq = {}

            def mk_post_x(n, t=t):
                def post_x(bank):
                    l = n % 2
                    par = n % 2
                    P.op("pool", lambda e: e.dma_start(out=lw[l][:], in_=wlru_d[n].rearrange("p (a m) -> p a m", m=128)),
                         writes=[("lw", l)], dma_sem=f"D_lw{l}")
                    xs = Ft[0]
                    xb = GS[par]
                    P.op("dve", lambda e: e.tensor_copy(out=xs[:, 0:3], in_=lc[:, n, :]), reads=["lc"], writes=["F0"])
                    P.op("act", lambda e: e.activation(out=xs[:, 3:3 + T], in_=G[bank][:], func=AF.Copy),
                         reads=[("G", bank)], writes=["F0"])
                    P.op("dve", lambda e: e.tensor_copy(out=lc[:, n, :], in_=xs[:, T:T + 3]), reads=["F0"], writes=["lc"])
                    P.op("dve", lambda e: e.tensor_scalar(out=xb[:], in0=xs[:, 3:3 + T], scalar1=vcol(V_CW + 3 * 16 + n),
                                                          scalar2=vcol(V_CB + n), op0=ALU.mult, op1=ALU.add),
                         reads=["F0", "vecs"], writes=[("GS", par)])
                    for j in (2, 1, 0):
                        P.op("dve", lambda e, j=j: e.scalar_tensor_tensor(out=xb[:], in0=xs[:, j:j + T],
                                                                          scalar=vcol(V_CW + j * 16 + n), in1=xb[:],
                                                                          op0=ALU.mult, op1=ALU.add),
                             reads=["F0", ("GS", par), "vecs"], writes=[("GS", par)])
                    if n == 0:
                        P.op("act", lambda e: e.activation(out=Bt[par][:], in_=xb[:], func=AF.Copy),
                             reads=[("GS", par)], writes=[f"B{par}"])
                return post_x

            def mk_post_y(n, t=t):
                def post_y(bank):
                    l = n % 2
                    par = n % 2
                    P.op("act", lambda e: e.activation(out=Ft[7][:, 0:T], in_=G[bank][:], func=AF.Gelu_apprx_tanh),
                         reads=[("G", bank)], writes=["F7"])

                    def ssq_mm(nn):
                        s_ = lru_sq[nn]
                        P.op("pe", lambda e: e.matmul(pP[:].rearrange("p a b -> p (a b)"), lhsT=ones, rhs=SQ[s_][:],
                                                      start=(nn == 0), stop=(nn == NL - 1)),
                             reads=[("SQ", s_), "cstb"], writes=["pP"])

                    def fin():
                        ssq_mm(NL - 1)
                        P.op("act", lambda e: e.activation(out=Ft[3][:, 0:T], in_=pP[:].rearrange("p a b -> p (a b)"),
                                                           func=AF.Ln, scale=1.0 / 2048.0, bias=EPS),
                             reads=["pP"], writes=["F3"])
                        P.op("act", lambda e: e.activation(out=Ft[5][:, 0:T], in_=Ft[3][:, 0:T], func=AF.Exp, scale=-0.5),
                             reads=["F3"], writes=["F5"])
                        for nn in range(NL):
                            P.op("dve", lambda e, nn=nn: e.tensor_tensor(out=mix[:, NH + nn, :], in0=mix[:, NH + nn, :],
                                                                         in1=Ft[5][:, 0:T], op=ALU.mult),
                                 reads=[("mix", NH + nn), "F5"], writes=[("mix", NH + nn)])

                    def stage1():
                        if n > 0:
                            ssq_mm(n - 1)
                        P.op("pe", lambda e: e.matmul(pST[:], lhsT=lw[l][:, 0, :], rhs=Bt[par][:], start=True, stop=True),
                             reads=[("lw", l), f"B{par}"], writes=["pST"])
                        P.op("pe", lambda e: e.matmul(pOT[:], lhsT=lw[l][:, 1, :], rhs=Bt[par][:], start=True, stop=True),
                             reads=[("lw", l), f"B{par}"], writes=["pOT"])
                        thr, thi, a, a2 = Ft[2], Ft[3], Ft[4], Ft[5]
                        P.op("act", lambda e: e.activation(out=thr[:, 0:T], in_=pST[:], func=AF.Tanh, scale=0.5, bias=dcol(DV_BAH + n)),
                             reads=["pST", "dv_bah"], writes=["F2"])
                        P.op("act", lambda e: e.activation(out=thi[:, 0:T], in_=pOT[:], func=AF.Tanh, scale=0.5, bias=dcol(DV_BXH + n)),
                             reads=["pOT", "dv_bxh"], writes=["F3"])
                        P.op("act", lambda e: e.activation(out=a[:, 0:T], in_=thr[:, 0:T], func=AF.Exp, scale=dcol(DV_C8H + n),
                                                           bias=dcol(DV_C8H + n)),
                             reads=["F2", "dv_c8h"], writes=["F4"])
                        P.op("act", lambda e: e.activation(out=a2[:, 0:T], in_=thr[:, 0:T], func=AF.Exp, scale=dcol(DV_C8 + n),
                                                           bias=dcol(DV_C8 + n)),
                             reads=["F2", "dv_c8"], writes=["F5"])
                        P.op("act", lambda e: e.activation(out=a2[:, 0:T], in_=a2[:, 0:T], func=AF.Sqrt, scale=-1.0, bias=1.0),
                             reads=["F5"], writes=["F5"])
                        if n + 1 < NL:
                            P.op("act", lambda e: e.activation(out=Bt[1 - par][:], in_=GS[1 - par][:], func=AF.Copy),
                                 reads=[("GS", 1 - par)], writes=[f"B{1 - par}"])
                        if t == 0:
                            P.op("dve", lambda e: e.memset(a2[:, 0:1], 1.0), reads=["F5"], writes=["F5"])
                        u = GS[par]
                        P.op("dve", lambda e: e.scalar_tensor_tensor(out=u[:], in0=thi[:, 0:T], scalar=1.0, in1=u[:],
                                                                     op0=ALU.add, op1=ALU.mult),
                             reads=[("GS", par), "F3"], writes=[("GS", par)])
                        P.op("dve", lambda e: e.tensor_tensor(out=u[:], in0=u[:], in1=a2[:, 0:T], op=ALU.mult),
                             reads=[("GS", par), "F5"], writes=[("GS", par)])
                        hst = Ft[6]
                        P.op("dve", lambda e: e.tensor_tensor_scan(out=hst[:, 0:T], data0=a[:, 0:T], data1=u[:],
                                                                   initial=lstate[:, n:n + 1], op0=ALU.mult, op1=ALU.add),
                             reads=["F4", ("GS", par), "lstate"], writes=["F6"])
                        P.op("dve", lambda e: e.tensor_copy(out=lstate[:, n:n + 1], in_=hst[:, T - 1:T]),
                             reads=["F6"], writes=["lstate"])
                        o = Ft[2]
                        P.op("dve", lambda e: e.scalar_tensor_tensor(out=o[:, 0:T], in0=hst[:, 0:T], scalar=0.5, in1=Ft[7][:, 0:T],
                                                                     op0=ALU.mult, op1=ALU.mult),
                             reads=["F6", "F7", "F2"], writes=["F2"])
                        s_ = sqc[0] % 2
                        sqc[0] += 1
                        lru_sq[n] = s_
                        P.op("act", lambda e: e.activation(out=SQ[s_][:], in_=o[:, 0:T], func=AF.Square),
                             reads=["F2"], writes=[("SQ", s_)])
                        P.op("act", lambda e: e.activation(out=mix[:, NH + n, :], in_=o[:, 0:T], func=AF.Copy,
                                                           scale=vcol(V_LRUW + n)),
                             reads=["F2", "vecs"], writes=[("mix", NH + n)])
                        if n == NL - 1:
                            pipe.defer(1, fin)
                    pipe.defer(1, stage1)
                return post_y

            def lru_blk(which, n):
                off = 64 if which == 0 else 80
                return dict(src=win_d[off + n].rearrange("p (k m) -> p k m", m=128), nk=KC, rhs=rhs_hn,
                            post=(mk_post_x(n) if which == 0 else mk_post_y(n)), nosig=(which == 1))

            b0 = lru_blk(0, 0)
            b0["fill"] = True
            b0["pre"] = tile_start
            blocks.append(b0)
            for n in range(NL):
                if n + 1 < NL:
                    blocks.append(lru_blk(0, n + 1))
                blocks.append(lru_blk(1, n))

            for hh in range(NH):
                def post_q(bank, hh=hh):
                    P.op("act", lambda e: e.activation(out=Ft[0][:, 0:T], in_=G[bank][:], func=AF.Silu),
                         reads=[("G", bank)], writes=["F0"])

                def post_f(bank, hh=hh):
                    sgn, g, b, dd, E1, E2, eb = Ft[1], Ft[2], Ft[3], Ft[4], Ft[5], Ft[6], Ft[7]
                    P.op("act", lambda e: e.activation(out=sgn[:, 0:T], in_=G[bank][:], func=AF.Tanh, scale=-0.5),
                         reads=[("G", bank)], writes=["F1"])
                    P.op("act", lambda e: e.activation(out=g[:, 0:T], in_=sgn[:, 0:T], func=AF.Ln, scale=dcol(DV_S1 + hh),
                                                       bias=dcol(DV_B1 + hh)),
                         reads=["F1", "dv_s1", "dv_b1"], writes=["F2"])
                    P.op("dve", lambda e: e.tensor_tensor_scan(out=b[:, 0:T], data0=maskm, data1=g[:, 0:T], initial=0.0,
                                                               op0=ALU.mult, op1=ALU.add),
                         reads=["F2", "cstf"], writes=["F3"])
                    bv = b[:, 0:T].rearrange("p (c t) -> p c t", t=64)
                    P.op("dve", lambda e: e.tensor_tensor(out=dd[:, 0:T].rearrange("p (c t) -> p c t", t=64), in0=bv,
                                                          in1=bv[:, :, 31:32].to_broadcast([128, 8, 64]), op=ALU.subtract),
                         reads=["F3"], writes=["F4"])
                    P.op("act", lambda e: e.activation(out=E1[:, 0:T], in_=dd[:, 0:T], func=AF.Exp), reads=["F4"], writes=["F5"])
                    P.op("act", lambda e: e.activation(out=E2[:, 0:T], in_=dd[:, 0:T], func=AF.Exp, scale=-1.0,
                                                       bias=dcol(DV_LNH + hh)),
                         reads=["F4", "dv_lnh"], writes=["F6"])
                    P.op("act", lambda e: e.activation(out=eb[:, 0:T], in_=b[:, 0:T], func=AF.Exp), reads=["F3"], writes=["F7"])
                    P.op("dve", lambda e: e.tensor_tensor(out=Bt[0][:], in0=Ft[0][:, 0:T], in1=E1[:, 0:T], op=ALU.mult),
                         reads=["F0", "F5"], writes=["B0"])
                    P.op("dve", lambda e: e.scalar_tensor_tensor(out=Bt[1][:], in0=sgn[:, 0:T], scalar=1.0, in1=E2[:, 0:T],
                                                                 op0=ALU.add, op1=ALU.mult),
                         reads=["F1", "F6"], writes=["B1"])

                def post_i(bank, hh=hh):
                    P.op("act", lambda e: e.activation(out=Bt[2][:], in_=G[bank][:], func=AF.Copy),
                         reads=[("G", bank)], writes=["B2"])

                def pre_g(hh=hh):
                    ktv = Bt[3][:].rearrange("p (j k) -> p j k", k=128)
                    ktv2 = Bt[6][:].rearrange("p (j k) -> p j k", k=128)
                    for j in range(4):
                        P.op("pe", lambda e, j=j: e.transpose(out=pTR[:, 0, j, :], in_=Bt[1][:, j * 128:(j + 1) * 128], identity=ident),
                             reads=["B1", "cstb"], writes=["pTR"], signal=(j == 3))
                    P.op("act", lambda e: e.activation(out=ktv[0:64], in_=pTR[0:64, 0, :, :], func=AF.Copy), reads=["pTR"], writes=["B3"])
                    P.op("act", lambda e: e.activation(out=ktv2[64:128], in_=pTR[64:128, 0, :, :], func=AF.Copy), reads=["pTR"], writes=["B6"])
                    for j in range(4):
                        P.op("pe", lambda e, j=j: e.matmul(pST[:, j * 128:(j + 1) * 128], lhsT=Bt[1][:, j * 128:(j + 1) * 128],
                                                           rhs=Bt[0][:, j * 128:(j + 1) * 128], start=True, stop=True),
                             reads=["B1", "B0"], writes=["pST"], signal=(j == 3))
                    P.op("dve", lambda e: e.tensor_tensor(out=Bt[5][:], in0=pST[:], in1=mask4, op=ALU.mult),
                         reads=["pST", "cstf"], writes=["B5"])

                def post_g(bank, hh=hh):
                    gsb = hh % 2
                    E1, eb = Ft[5], Ft[7]
                    ktok, vtok, sTm = Bt[3], Bt[4], Bt[5]
                    ktv = ktok[:].rearrange("p (j k) -> p j k", k=128)
                    vtv = vtok[:].rearrange("p (j k) -> p j k", k=128)
                    ktv2 = Bt[6][:].rearrange("p (j k) -> p j k", k=128)
                    for j in range(4):
                        P.op("pe", lambda e, j=j: e.transpose(out=pTR[:, 1, j, :], in_=Bt[2][:, j * 128:(j + 1) * 128], identity=ident),
                             reads=["B2", "cstb"], writes=["pTR"], signal=(j == 3))
                    P.op("dve", lambda e: e.tensor_copy(out=vtv, in_=pTR[:, 1, :, :]), reads=["pTR"], writes=["B4"])
                    pbank = [pP[:].rearrange("p a b -> p (a b)"), pOT[:]]
                    pres = ["pP", "pOT"]
                    for c in range(8):
                        j, half = c // 2, c % 2
                        sl = c % 4
                        P.op("pe", lambda e, j=j, half=half, sl=sl, c=c: e.matmul(pbank[c // 4][:, sl * 128:(sl + 1) * 128],
                                                                                  lhsT=(ktv if half == 0 else ktv2)[:, j, :],
                                                                                  rhs=vtv[:, j, :], start=True, stop=True),
                             reads=["B3", "B6", "B4"], writes=[pres[c // 4]], signal=(sl == 3))
                    for c in range(8):
                        sl = c % 4
                        P.op("act", lambda e, c=c, sl=sl: e.activation(out=PT[sl][:], in_=pbank[c // 4][:, sl * 128:(sl + 1) * 128],
                                                                       func=AF.Copy, scale=E1[:, c * 64 + 63:c * 64 + 64]),
                             reads=[pres[c // 4], "F5"], writes=[("PT", sl)])
                        P.op("dve", lambda e, c=c: e.tensor_scalar_mul(out=smid[:, c, :], in0=Sst[:, hh, :],
                                                                       scalar1=eb[:, c * 64 + 31:c * 64 + 32]),
                             reads=[("S", hh), "F7"], writes=[("smid", c)])
                        P.op("dve", lambda e, c=c, sl=sl: e.scalar_tensor_tensor(out=Sst[:, hh, :], in0=Sst[:, hh, :],
                                                                                 scalar=eb[:, c * 64 + 63:c * 64 + 64],
                                                                                 in1=PT[sl][:], op0=ALU.mult, op1=ALU.add),
                             reads=[("S", hh), "F7", ("PT", sl)], writes=[("S", hh)])
                    P.op("act", lambda e: e.activation(out=GS[gsb][:], in_=G[bank][:], func=AF.Silu),
                         reads=[("G", bank)], writes=[("GS", gsb)])

                    def stage2a():
                        for j in range(4):
                            for half in range(2):
                                c = 2 * j + half
                                P.op("pe", lambda e, c=c, half=half: e.matmul(pOT[:, c * 64:(c + 1) * 64], lhsT=smid[:, c, :],
                                                                              rhs=Bt[0][:, c * 64:(c + 1) * 64],
                                                                              start=(half == 0), stop=False, skip_group_check=True),
                                     reads=[("smid", c), "B0"], writes=["pOT"], signal=False)
                            P.op("pe", lambda e, j=j: e.matmul(pOT[:, j * 128:(j + 1) * 128], lhsT=vtv[:, j, :],
                                                               rhs=sTm[:, j * 128:(j + 1) * 128], start=False, stop=True,
                                                               skip_group_check=True),
                                 reads=["B4", "B5"], writes=["pOT"], signal=(j == 3))
                        s_ = sqc[0] % 2
                        sqc[0] += 1
                        P.op("act", lambda e: e.activation(out=SQ[s_][:], in_=pOT[:], func=AF.Square), reads=["pOT"], writes=[("SQ", s_)])

                        def stage2b():
                            tb = 1 - gsb
                            P.op("pe", lambda e: e.matmul(pST[:], lhsT=ones, rhs=SQ[s_][:], start=True, stop=True),
                                 reads=[("SQ", s_), "cstb"], writes=["pST"])
                            P.op("act", lambda e: e.activation(out=GS[tb][:], in_=pST[:], func=AF.Ln, scale=1.0 / 128.0, bias=EPS),
                                 reads=["pST"], writes=[("GS", tb)])
                            P.op("act", lambda e: e.activation(out=GS[tb][:], in_=GS[tb][:], func=AF.Exp, scale=-0.5),
                                 reads=[("GS", tb)], writes=[("GS", tb)])
                            P.op("dve", lambda e: e.tensor_tensor(out=Ft[0][:, 0:T], in0=pOT[:], in1=GS[tb][:], op=ALU.mult),
                                 reads=["pOT", ("GS", tb)], writes=["F0"])
                            P.op("dve", lambda e: e.scalar_tensor_tensor(out=mix[:, hh, :], in0=Ft[0][:, 0:T], scalar=vcol(V_HGW + hh),
                                                                         in1=GS[gsb][:], op0=ALU.mult, op1=ALU.mult),
                                 reads=["F0", ("GS", gsb), "vecs"], writes=[("mix", hh)])
                        pipe.defer(1, stage2b)
                    pipe.defer(2, stage2a)

                for off, post in ((0, post_q), (16, post_f), (32, post_i), (48, post_g)):
                    blocks.append(dict(src=win_d[off + hh].rearrange("p (k m) -> p k m", m=128), nk=KC, rhs=rhs_hn, post=post,
                                       pre=(pre_g if off == 48 else None), nosig=(off != 16)))

            if small:
                continue
            wsq = {}

            def wout_ssq(mm):
                s_ = wsq[mm]
                P.op("pe", lambda e: e.matmul(pST[:], lhsT=ones, rhs=SQ[s_][:], start=(mm == 0), stop=(mm == KC - 1)),
                     reads=[("SQ", s_), "cstb"], writes=["pST"])

            for m in range(KC):
                def post_o(bank, m=m):
                    if m > 0:
                        wout_ssq(m - 1)
                    P.op("dve", lambda e: e.tensor_tensor(out=h[:, m, :], in0=h[:, m, :], in1=G[bank][:], op=ALU.add),
                         reads=[("h", m), ("G", bank)], writes=[("h", m)])
                    P.op("act", lambda e: e.activation(out=hn[:, m, :], in_=h[:, m, :], func=AF.Copy, scale=vcol(V_LN2 + m)),
                         reads=[("h", m), "vecs"], writes=[("hn", m)])
                    s_ = sqc[0] % 2
                    sqc[0] += 1
                    wsq[m] = s_
                    P.op("act", lambda e: e.activation(out=SQ[s_][:], in_=h[:, m, :], func=AF.Square),
                         reads=[("h", m)], writes=[("SQ", s_)])
                    if m == KC - 1:
                        wout_ssq(m)
                        P.op("act", lambda e: e.activation(out=rstd2[:], in_=pST[:], func=AF.Ln, scale=1.0 / D, bias=EPS),
                             reads=["pST"], writes=["rstd2"])
                        P.op("act", lambda e: e.activation(out=rstd2[:], in_=rstd2[:], func=AF.Exp, scale=-0.5),
                             reads=["rstd2"], writes=["rstd2"])
                blocks.append(dict(src=wout_d[m].rearrange("p (k m) -> p k m", m=128), nk=KC, rhs=rhs_mix,
                                   pre=(pipe.drain if m == 0 else None), post=post_o, nosig=(m > 0)))

            gsz = [15, 15, 14, 14, 14, 14]
            gst = [sum(gsz[:i]) for i in range(len(gsz))]
            ngrp = len(gsz)
            j2g = {}
            for gi in range(ngrp):
                for jl_ in range(gsz[gi]):
                    j2g[gst[gi] + jl_] = (gi, jl_)

            def ffn_pre():
                pass

            def mk_up(j, which, t=t):
                g, jl = j2g[j]
                ab = (g % 2) * GRP + jl
                fs = (j % 2) * 4
                ci = j + which * NFF

                def post(bank):
                    xs = Ft[fs + which]
                    y = Ft[fs + 2 + which]
                    P.op("dve", lambda e: e.tensor_copy(out=xs[:, 0:2], in_=fc[:, ci, :]), reads=["fc"], writes=[f"F{fs + which}"])
                    P.op("dve", lambda e: e.tensor_tensor(out=xs[:, 2:2 + T], in0=G[bank][:], in1=rstd2[:], op=ALU.mult),
                         reads=[("G", bank), "rstd2"], writes=[f"F{fs + which}"])
                    P.op("dve", lambda e: e.tensor_copy(out=fc[:, ci, :], in_=xs[:, T:T + 2]), reads=[f"F{fs + which}"], writes=["fc"])
                    P.op("dve", lambda e: e.tensor_scalar(out=y[:, 0:T], in0=xs[:, 2:2 + T], scalar1=vcol(V_FCW + 2 * 172 + ci),
                                                          scalar2=vcol(V_FCB + ci), op0=ALU.mult, op1=ALU.add),
                         reads=[f"F{fs + which}", "vecs"], writes=[f"F{fs + 2 + which}"])
                    for jj in (1, 0):
                        P.op("dve", lambda e, jj=jj: e.scalar_tensor_tensor(out=y[:, 0:T], in0=xs[:, jj:jj + T],
                                                                            scalar=vcol(V_FCW + jj * 172 + ci), in1=y[:, 0:T],
                                                                            op0=ALU.mult, op1=ALU.add),
                             reads=[f"F{fs + which}", f"F{fs + 2 + which}", "vecs"], writes=[f"F{fs + 2 + which}"])
                    if which == 0:
                        P.op("act", lambda e: e.activation(out=y[:, 0:T], in_=y[:, 0:T], func=AF.Silu),
                             reads=[f"F{fs + 2}"], writes=[f"F{fs + 2}"])
                    else:
                        P.op("dve", lambda e: e.tensor_tensor(out=mix[:, ab, :], in0=Ft[fs + 2][:, 0:T], in1=y[:, 0:T], op=ALU.mult),
                             reads=[f"F{fs + 2}", f"F{fs + 3}"], writes=[("mix", ab)])
                return dict(src=(wupg_d if which == 0 else wupv_d)[j].rearrange("p (k m) -> p k m", m=128), nk=KC, rhs=rhs_hn, post=post,
                            nosig=(which == 0))

            fsq = {}

            def fin_ssq(mm):
                s_ = fsq[mm]
                P.op("pe", lambda e: e.matmul(pST[:], lhsT=ones, rhs=SQ[s_][:], start=(mm == 0), stop=(mm == KC - 1)),
                     reads=[("SQ", s_), "cstb"], writes=["pST"])

            def mk_dn2(g, m):
                k0 = gst[g]
                nk = gsz[g]
                base = (g % 2) * GRP

                def rhs(k):
                    return mix[:, base + k, :], ("mix", base + k)

                def mkpost(mm):
                    def post(bank):
                        last = (g == ngrp - 1)
                        if last and mm > 0:
                            fin_ssq(mm - 1)
                        P.op("dve", lambda e: e.tensor_tensor(out=h[:, mm, :], in0=h[:, mm, :], in1=G[bank][:], op=ALU.add),
                             reads=[("h", mm), ("G", bank)], writes=[("h", mm)])
                        if last:
                            s_ = sqc[0] % 2
                            sqc[0] += 1
                            fsq[mm] = s_
                            P.op("act", lambda e: e.activation(out=SQ[s_][:], in_=h[:, mm, :], func=AF.Square),
                                 reads=[("h", mm)], writes=[("SQ", s_)])
                    return post

                def dma(e, wbuf):
                    return e.dma_start(out=wbuf[:, 0:2 * nk, :].rearrange("p (a k) m -> p a k m", a=2),
                                       in_=wdn_d[m:m + 2][:, :, k0 * 128:(k0 + nk) * 128].rearrange("a p (k m) -> p a k m", m=128))
                return dict(dma=dma, subs=[dict(k0=0, nk=nk, rhs=rhs, post=mkpost(m), nosig=True),
                                           dict(k0=nk, nk=nk, rhs=rhs, post=mkpost(m + 1))])

            first = True
            for g in range(ngrp):
                js = list(range(gst[g], gst[g] + gsz[g]))
                for idx, j in enumerate(js):
                    bq = mk_up(j, 0)
                    if first:
                        bq["pre"] = ffn_pre
                        first = False
                    blocks.append(bq)
                    blocks.append(mk_up(j, 1))
                    if idx == 1 and g > 0:
                        for m in range(0, KC, 2):
                            blocks.append(mk_dn2(g - 1, m))
            for m in range(0, KC, 2):
                blocks.append(mk_dn2(ngrp - 1, m))

            def tile_end(t0=t0, fin_ssq=fin_ssq, t=t):
                fin_ssq(KC - 1)
                if t + 1 < ntiles:
                    pe_filler(48)
                P.op("act", lambda e: e.activation(out=Ft[2][:, 0:T], in_=pST[:], func=AF.Ln, scale=1.0 / D, bias=EPS),
                     reads=["pST"], writes=["F2"])
                P.op("act", lambda e: e.activation(out=Ft[4][:, 0:T], in_=Ft[2][:, 0:T], func=AF.Exp, scale=-0.5),
                     reads=["F2"], writes=["F4"])
                for c in range(KC):
                    P.op("dve", lambda e, c=c: e.scalar_tensor_tensor(out=h[:, c, :], in0=h[:, c, :], scalar=vcol(V_FIN + c),
                                                                     in1=Ft[4][:, 0:T], op0=ALU.mult, op1=ALU.mult),
                         reads=[("h", c), "F4", "vecs"], writes=[("h", c)])
                    if c % 4 == 3:
                        g8 = c // 4
                        P.op("sp", lambda e, g8=g8: e.dma_start(out=out_d[:, 4 * g8:4 * g8 + 4, t0:t0 + T], in_=h[:, 4 * g8:4 * g8 + 4, :]),
                             reads=[("h", cc) for cc in range(4 * g8, 4 * g8 + 4)], dma_sem=f"D_o{g8}")
            blocks[-1]["post_tile"] = tile_end

        for b in blocks:
            if "post_tile" in b:
                sub = b["subs"][-1]
                p0, p1 = sub["post"], b["post_tile"]
                sub["post"] = (lambda bank, p0=p0, p1=p1: (p0(bank), p1()))

        if maxblocks is not None:
            del blocks[maxblocks:]
        run_blocks()
        pipe.drain()
        P.final_waits("sp")
        sems = {n: st.enter_context(nc.semaphore(n)) for n in sorted(P.cnt.keys())}
        P.emit(nc, sems)
    return nc


def _colmajor(v, n):
    return np.ascontiguousarray(np.asarray(v, dtype=np.float32).reshape(n, 128).T)


def _wblocks(w, nblk):
    K = w.shape[0]
    kc = K // 128
    return np.ascontiguousarray(w.reshape(kc, 128, nblk, 128).transpose(2, 1, 0, 3)).reshape(nblk, 128, kc * 128)


def _prep_shared(inp):
    vec = np.zeros((128, NV), np.float32)
    vec[:, V_LN1:V_LN1 + 32] = _colmajor(inp["ln1_w"][0], 32)
    vec[:, V_LN2:V_LN2 + 32] = _colmajor(inp["ln2_w"][0], 32)
    vec[:, V_FIN:V_FIN + 32] = _colmajor(inp["final_norm_w"], 32)
    vec[:, V_G0:V_G0 + 16] = _colmajor(inp["lb_gamma"][0], 16)
    vec[:, V_G1:V_G1 + 16] = _colmajor(inp["lb_gamma"][1], 16)
    vec[:, V_HGW:V_HGW + 16] = _colmajor(inp["hg_norm_w"][0], 16)
    for j in range(4):
        vec[:, V_CW + j * 16:V_CW + (j + 1) * 16] = _colmajor(inp["lru_conv_w"][0, j], 16)
    vec[:, V_CB:V_CB + 16] = _colmajor(inp["lru_conv_b"][0], 16)
    vec[:, V_BA:V_BA + 16] = _colmajor(inp["lru_ba"][0], 16)
    vec[:, V_BX:V_BX + 16] = _colmajor(inp["lru_bx"][0], 16)
    vec[:, V_LAM:V_LAM + 16] = _colmajor(inp["lru_lambda"][0], 16)
    vec[:, V_LRUW:V_LRUW + 16] = _colmajor(inp["lru_norm_w"][0], 16)
    for j in range(3):
        vec[:, V_FCW + j * 172:V_FCW + (j + 1) * 172] = _colmajor(inp["ffn_conv_w"][0, j], 172)
    vec[:, V_FCB:V_FCB + 172] = _colmajor(inp["ffn_conv_b"][0], 172)

    cstb = np.zeros((128, 256), np.float32)
    cstb[:, 0:128] = np.eye(128, dtype=np.float32)
    cstb[:, 128:256] = 1.0
    cstb = cstb.astype(ml_dtypes.bfloat16)
    cstf = np.ones((128, 1024), np.float32)
    cstf[:, 0:512:64] = 0.0
    s = np.arange(128)[:, None]
    tt = np.arange(128)[None, :]
    pm = ((s // 64 == tt // 64) & (s <= tt)).astype(np.float32)
    cstf[:, 512:1024] = np.tile(pm, (1, 4))

    wa = np.asarray(inp["lru_wa"][0], np.float32)
    wx = np.asarray(inp["lru_wx"][0], np.float32)
    wlru = np.ascontiguousarray(np.stack([wa, wx], axis=2)).reshape(NL, 128, 256)
    shared = {
        "w_in": _wblocks(np.asarray(inp["w_in"][0], np.float32), 96),
        "w_out": _wblocks(np.asarray(inp["w_out"][0], np.float32), 32),
        "w_upg": _wblocks(np.asarray(inp["ffn_w_up"][0][:, :DFF], np.float32), NFF),
        "w_upv": _wblocks(np.asarray(inp["ffn_w_up"][0][:, DFF:], np.float32), NFF),
        "w_dn": _wblocks(np.asarray(inp["ffn_w_down"][0], np.float32), 32),
        "w_lru": wlru,
        "vecs": vec,
        "cstb": cstb,
        "cstf": cstf,
    }
    return shared


def kernel(**inputs):
    x = np.asarray(inputs["x"], np.float32)
    B = x.shape[0]
    shared = _prep_shared(inputs)
    nc = build_program()
    in_maps = []
    for b in range(B):
        xf = np.ascontiguousarray(x[b].T.reshape(KC, 128, SEQ).transpose(1, 0, 2))
        m = dict(shared)
        m["x"] = xf
        in_maps.append(m)
    res = run_bass_kernel_spmd(nc, in_maps, core_ids=list(range(B)))
    outs = []
    for b in range(B):
        o = np.asarray(res.results[b]["out"], np.float32)
        outs.append(np.ascontiguousarray(o.transpose(1, 0, 2).reshape(D, SEQ).T))
    return np.stack(outs, axis=0)
```

```python
import numpy as np
import ml_dtypes
from contextlib import ExitStack
import concourse.bass as bass
import concourse.mybir as mybir
from concourse.bass_utils import run_bass_kernel_spmd

F32 = mybir.dt.float32
BF16 = mybir.dt.bfloat16
AF = mybir.ActivationFunctionType
ALU = mybir.AluOpType

D = 4096
SEQ = 2048
T = 512
NT = SEQ // T
KC = D // 128
NH = 16
NL = 16
DFF = 11008
NFF = DFF // 128
GRP = 16
EPS = 1e-6
NB = 3
NG = 4

V_LN1, V_LN2, V_FIN, V_G0, V_G1, V_HGW = 0, 32, 64, 96, 112, 128
V_CW, V_CB, V_BA, V_BX, V_LAM, V_LRUW = 144, 208, 224, 240, 256, 272
V_FCW, V_FCB, NV = 288, 804, 976
DV_LB, DV_NOML, DV_LNOML, DV_C8, DV_C16, DV_SCR, DV_BAH, DV_BXH, DV_C8H, DV_S1, DV_B1, DV_LNH, NDV = 0, 16, 32, 48, 64, 80, 96, 112, 128, 144, 160, 176, 192

ENGS = ["pe", "act", "dve", "pool", "sp"]


class Prog:
    def __init__(self):
        self.ops = {e: [] for e in ENGS}
        self.cnt = {}
        self.seen = {e: {} for e in ENGS}
        self.lastw = {}
        self.readers = {}

    def op(self, eng, fn, reads=(), writes=(), signal=True, dma_sem=None, pre_r=(), pre_w=()):
        own = "S_" + eng
        deps = []
        for r in list(reads) + list(pre_r):
            if r in self.lastw:
                deps.append(self.lastw[r])
        for w in list(writes) + list(pre_w):
            if w in self.lastw:
                deps.append(self.lastw[w])
            deps += self.readers.get(w, [])
        waits = {}
        for (s, v) in deps:
            if eng == "pe" and s == own:
                continue
            if self.seen[eng].get(s, 0) < v:
                waits[s] = max(waits.get(s, 0), v)
        for s, v in waits.items():
            self.seen[eng][s] = v
        if dma_sem is not None:
            self.cnt[dma_sem] = self.cnt.get(dma_sem, 0) + 16
            tag = (dma_sem, self.cnt[dma_sem])
            inc = (dma_sem, 16)
        elif signal:
            self.cnt[own] = self.cnt.get(own, 0) + 1
            tag = (own, self.cnt[own])
            inc = (own, 1)
        else:
            tag = (own, self.cnt.get(own, 0) + 1)
            inc = None
        for r in reads:
            self.readers.setdefault(r, []).append(tag)
        for w in writes:
            self.lastw[w] = tag
            self.readers[w] = []
        self.ops[eng].append((waits, fn, inc))

    def final_waits(self, eng):
        waits = {}
        for s, v in self.cnt.items():
            if self.seen[eng].get(s, 0) < v:
                waits[s] = v
        self.ops[eng].append((waits, None, None))

    def emit(self, nc, sems):
        with nc.Block() as block:
            def run(engname, e):
                for (waits, fn, inc) in self.ops[engname]:
                    for s, v in waits.items():
                        e.wait_ge(sems[s], v)
                    if fn is None:
                        continue
                    ins = fn(e)
                    if inc is not None:
                        ins.then_inc(sems[inc[0]], inc[1])

            @block.tensor
            def _(e):
                run("pe", e)

            @block.scalar
            def _(e):
                run("act", e)

            @block.vector
            def _(e):
                run("dve", e)

            @block.gpsimd
            def _(e):
                run("pool", e)

            @block.sync
            def _(e):
                run("sp", e)


class Pipe:
    def __init__(self):
        self.blk = 0
        self.pending = []

    def defer(self, delay, fn):
        self.pending.append((self.blk + delay, fn))

    def flush(self):
        while True:
            due = [p for p in self.pending if p[0] <= self.blk]
            if not due:
                return
            p = due[0]
            self.pending.remove(p)
            p[1]()

    def tick(self):
        self.blk += 1
        self.flush()

    def drain(self):
        while self.pending:
            self.blk = max(self.blk, min(p[0] for p in self.pending))
            self.flush()


def build_program(ntiles=NT, maxblocks=None, small=False):
    nc = bass.Bass("TRN2", target_bir_lowering=False)
    x_d = nc.dram_tensor("x", [128, KC, SEQ], F32, kind="ExternalInput").ap()
    out_d = nc.dram_tensor("out", [128, KC, SEQ], F32, kind="ExternalOutput").ap()
    win_d = nc.dram_tensor("w_in", [96, 128, KC * 128], F32, kind="ExternalInput").ap()
    if not small:
        wout_d = nc.dram_tensor("w_out", [32, 128, KC * 128], F32, kind="ExternalInput").ap()
        wupg_d = nc.dram_tensor("w_upg", [NFF, 128, KC * 128], F32, kind="ExternalInput").ap()
        wupv_d = nc.dram_tensor("w_upv", [NFF, 128, KC * 128], F32, kind="ExternalInput").ap()
        wdn_d = nc.dram_tensor("w_dn", [32, 128, NFF * 128], F32, kind="ExternalInput").ap()
    wlru_d = nc.dram_tensor("w_lru", [NL, 128, 256], F32, kind="ExternalInput").ap()
    vecs_d = nc.dram_tensor("vecs", [128, NV], F32, kind="ExternalInput").ap()
    cstb_d = nc.dram_tensor("cstb", [128, 256], BF16, kind="ExternalInput").ap()
    cstf_d = nc.dram_tensor("cstf", [128, 1024], F32, kind="ExternalInput").ap()

    P = Prog()
    pipe = Pipe()
    with ExitStack() as st:
        def sb(name, shape, dt):
            return st.enter_context(nc.sbuf_tensor(name, shape, dt))

        def ps(name, shape, dt):
            return st.enter_context(nc.psum_tensor(name, shape, dt))

        h = sb("h", [128, KC, T], F32)
        hn = sb("hn", [128, KC, T], BF16)
        mix = sb("mix", [128, KC, T], BF16)
        wb = [sb(f"wb{i}", [128, KC, 128], BF16) for i in range(NB)]
        lw = [sb(f"lw{i}", [128, 2, 128], BF16) for i in range(2)]
        vecs = sb("vecs_sb", [128, NV], F32)
        dv = sb("dv", [128, NDV], F32)
        cstb = sb("cstb_sb", [128, 256], BF16)
        cstf = sb("cstf_sb", [128, 1024], F32)
        Sst = sb("Sst", [128, NH, 128], F32)
        lstate = sb("lstate", [128, NL], F32)
        lc = sb("lc", [128, NL, 3], F32)
        fc = sb("fc", [128, 2 * NFF, 2], F32)
        Ft = [sb(f"F{i}", [128, 516], F32) for i in range(8)]
        GS = [sb(f"GS{i}", [128, T], F32) for i in range(2)]
        Bt = [sb(f"B{i}", [128, T], BF16) for i in range(7)]
        smid = sb("smid", [128, 8, 128], BF16)
        SQ = [sb(f"SQ{i}", [128, T], BF16) for i in range(2)]
        PT = [sb(f"PT{i}", [128, 128], F32) for i in range(4)]
        rstd2 = sb("rstd2", [128, T], F32)

        G = [ps(f"G{i}", [128, T], F32) for i in range(NG)]
        pST = ps("pST", [128, T], F32)
        pOT = ps("pOT", [128, T], F32)
        pTR = ps("pTR", [128, 2, 4, 128], BF16)
        pP = ps("pP", [128, 4, 128], F32)

        ident = cstb[:, 0:128]
        ones = cstb[:, 128:256]
        maskm = cstf[:, 0:512]
        mask4 = cstf[:, 512:1024]

        def vcol(off):
            return vecs[:, off:off + 1]

        def dcol(off):
            return dv[:, off:off + 1]

        P.op("sp", lambda e: e.dma_start(out=vecs[:], in_=vecs_d), writes=["vecs"], dma_sem="D_m0")
        P.op("sp", lambda e: e.dma_start(out=cstb[:], in_=cstb_d), writes=["cstb"], dma_sem="D_m1")
        P.op("sp", lambda e: e.dma_start(out=cstf[:], in_=cstf_d), writes=["cstf"], dma_sem="D_m2")
        P.op("dve", lambda e: e.memset(Sst[:], 0.0), writes=[("S", i) for i in range(NH)])
        P.op("dve", lambda e: e.memset(lstate[:], 0.0), writes=["lstate"])
        P.op("dve", lambda e: e.memset(lc[:], 0.0), writes=["lc"])
        P.op("dve", lambda e: e.memset(fc[:], 0.0), writes=["fc"])
        P.op("dve", lambda e: e.memset(smid[:], 0.5), writes=[("smid", c) for c in range(8)])
        P.op("dve", lambda e: e.memset(Bt[3][:], 0.0), writes=["B3"])
        P.op("dve", lambda e: e.memset(Bt[6][:], 0.0), writes=["B6"])
        P.op("dve", lambda e: e.tensor_tensor(out=dv[:, DV_SCR:DV_SCR + 16], in0=vecs[:, V_G0:V_G0 + 16],
                                              in1=vecs[:, V_G1:V_G1 + 16], op=ALU.subtract),
             reads=["vecs"], writes=["dvs"])
        P.op("act", lambda e: e.activation(out=dv[:, DV_LB:DV_LB + 16], in_=dv[:, DV_SCR:DV_SCR + 16], func=AF.Sigmoid),
             reads=["dvs"], writes=["dv_lb"])
        P.op("dve", lambda e: e.tensor_scalar_add(out=dv[:, DV_NOML:DV_NOML + 16], in0=dv[:, DV_LB:DV_LB + 16], scalar1=-1.0),
             reads=["dv_lb"], writes=["dv_noml"])
        P.op("act", lambda e: e.activation(out=dv[:, DV_LNOML:DV_LNOML + 16], in_=dv[:, DV_LB:DV_LB + 16], func=AF.Ln,
                                           scale=-1.0, bias=1.0),
             reads=["dv_lb"], writes=["dv_lnoml"])
        P.op("act", lambda e: e.activation(out=dv[:, DV_SCR:DV_SCR + 16], in_=vecs[:, V_LAM:V_LAM + 16], func=AF.Exp, scale=-1.0),
             reads=["vecs", "dvs"], writes=["dvs"])
        P.op("act", lambda e: e.activation(out=dv[:, DV_SCR:DV_SCR + 16], in_=dv[:, DV_SCR:DV_SCR + 16], func=AF.Ln, bias=1.0),
             reads=["dvs"], writes=["dvs"])
        P.op("dve", lambda e: e.tensor_scalar_mul(out=dv[:, DV_C8:DV_C8 + 16], in0=dv[:, DV_SCR:DV_SCR + 16], scalar1=-8.0),
             reads=["dvs"], writes=["dv_c8"])
        P.op("dve", lambda e: e.tensor_scalar_mul(out=dv[:, DV_C16:DV_C16 + 16], in0=dv[:, DV_SCR:DV_SCR + 16], scalar1=-16.0),
             reads=["dvs"], writes=["dv_c16"])
        P.op("dve", lambda e: e.tensor_scalar_mul(out=dv[:, DV_C8H:DV_C8H + 16], in0=dv[:, DV_SCR:DV_SCR + 16], scalar1=-4.0),
             reads=["dvs"], writes=["dv_c8h"])
        P.op("dve", lambda e: e.tensor_scalar_mul(out=dv[:, DV_BAH:DV_BAH + 16], in0=vecs[:, V_BA:V_BA + 16], scalar1=0.5),
             reads=["vecs"], writes=["dv_bah"])
        P.op("dve", lambda e: e.tensor_scalar_mul(out=dv[:, DV_BXH:DV_BXH + 16], in0=vecs[:, V_BX:V_BX + 16], scalar1=0.5),
             reads=["vecs"], writes=["dv_bxh"])
        P.op("dve", lambda e: e.tensor_scalar_mul(out=dv[:, DV_S1:DV_S1 + 16], in0=dv[:, DV_NOML:DV_NOML + 16], scalar1=0.5),
             reads=["dv_noml"], writes=["dv_s1"])
        P.op("dve", lambda e: e.tensor_scalar_add(out=dv[:, DV_B1:DV_B1 + 16], in0=dv[:, DV_S1:DV_S1 + 16], scalar1=1.0),
             reads=["dv_s1"], writes=["dv_b1"])
        P.op("act", lambda e: e.activation(out=dv[:, DV_LNH:DV_LNH + 16], in_=dv[:, DV_LB:DV_LB + 16], func=AF.Ln,
                                           scale=-0.5, bias=0.5),
             reads=["dv_lb"], writes=["dv_lnh"])
        CONST = ["vecs", "cstb", "cstf", "dv_lb", "dv_noml", "dv_lnoml", "dv_c8", "dv_c16"]

        sqc = [0]

        def rmsnorm_to_hn(wcol):
            for c in range(KC):
                s = sqc[0] % 2
                sqc[0] += 1
                P.op("act", lambda e, c=c, s=s: e.activation(out=SQ[s][:], in_=h[:, c, :], func=AF.Square),
                     reads=[("h", c)], writes=[("SQ", s)])
                P.op("pe", lambda e, c=c, s=s: e.matmul(pST[:], lhsT=ones, rhs=SQ[s][:], start=(c == 0), stop=(c == KC - 1)),
                     reads=[("SQ", s), "cstb"], writes=["pST"], signal=True)
                pe_filler(2)
            P.op("act", lambda e: e.activation(out=Ft[2][:, 0:T], in_=pST[:], func=AF.Ln, scale=1.0 / D, bias=EPS),
                 reads=["pST"], writes=["F2"])
            P.op("act", lambda e: e.activation(out=Ft[4][:, 0:T], in_=Ft[2][:, 0:T], func=AF.Exp, scale=-0.5),
                 reads=["F2"], writes=["F4"])
            for c in range(KC):
                P.op("dve", lambda e, c=c: e.scalar_tensor_tensor(out=hn[:, c, :], in0=h[:, c, :], scalar=vcol(wcol + c),
                                                                 in1=Ft[4][:, 0:T], op0=ALU.mult, op1=ALU.mult),
                     reads=[("h", c), "F4", "vecs"], writes=[("hn", c)])

        blocks = []

        def issue_wdma(i):
            b = blocks[i]
            buf = i % NB
            if "dma" in b:
                P.op("pool", lambda e, b=b, buf=buf: b["dma"](e, wb[buf]), writes=[("wb", buf)], dma_sem=f"D_w{buf}")
                return
            nk = b["nk"]
            P.op("pool", lambda e, b=b, buf=buf, nk=nk: e.dma_start(
                out=wb[buf][:, 0:nk, :], in_=b["src"]), writes=[("wb", buf)], dma_sem=f"D_w{buf}")

        bankc = [0]

        def run_blocks():
            for i in range(min(NB - 1, len(blocks))):
                issue_wdma(i)
            for i, b in enumerate(blocks):
                if b.get("pre"):
                    b["pre"]()
                if i + NB - 1 < len(blocks):
                    issue_wdma(i + NB - 1)
                buf = i % NB
                subs = b.get("subs") or [dict(k0=0, nk=b["nk"], rhs=b["rhs"], post=b["post"], nosig=b.get("nosig", False))]
                for si, sub in enumerate(subs):
                    bank = bankc[0] % NG
                    bankc[0] += 1
                    nk, k0 = sub["nk"], sub["k0"]
                    for k in range(nk):
                        rap, rres = sub["rhs"](k)
                        pre_r, pre_w = [], []
                        if k == nk - 1:
                            pre_w.append(("G", bankc[0] % NG))
                            if si == len(subs) - 1 and i + 1 < len(blocks):
                                pre_r.append(("wb", (i + 1) % NB))
                        P.op("pe", lambda e, buf=buf, bank=bank, k=k, rap=rap, nk=nk, k0=k0: e.matmul(
                            G[bank][:], lhsT=wb[buf][:, k0 + k, :], rhs=rap, start=(k == 0), stop=(k == nk - 1)),
                            reads=[("wb", buf), rres], writes=[("G", bank)], signal=(k == nk - 1 and not sub.get("nosig")),
                            pre_r=pre_r, pre_w=pre_w)
                        if b.get("fill") and k < nk - 1:
                            pe_filler(1)
                    sub["post"](bank)
                pipe.tick()

        def pe_filler(n, bank_off=0):
            bank = (bankc[0] + bank_off) % NG
            for _ in range(n):
                P.op("pe", lambda e, bank=bank: e.matmul(G[bank][:], lhsT=ident, rhs=smid[:, 0:4, :].rearrange("p a b -> p (a b)"),
                                                         start=True, stop=True),
                     reads=["cstb"] + [("smid", c) for c in range(4)], writes=[("G", bank)], signal=False)

        def rhs_hn(k):
            return hn[:, k, :], ("hn", k)

        def rhs_mix(k):
            return mix[:, k, :], ("mix", k)

        for t in range(ntiles):
            t0 = t * T

            def tile_start(t=t, t0=t0):
                for g8 in range(8):
                    P.op("sp", lambda e, g8=g8: e.dma_start(out=h[:, 4 * g8:4 * g8 + 4, :], in_=x_d[:, 4 * g8:4 * g8 + 4, t0:t0 + T]),
                         writes=[("h", c) for c in range(4 * g8, 4 * g8 + 4)], dma_sem=f"D_h{g8}")
                rmsnorm_to_hn(V_LN1)

            lru_sq = {}

            def mk_post_x(n, t=t):
                def post_x(bank):
                    l = n % 2
                    par = n % 2
                    P.op("pool", lambda e: e.dma_start(out=lw[l][:], in_=wlru_d[n].rearrange("p (a m) -> p a m", m=128)),
                         writes=[("lw", l)], dma_sem=f"D_lw{l}")
                    xs = Ft[0]
                    xb = GS[par]
                    P.op("dve", lambda e: e.tensor_copy(out=xs[:, 0:3], in_=lc[:, n, :]), reads=["lc"], writes=["F0"])
                    P.op("act", lambda e: e.activation(out=xs[:, 3:3 + T], in_=G[bank][:], func=AF.Copy),
                         reads=[("G", bank)], writes=["F0"])
                    P.op("dve", lambda e: e.tensor_copy(out=lc[:, n, :], in_=xs[:, T:T + 3]), reads=["F0"], writes=["lc"])
                    P.op("dve", lambda e: e.tensor_scalar(out=xb[:], in0=xs[:, 3:3 + T], scalar1=vcol(V_CW + 3 * 16 + n),
                                                          scalar2=vcol(V_CB + n), op0=ALU.mult, op1=ALU.add),
                         reads=["F0", "vecs"], writes=[("GS", par)])
                    for j in (2, 1, 0):
                        P.op("dve", lambda e, j=j: e.scalar_tensor_tensor(out=xb[:], in0=xs[:, j:j + T],
                                                                          scalar=vcol(V_CW + j * 16 + n), in1=xb[:],
                                                                          op0=ALU.mult, op1=ALU.add),
                             reads=["F0", ("GS", par), "vecs"], writes=[("GS", par)])
                    if n == 0:
                        P.op("act", lambda e: e.activation(out=Bt[par][:], in_=xb[:], func=AF.Copy),
                             reads=[("GS", par)], writes=[f"B{par}"])
                return post_x

            def mk_post_y(n, t=t):
                def post_y(bank):
                    l = n % 2
                    par = n % 2
                    P.op("act", lambda e: e.activation(out=Ft[7][:, 0:T], in_=G[bank][:], func=AF.Gelu_apprx_tanh),
                         reads=[("G", bank)], writes=["F7"])

                    def ssq_mm(nn):
                        s_ = lru_sq[nn]
                        P.op("pe", lambda e: e.matmul(pP[:].rearrange("p a b -> p (a b)"), lhsT=ones, rhs=SQ[s_][:],
                                                      start=(nn == 0), stop=(nn == NL - 1)),
                             reads=[("SQ", s_), "cstb"], writes=["pP"])

                    def fin():
                        ssq_mm(NL - 1)
                        P.op("act", lambda e: e.activation(out=Ft[3][:, 0:T], in_=pP[:].rearrange("p a b -> p (a b)"),
                                                           func=AF.Ln, scale=1.0 / 2048.0, bias=EPS),
                             reads=["pP"], writes=["F3"])
                        P.op("act", lambda e: e.activation(out=Ft[5][:, 0:T], in_=Ft[3][:, 0:T], func=AF.Exp, scale=-0.5),
                             reads=["F3"], writes=["F5"])
                        for nn in range(NL):
                            P.op("dve", lambda e, nn=nn: e.tensor_tensor(out=mix[:, NH + nn, :], in0=mix[:, NH + nn, :],
                                                                         in1=Ft[5][:, 0:T], op=ALU.mult),
                                 reads=[("mix", NH + nn), "F5"], writes=[("mix", NH + nn)])

                    def stage1():
                        if n > 0:
                            ssq_mm(n - 1)
                        P.op("pe", lambda e: e.matmul(pST[:], lhsT=lw[l][:, 0, :], rhs=Bt[par][:], start=True, stop=True),
                             reads=[("lw", l), f"B{par}"], writes=["pST"])
                        P.op("pe", lambda e: e.matmul(pOT[:], lhsT=lw[l][:, 1, :], rhs=Bt[par][:], start=True, stop=True),
                             reads=[("lw", l), f"B{par}"], writes=["pOT"])
                        thr, thi, a, a2 = Ft[2], Ft[3], Ft[4], Ft[5]
                        P.op("act", lambda e: e.activation(out=thr[:, 0:T], in_=pST[:], func=AF.Tanh, scale=0.5, bias=dcol(DV_BAH + n)),
                             reads=["pST", "dv_bah"], writes=["F2"])
                        P.op("act", lambda e: e.activation(out=thi[:, 0:T], in_=pOT[:], func=AF.Tanh, scale=0.5, bias=dcol(DV_BXH + n)),
                             reads=["pOT", "dv_bxh"], writes=["F3"])
                        P.op("act", lambda e: e.activation(out=a[:, 0:T], in_=thr[:, 0:T], func=AF.Exp, scale=dcol(DV_C8H + n),
                                                           bias=dcol(DV_C8H + n)),
                             reads=["F2", "dv_c8h"], writes=["F4"])
                        P.op("act", lambda e: e.activation(out=a2[:, 0:T], in_=thr[:, 0:T], func=AF.Exp, scale=dcol(DV_C8 + n),
                                                           bias=dcol(DV_C8 + n)),
                             reads=["F2", "dv_c8"], writes=["F5"])
                        P.op("act", lambda e: e.activation(out=a2[:, 0:T], in_=a2[:, 0:T], func=AF.Sqrt, scale=-1.0, bias=1.0),
                             reads=["F5"], writes=["F5"])
                        if n + 1 < NL:
                            P.op("act", lambda e: e.activation(out=Bt[1 - par][:], in_=GS[1 - par][:], func=AF.Copy),
                                 reads=[("GS", 1 - par)], writes=[f"B{1 - par}"])
                        if t == 0:
                            P.op("dve", lambda e: e.memset(a2[:, 0:1], 1.0), reads=["F5"], writes=["F5"])
                        u = GS[par]
                        P.op("dve", lambda e: e.scalar_tensor_tensor(out=u[:], in0=thi[:, 0:T], scalar=1.0, in1=u[:],
                                                                     op0=ALU.add, op1=ALU.mult),
                             reads=[("GS", par), "F3"], writes=[("GS", par)])
                        P.op("dve", lambda e: e.tensor_tensor(out=u[:], in0=u[:], in1=a2[:, 0:T], op=ALU.mult),
                             reads=[("GS", par), "F5"], writes=[("GS", par)])
                        hst = Ft[6]
                        P.op("dve", lambda e: e.tensor_tensor_scan(out=hst[:, 0:T], data0=a[:, 0:T], data1=u[:],
                                                                   initial=lstate[:, n:n + 1], op0=ALU.mult, op1=ALU.add),
                             reads=["F4", ("GS", par), "lstate"], writes=["F6"])
                        P.op("dve", lambda e: e.tensor_copy(out=lstate[:, n:n + 1], in_=hst[:, T - 1:T]),
                             reads=["F6"], writes=["lstate"])
                        o = Ft[2]
                        P.op("dve", lambda e: e.scalar_tensor_tensor(out=o[:, 0:T], in0=hst[:, 0:T], scalar=0.5, in1=Ft[7][:, 0:T],
                                                                     op0=ALU.mult, op1=ALU.mult),
                             reads=["F6", "F7", "F2"], writes=["F2"])
                        s_ = sqc[0] % 2
                        sqc[0] += 1
                        lru_sq[n] = s_
                        P.op("act", lambda e: e.activation(out=SQ[s_][:], in_=o[:, 0:T], func=AF.Square),
                             reads=["F2"], writes=[("SQ", s_)])
                        P.op("act", lambda e: e.activation(out=mix[:, NH + n, :], in_=o[:, 0:T], func=AF.Copy,
                                                           scale=vcol(V_LRUW + n)),
                             reads=["F2", "vecs"], writes=[("mix", NH + n)])
                        if n == NL - 1:
                            pipe.defer(1, fin)
                    pipe.defer(1, stage1)
                return post_y

            def lru_blk(which, n):
                off = 64 if which == 0 else 80
                return dict(src=win_d[off + n].rearrange("p (k m) -> p k m", m=128), nk=KC, rhs=rhs_hn,
                            post=(mk_post_x(n) if which == 0 else mk_post_y(n)), nosig=(which == 1))

            b0 = lru_blk(0, 0)
            b0["fill"] = True
            b0["pre"] = tile_start
            blocks.append(b0)
            for n in range(NL):
                if n + 1 < NL:
                    blocks.append(lru_blk(0, n + 1))
                blocks.append(lru_blk(1, n))

            for hh in range(NH):
                def post_q(bank, hh=hh):
                    P.op("act", lambda e: e.activation(out=Ft[0][:, 0:T], in_=G[bank][:], func=AF.Silu),
                         reads=[("G", bank)], writes=["F0"])

                def post_f(bank, hh=hh):
                    sgn, g, b, dd, E1, E2, eb = Ft[1], Ft[2], Ft[3], Ft[4], Ft[5], Ft[6], Ft[7]
                    P.op("act", lambda e: e.activation(out=sgn[:, 0:T], in_=G[bank][:], func=AF.Tanh, scale=-0.5),
                         reads=[("G", bank)], writes=["F1"])
                    P.op("act", lambda e: e.activation(out=g[:, 0:T], in_=sgn[:, 0:T], func=AF.Ln, scale=dcol(DV_S1 + hh),
                                                       bias=dcol(DV_B1 + hh)),
                         reads=["F1", "dv_s1", "dv_b1"], writes=["F2"])
                    P.op("dve", lambda e: e.tensor_tensor_scan(out=b[:, 0:T], data0=maskm, data1=g[:, 0:T], initial=0.0,
                                                               op0=ALU.mult, op1=ALU.add),
                         reads=["F2", "cstf"], writes=["F3"])
                    bv = b[:, 0:T].rearrange("p (c t) -> p c t", t=64)
                    P.op("dve", lambda e: e.tensor_tensor(out=dd[:, 0:T].rearrange("p (c t) -> p c t", t=64), in0=bv,
                                                          in1=bv[:, :, 31:32].to_broadcast([128, 8, 64]), op=ALU.subtract),
                         reads=["F3"], writes=["F4"])
                    P.op("act", lambda e: e.activation(out=E1[:, 0:T], in_=dd[:, 0:T], func=AF.Exp), reads=["F4"], writes=["F5"])
                    P.op("act", lambda e: e.activation(out=E2[:, 0:T], in_=dd[:, 0:T], func=AF.Exp, scale=-1.0,
                                                       bias=dcol(DV_LNH + hh)),
                         reads=["F4", "dv_lnh"], writes=["F6"])
                    P.op("act", lambda e: e.activation(out=eb[:, 0:T], in_=b[:, 0:T], func=AF.Exp), reads=["F3"], writes=["F7"])
                    P.op("dve", lambda e: e.tensor_tensor(out=Bt[0][:], in0=Ft[0][:, 0:T], in1=E1[:, 0:T], op=ALU.mult),
                         reads=["F0", "F5"], writes=["B0"])
                    P.op("dve", lambda e: e.scalar_tensor_tensor(out=Bt[1][:], in0=sgn[:, 0:T], scalar=1.0, in1=E2[:, 0:T],
                                                                 op0=ALU.add, op1=ALU.mult),
                         reads=["F1", "F6"], writes=["B1"])

                def post_i(bank, hh=hh):
                    P.op("act", lambda e: e.activation(out=Bt[2][:], in_=G[bank][:], func=AF.Copy),
                         reads=[("G", bank)], writes=["B2"])

                def pre_g(hh=hh):
                    ktv = Bt[3][:].rearrange("p (j k) -> p j k", k=128)
                    ktv2 = Bt[6][:].rearrange("p (j k) -> p j k", k=128)
                    for j in range(4):
                        P.op("pe", lambda e, j=j: e.transpose(out=pTR[:, 0, j, :], in_=Bt[1][:, j * 128:(j + 1) * 128], identity=ident),
                             reads=["B1", "cstb"], writes=["pTR"], signal=(j == 3))
                    P.op("act", lambda e: e.activation(out=ktv[0:64], in_=pTR[0:64, 0, :, :], func=AF.Copy), reads=["pTR"], writes=["B3"])
                    P.op("act", lambda e: e.activation(out=ktv2[64:128], in_=pTR[64:128, 0, :, :], func=AF.Copy), reads=["pTR"], writes=["B6"])
                    for j in range(4):
                        P.op("pe", lambda e, j=j: e.matmul(pST[:, j * 128:(j + 1) * 128], lhsT=Bt[1][:, j * 128:(j + 1) * 128],
                                                           rhs=Bt[0][:, j * 128:(j + 1) * 128], start=True, stop=True),
                             reads=["B1", "B0"], writes=["pST"], signal=(j == 3))
                    P.op("dve", lambda e: e.tensor_tensor(out=Bt[5][:], in0=pST[:], in1=mask4, op=ALU.mult),
                         reads=["pST", "cstf"], writes=["B5"])

                def post_g(bank, hh=hh):
                    gsb = hh % 2
                    E1, eb = Ft[5], Ft[7]
                    ktok, vtok, sTm = Bt[3], Bt[4], Bt[5]
                    ktv = ktok[:].rearrange("p (j k) -> p j k", k=128)
                    vtv = vtok[:].rearrange("p (j k) -> p j k", k=128)
                    ktv2 = Bt[6][:].rearrange("p (j k) -> p j k", k=128)
                    for j in range(4):
                        P.op("pe", lambda e, j=j: e.transpose(out=pTR[:, 1, j, :], in_=Bt[2][:, j * 128:(j + 1) * 128], identity=ident),
                             reads=["B2", "cstb"], writes=["pTR"], signal=(j == 3))
                    P.op("dve", lambda e: e.tensor_copy(out=vtv, in_=pTR[:, 1, :, :]), reads=["pTR"], writes=["B4"])
                    pbank = [pP[:].rearrange("p a b -> p (a b)"), pOT[:]]
                    pres = ["pP", "pOT"]
                    for c in range(8):
                        j, half = c // 2, c % 2
                        sl = c % 4
                        P.op("pe", lambda e, j=j, half=half, sl=sl, c=c: e.matmul(pbank[c // 4][:, sl * 128:(sl + 1) * 128],
                                                                                  lhsT=(ktv if half == 0 else ktv2)[:, j, :],
                                                                                  rhs=vtv[:, j, :], start=True, stop=True),
                             reads=["B3", "B6", "B4"], writes=[pres[c // 4]], signal=(sl == 3))
                    for c in range(8):
                        sl = c % 4
                        P.op("act", lambda e, c=c, sl=sl: e.activation(out=PT[sl][:], in_=pbank[c // 4][:, sl * 128:(sl + 1) * 128],
                                                                       func=AF.Copy, scale=E1[:, c * 64 + 63:c * 64 + 64]),
                             reads=[pres[c // 4], "F5"], writes=[("PT", sl)])
                        P.op("dve", lambda e, c=c: e.tensor_scalar_mul(out=smid[:, c, :], in0=Sst[:, hh, :],
                                                                       scalar1=eb[:, c * 64 + 31:c * 64 + 32]),
                             reads=[("S", hh), "F7"], writes=[("smid", c)])
                        P.op("dve", lambda e, c=c, sl=sl: e.scalar_tensor_tensor(out=Sst[:, hh, :], in0=Sst[:, hh, :],
                                                                                 scalar=eb[:, c * 64 + 63:c * 64 + 64],
                                                                                 in1=PT[sl][:], op0=ALU.mult, op1=ALU.add),
                             reads=[("S", hh), "F7", ("PT", sl)], writes=[("S", hh)])
                    P.op("act", lambda e: e.activation(out=GS[gsb][:], in_=G[bank][:], func=AF.Silu),
                         reads=[("G", bank)], writes=[("GS", gsb)])

                    def stage2a():
                        for j in range(4):
                            for half in range(2):
                                c = 2 * j + half
                                P.op("pe", lambda e, c=c, half=half: e.matmul(pOT[:, c * 64:(c + 1) * 64], lhsT=smid[:, c, :],
                                                                              rhs=Bt[0][:, c * 64:(c + 1) * 64],
                                                                              start=(half == 0), stop=False, skip_group_check=True),
                                     reads=[("smid", c), "B0"], writes=["pOT"], signal=False)
                            P.op("pe", lambda e, j=j: e.matmul(pOT[:, j * 128:(j + 1) * 128], lhsT=vtv[:, j, :],
                                                               rhs=sTm[:, j * 128:(j + 1) * 128], start=False, stop=True,
                                                               skip_group_check=True),
                                 reads=["B4", "B5"], writes=["pOT"], signal=(j == 3))
                        s_ = sqc[0] % 2
                        sqc[0] += 1
                        P.op("act", lambda e: e.activation(out=SQ[s_][:], in_=pOT[:], func=AF.Square), reads=["pOT"], writes=[("SQ", s_)])

                        def stage2b():
                            tb = 1 - gsb
                            P.op("pe", lambda e: e.matmul(pST[:], lhsT=ones, rhs=SQ[s_][:], start=True, stop=True),
                                 reads=[("SQ", s_), "cstb"], writes=["pST"])
                            P.op("act", lambda e: e.activation(out=GS[tb][:], in_=pST[:], func=AF.Ln, scale=1.0 / 128.0, bias=EPS),
                                 reads=["pST"], writes=[("GS", tb)])
                            P.op("act", lambda e: e.activation(out=GS[tb][:], in_=GS[tb][:], func=AF.Exp, scale=-0.5),
                                 reads=[("GS", tb)], writes=[("GS", tb)])
                            P.op("dve", lambda e: e.tensor_tensor(out=Ft[0][:, 0:T], in0=pOT[:], in1=GS[tb][:], op=ALU.mult),
                                 reads=["pOT", ("GS", tb)], writes=["F0"])
                            P.op("dve", lambda e: e.scalar_tensor_tensor(out=mix[:, hh, :], in0=Ft[0][:, 0:T], scalar=vcol(V_HGW + hh),
                                                                         in1=GS[gsb][:], op0=ALU.mult, op1=ALU.mult),
                                 reads=["F0", ("GS", gsb), "vecs"], writes=[("mix", hh)])
                        pipe.defer(1, stage2b)
                    pipe.defer(2, stage2a)

                for off, post in ((0, post_q), (16, post_f), (32, post_i), (48, post_g)):
                    blocks.append(dict(src=win_d[off + hh].rearrange("p (k m) -> p k m", m=128), nk=KC, rhs=rhs_hn, post=post,
                                       pre=(pre_g if off == 48 else None), nosig=(off != 16)))

            if small:
                continue
            wsq = {}

            def wout_ssq(mm):
                s_ = wsq[mm]
                P.op("pe", lambda e: e.matmul(pST[:], lhsT=ones, rhs=SQ[s_][:], start=(mm == 0), stop=(mm == KC - 1)),
                     reads=[("SQ", s_), "cstb"], writes=["pST"])

            for m in range(KC):
                def post_o(bank, m=m):
                    if m > 0:
                        wout_ssq(m - 1)
                    P.op("dve", lambda e: e.tensor_tensor(out=h[:, m, :], in0=h[:, m, :], in1=G[bank][:], op=ALU.add),
                         reads=[("h", m), ("G", bank)], writes=[("h", m)])
                    P.op("act", lambda e: e.activation(out=hn[:, m, :], in_=h[:, m, :], func=AF.Copy, scale=vcol(V_LN2 + m)),
                         reads=[("h", m), "vecs"], writes=[("hn", m)])
                    s_ = sqc[0] % 2
                    sqc[0] += 1
                    wsq[m] = s_
                    P.op("act", lambda e: e.activation(out=SQ[s_][:], in_=h[:, m, :], func=AF.Square),
                         reads=[("h", m)], writes=[("SQ", s_)])
                    if m == KC - 1:
                        wout_ssq(m)
                        P.op("act", lambda e: e.activation(out=rstd2[:], in_=pST[:], func=AF.Ln, scale=1.0 / D, bias=EPS),
                             reads=["pST"], writes=["rstd2"])
                        P.op("act", lambda e: e.activation(out=rstd2[:], in_=rstd2[:], func=AF.Exp, scale=-0.5),
                             reads=["rstd2"], writes=["rstd2"])
                blocks.append(dict(src=wout_d[m].rearrange("p (k m) -> p k m", m=128), nk=KC, rhs=rhs_mix,
                                   pre=(pipe.drain if m == 0 else None), post=post_o, nosig=(m > 0)))

            gsz = [15, 15, 14, 14, 14, 14]
            gst = [sum(gsz[:i]) for i in range(len(gsz))]
            ngrp = len(gsz)
            j2g = {}
            for gi in range(ngrp):
                for jl_ in range(gsz[gi]):
                    j2g[gst[gi] + jl_] = (gi, jl_)

            def ffn_pre():
                pass

            def mk_up(j, which, t=t):
                g, jl = j2g[j]
                ab = (g % 2) * GRP + jl
                fs = (j % 2) * 4
                ci = j + which * NFF

                def post(bank):
                    xs = Ft[fs + which]
                    y = Ft[fs + 2 + which]
                    P.op("dve", lambda e: e.tensor_copy(out=xs[:, 0:2], in_=fc[:, ci, :]), reads=["fc"], writes=[f"F{fs + which}"])
                    P.op("dve", lambda e: e.tensor_tensor(out=xs[:, 2:2 + T], in0=G[bank][:], in1=rstd2[:], op=ALU.mult),
                         reads=[("G", bank), "rstd2"], writes=[f"F{fs + which}"])
                    P.op("dve", lambda e: e.tensor_copy(out=fc[:, ci, :], in_=xs[:, T:T + 2]), reads=[f"F{fs + which}"], writes=["fc"])
                    P.op("dve", lambda e: e.tensor_scalar(out=y[:, 0:T], in0=xs[:, 2:2 + T], scalar1=vcol(V_FCW + 2 * 172 + ci),
                                                          scalar2=vcol(V_FCB + ci), op0=ALU.mult, op1=ALU.add),
                         reads=[f"F{fs + which}", "vecs"], writes=[f"F{fs + 2 + which}"])
                    for jj in (1, 0):
                        P.op("dve", lambda e, jj=jj: e.scalar_tensor_tensor(out=y[:, 0:T], in0=xs[:, jj:jj + T],
                                                                            scalar=vcol(V_FCW + jj * 172 + ci), in1=y[:, 0:T],
                                                                            op0=ALU.mult, op1=ALU.add),
                             reads=[f"F{fs + which}", f"F{fs + 2 + which}", "vecs"], writes=[f"F{fs + 2 + which}"])
                    if which == 0:
                        P.op("act", lambda e: e.activation(out=y[:, 0:T], in_=y[:, 0:T], func=AF.Silu),
                             reads=[f"F{fs + 2}"], writes=[f"F{fs + 2}"])
                    else:
                        P.op("dve", lambda e: e.tensor_tensor(out=mix[:, ab, :], in0=Ft[fs + 2][:, 0:T], in1=y[:, 0:T], op=ALU.mult),
                             reads=[f"F{fs + 2}", f"F{fs + 3}"], writes=[("mix", ab)])
                return dict(src=(wupg_d if which == 0 else wupv_d)[j].rearrange("p (k m) -> p k m", m=128), nk=KC, rhs=rhs_hn, post=post)

            fsq = {}

            def fin_ssq(mm):
                s_ = fsq[mm]
                P.op("pe", lambda e: e.matmul(pST[:], lhsT=ones, rhs=SQ[s_][:], start=(mm == 0), stop=(mm == KC - 1)),
                     reads=[("SQ", s_), "cstb"], writes=["pST"])

            def mk_dn2(g, m):
                k0 = gst[g]
                nk = gsz[g]
                base = (g % 2) * GRP

                def rhs(k):
                    return mix[:, base + k, :], ("mix", base + k)

                def mkpost(mm):
                    def post(bank):
                        last = (g == ngrp - 1)
                        if last and mm > 0:
                            fin_ssq(mm - 1)
                        P.op("dve", lambda e: e.tensor_tensor(out=h[:, mm, :], in0=h[:, mm, :], in1=G[bank][:], op=ALU.add),
                             reads=[("h", mm), ("G", bank)], writes=[("h", mm)])
                        if last:
                            s_ = sqc[0] % 2
                            sqc[0] += 1
                            fsq[mm] = s_
                            P.op("act", lambda e: e.activation(out=SQ[s_][:], in_=h[:, mm, :], func=AF.Square),
                                 reads=[("h", mm)], writes=[("SQ", s_)])
                    return post

                def dma(e, wbuf):
                    return e.dma_start(out=wbuf[:, 0:2 * nk, :].rearrange("p (a k) m -> p a k m", a=2),
                                       in_=wdn_d[m:m + 2][:, :, k0 * 128:(k0 + nk) * 128].rearrange("a p (k m) -> p a k m", m=128))
                return dict(dma=dma, subs=[dict(k0=0, nk=nk, rhs=rhs, post=mkpost(m), nosig=True),
                                           dict(k0=nk, nk=nk, rhs=rhs, post=mkpost(m + 1))])

            first = True
            for g in range(ngrp):
                js = list(range(gst[g], gst[g] + gsz[g]))
                for idx, j in enumerate(js):
                    bq = mk_up(j, 0)
                    if first:
                        bq["pre"] = ffn_pre
                        first = False
                    blocks.append(bq)
                    blocks.append(mk_up(j, 1))
                    if idx == 1 and g > 0:
                        for m in range(0, KC, 2):
                            blocks.append(mk_dn2(g - 1, m))
            for m in range(0, KC, 2):
                blocks.append(mk_dn2(ngrp - 1, m))

            def tile_end(t0=t0, fin_ssq=fin_ssq, t=t):
                fin_ssq(KC - 1)
                if t + 1 < ntiles:
                    pe_filler(48)
                P.op("act", lambda e: e.activation(out=Ft[2][:, 0:T], in_=pST[:], func=AF.Ln, scale=1.0 / D, bias=EPS),
                     reads=["pST"], writes=["F2"])
                P.op("act", lambda e: e.activation(out=Ft[4][:, 0:T], in_=Ft[2][:, 0:T], func=AF.Exp, scale=-0.5),
                     reads=["F2"], writes=["F4"])
                for c in range(KC):
                    P.op("dve", lambda e, c=c: e.scalar_tensor_tensor(out=h[:, c, :], in0=h[:, c, :], scalar=vcol(V_FIN + c),
                                                                     in1=Ft[4][:, 0:T], op0=ALU.mult, op1=ALU.mult),
                         reads=[("h", c), "F4", "vecs"], writes=[("h", c)])
                    if c % 4 == 3:
                        g8 = c // 4
                        P.op("sp", lambda e, g8=g8: e.dma_start(out=out_d[:, 4 * g8:4 * g8 + 4, t0:t0 + T], in_=h[:, 4 * g8:4 * g8 + 4, :]),
                             reads=[("h", cc) for cc in range(4 * g8, 4 * g8 + 4)], dma_sem=f"D_o{g8}")
            blocks[-1]["post_tile"] = tile_end

        for b in blocks:
            if "post_tile" in b:
                sub = b["subs"][-1]
                p0, p1 = sub["post"], b["post_tile"]
                sub["post"] = (lambda bank, p0=p0, p1=p1: (p0(bank), p1()))

        if maxblocks is not None:
            del blocks[maxblocks:]
        run_blocks()
        pipe.drain()
        P.final_waits("sp")
        sems = {n: st.enter_context(nc.semaphore(n)) for n in sorted(P.cnt.keys())}
        P.emit(nc, sems)
    return nc


def _colmajor(v, n):
    return np.ascontiguousarray(np.asarray(v, dtype=np.float32).reshape(n, 128).T)


def _wblocks(w, nblk):
    K = w.shape[0]
    kc = K // 128
    return np.ascontiguousarray(w.reshape(kc, 128, nblk, 128).transpose(2, 1, 0, 3)).reshape(nblk, 128, kc * 128)


def _prep_shared(inp):
    vec = np.zeros((128, NV), np.float32)
    vec[:, V_LN1:V_LN1 + 32] = _colmajor(inp["ln1_w"][0], 32)
    vec[:, V_LN2:V_LN2 + 32] = _colmajor(inp["ln2_w"][0], 32)
    vec[:, V_FIN:V_FIN + 32] = _colmajor(inp["final_norm_w"], 32)
    vec[:, V_G0:V_G0 + 16] = _colmajor(inp["lb_gamma"][0], 16)
    vec[:, V_G1:V_G1 + 16] = _colmajor(inp["lb_gamma"][1], 16)
    vec[:, V_HGW:V_HGW + 16] = _colmajor(inp["hg_norm_w"][0], 16)
    for j in range(4):
        vec[:, V_CW + j * 16:V_CW + (j + 1) * 16] = _colmajor(inp["lru_conv_w"][0, j], 16)
    vec[:, V_CB:V_CB + 16] = _colmajor(inp["lru_conv_b"][0], 16)
    vec[:, V_BA:V_BA + 16] = _colmajor(inp["lru_ba"][0], 16)
    vec[:, V_BX:V_BX + 16] = _colmajor(inp["lru_bx"][0], 16)
    vec[:, V_LAM:V_LAM + 16] = _colmajor(inp["lru_lambda"][0], 16)
    vec[:, V_LRUW:V_LRUW + 16] = _colmajor(inp["lru_norm_w"][0], 16)
    for j in range(3):
        vec[:, V_FCW + j * 172:V_FCW + (j + 1) * 172] = _colmajor(inp["ffn_conv_w"][0, j], 172)
    vec[:, V_FCB:V_FCB + 172] = _colmajor(inp["ffn_conv_b"][0], 172)

    cstb = np.zeros((128, 256), np.float32)
    cstb[:, 0:128] = np.eye(128, dtype=np.float32)
    cstb[:, 128:256] = 1.0
    cstb = cstb.astype(ml_dtypes.bfloat16)
    cstf = np.ones((128, 1024), np.float32)
    cstf[:, 0:512:64] = 0.0
    s = np.arange(128)[:, None]
    tt = np.arange(128)[None, :]
    pm = ((s // 64 == tt // 64) & (s <= tt)).astype(np.float32)
    cstf[:, 512:1024] = np.tile(pm, (1, 4))

    wa = np.asarray(inp["lru_wa"][0], np.float32)
    wx = np.asarray(inp["lru_wx"][0], np.float32)
    wlru = np.ascontiguousarray(np.stack([wa, wx], axis=2)).reshape(NL, 128, 256)
    shared = {
        "w_in": _wblocks(np.asarray(inp["w_in"][0], np.float32), 96),
        "w_out": _wblocks(np.asarray(inp["w_out"][0], np.float32), 32),
        "w_upg": _wblocks(np.asarray(inp["ffn_w_up"][0][:, :DFF], np.float32), NFF),
        "w_upv": _wblocks(np.asarray(inp["ffn_w_up"][0][:, DFF:], np.float32), NFF),
        "w_dn": _wblocks(np.asarray(inp["ffn_w_down"][0], np.float32), 32),
        "w_lru": wlru,
        "vecs": vec,
        "cstb": cstb,
        "cstf": cstf,
    }
    return shared


def kernel(**inputs):
    x = np.asarray(inputs["x"], np.float32)
    B = x.shape[0]
    shared = _prep_shared(inputs)
    nc = build_program()
    in_maps = []
    for b in range(B):
        xf = np.ascontiguousarray(x[b].T.reshape(KC, 128, SEQ).transpose(1, 0, 2))
        m = dict(shared)
        m["x"] = xf
        in_maps.append(m)
    res = run_bass_kernel_spmd(nc, in_maps, core_ids=list(range(B)))
    outs = []
    for b in range(B):
        o = np.asarray(res.results[b]["out"], np.float32)
        outs.append(np.ascontiguousarray(o.transpose(1, 0, 2).reshape(D, SEQ).T))
    return np.stack(outs, axis=0)
```

```python
import numpy as np
import ml_dtypes
from contextlib import ExitStack
import concourse.bass as bass
import concourse.mybir as mybir
from concourse.bass_utils import run_bass_kernel_spmd

F32 = mybir.dt.float32
BF16 = mybir.dt.bfloat16
AF = mybir.ActivationFunctionType
ALU = mybir.AluOpType

D = 4096
SEQ = 2048
T = 512
NT = SEQ // T
KC = D // 128
NH = 16
NL = 16
DFF = 11008
NFF = DFF // 128
GRP = 16
EPS = 1e-6
NB = 3
NG = 4

V_LN1, V_LN2, V_FIN, V_G0, V_G1, V_HGW = 0, 32, 64, 96, 112, 128
V_CW, V_CB, V_BA, V_BX, V_LAM, V_LRUW = 144, 208, 224, 240, 256, 272
V_FCW, V_FCB, NV = 288, 804, 976
DV_LB, DV_NOML, DV_LNOML, DV_C8, DV_C16, DV_SCR, DV_BAH, DV_BXH, DV_C8H, DV_S1, DV_B1, DV_LNH, NDV = 0, 16, 32, 48, 64, 80, 96, 112, 128, 144, 160, 176, 192

ENGS = ["pe", "act", "dve", "pool", "sp"]


class Prog:
    def __init__(self):
        self.ops = {e: [] for e in ENGS}
        self.cnt = {}
        self.seen = {e: {} for e in ENGS}
        self.lastw = {}
        self.readers = {}

    def op(self, eng, fn, reads=(), writes=(), signal=True, dma_sem=None, pre_r=(), pre_w=()):
        own = "S_" + eng
        deps = []
        for r in list(reads) + list(pre_r):
            if r in self.lastw:
                deps.append(self.lastw[r])
        for w in list(writes) + list(pre_w):
            if w in self.lastw:
                deps.append(self.lastw[w])
            deps += self.readers.get(w, [])
        waits = {}
        for (s, v) in deps:
            if eng == "pe" and s == own:
                continue
            if self.seen[eng].get(s, 0) < v:
                waits[s] = max(waits.get(s, 0), v)
        for s, v in waits.items():
            self.seen[eng][s] = v
        if dma_sem is not None:
            self.cnt[dma_sem] = self.cnt.get(dma_sem, 0) + 16
            tag = (dma_sem, self.cnt[dma_sem])
            inc = (dma_sem, 16)
        elif signal:
            self.cnt[own] = self.cnt.get(own, 0) + 1
            tag = (own, self.cnt[own])
            inc = (own, 1)
        else:
            tag = (own, self.cnt.get(own, 0) + 1)
            inc = None
        for r in reads:
            self.readers.setdefault(r, []).append(tag)
        for w in writes:
            self.lastw[w] = tag
            self.readers[w] = []
        self.ops[eng].append((waits, fn, inc))

    def final_waits(self, eng):
        waits = {}
        for s, v in self.cnt.items():
            if self.seen[eng].get(s, 0) < v:
                waits[s] = v
        self.ops[eng].append((waits, None, None))

    def emit(self, nc, sems):
        with nc.Block() as block:
            def run(engname, e):
                for (waits, fn, inc) in self.ops[engname]:
                    for s, v in waits.items():
                        e.wait_ge(sems[s], v)
                    if fn is None:
                        continue
                    ins = fn(e)
                    if inc is not None:
                        ins.then_inc(sems[inc[0]], inc[1])

            @block.tensor
            def _(e):
                run("pe", e)

            @block.scalar
            def _(e):
                run("act", e)

            @block.vector
            def _(e):
                run("dve", e)

            @block.gpsimd
            def _(e):
                run("pool", e)

            @block.sync
            def _(e):
                run("sp", e)


class Pipe:
    def __init__(self):
        self.blk = 0
        self.pending = []

    def defer(self, delay, fn):
        self.pending.append((self.blk + delay, fn))

    def flush(self):
        while True:
            due = [p for p in self.pending if p[0] <= self.blk]
            if not due:
                return
            p = due[0]
            self.pending.remove(p)
            p[1]()

    def tick(self):
        self.blk += 1
        self.flush()

    def drain(self):
        while self.pending:
            self.blk = max(self.blk, min(p[0] for p in self.pending))
            self.flush()


def build_program(ntiles=NT, maxblocks=None, small=False):
    nc = bass.Bass("TRN2", target_bir_lowering=False)
    x_d = nc.dram_tensor("x", [128, KC, SEQ], F32, kind="ExternalInput").ap()
    out_d = nc.dram_tensor("out", [128, KC, SEQ], F32, kind="ExternalOutput").ap()
    win_d = nc.dram_tensor("w_in", [96, 128, KC * 128], F32, kind="ExternalInput").ap()
    if not small:
        wout_d = nc.dram_tensor("w_out", [32, 128, KC * 128], F32, kind="ExternalInput").ap()
        wupg_d = nc.dram_tensor("w_upg", [NFF, 128, KC * 128], F32, kind="ExternalInput").ap()
        wupv_d = nc.dram_tensor("w_upv", [NFF, 128, KC * 128], F32, kind="ExternalInput").ap()
        wdn_d = nc.dram_tensor("w_dn", [32, 128, NFF * 128], F32, kind="ExternalInput").ap()
    wlru_d = nc.dram_tensor("w_lru", [NL, 128, 256], F32, kind="ExternalInput").ap()
    vecs_d = nc.dram_tensor("vecs", [128, NV], F32, kind="ExternalInput").ap()
    cstb_d = nc.dram_tensor("cstb", [128, 256], BF16, kind="ExternalInput").ap()
    cstf_d = nc.dram_tensor("cstf", [128, 1024], F32, kind="ExternalInput").ap()

    P = Prog()
    pipe = Pipe()
    with ExitStack() as st:
        def sb(name, shape, dt):
            return st.enter_context(nc.sbuf_tensor(name, shape, dt))

        def ps(name, shape, dt):
            return st.enter_context(nc.psum_tensor(name, shape, dt))

        h = sb("h", [128, KC, T], F32)
        hn = sb("hn", [128, KC, T], BF16)
        mix = sb("mix", [128, KC, T], BF16)
        wb = [sb(f"wb{i}", [128, KC, 128], BF16) for i in range(NB)]
        lw = [sb(f"lw{i}", [128, 2, 128], BF16) for i in range(2)]
        vecs = sb("vecs_sb", [128, NV], F32)
        dv = sb("dv", [128, NDV], F32)
        cstb = sb("cstb_sb", [128, 256], BF16)
        cstf = sb("cstf_sb", [128, 1024], F32)
        Sst = sb("Sst", [128, NH, 128], F32)
        lstate = sb("lstate", [128, NL], F32)
        lc = sb("lc", [128, NL, 3], F32)
        fc = sb("fc", [128, 2 * NFF, 2], F32)
        Ft = [sb(f"F{i}", [128, w_], F32) for i, w_ in enumerate([516, 514, 512, 512, 514, 514, 512, 512])]
        GS = [sb(f"GS{i}", [128, T], F32) for i in range(2)]
        wb3t = sb("wb3", [128, KC, 128], BF16)
        wb.append(wb3t)
        Bt = [wb3t[:, 4 * i:4 * i + 4, :].rearrange("p a b -> p (a b)") for i in range(7)]
        BRES = [f"B{i}" for i in range(7)]
        smid = sb("smid", [128, 8, 128], BF16)
        SQ = [sb(f"SQ{i}", [128, T], BF16) for i in range(2)]
        PT = [sb(f"PT{i}", [128, 128], F32) for i in range(4)]
        rstd2 = sb("rstd2", [128, T], F32)

        G = [ps(f"G{i}", [128, T], F32) for i in range(NG)]
        pST = ps("pST", [128, T], F32)
        pOT = ps("pOT", [128, T], F32)
        pTR = ps("pTR", [128, 2, 4, 128], BF16)
        pP = ps("pP", [128, 4, 128], F32)

        ident = cstb[:, 0:128]
        ones = cstb[:, 128:256]
        maskm = cstf[:, 0:512]
        mask4 = cstf[:, 512:1024]

        def vcol(off):
            return vecs[:, off:off + 1]

        def dcol(off):
            return dv[:, off:off + 1]

        P.op("sp", lambda e: e.dma_start(out=vecs[:], in_=vecs_d), writes=["vecs"], dma_sem="D_m0")
        P.op("sp", lambda e: e.dma_start(out=cstb[:], in_=cstb_d), writes=["cstb"], dma_sem="D_m1")
        P.op("sp", lambda e: e.dma_start(out=cstf[:], in_=cstf_d), writes=["cstf"], dma_sem="D_m2")
        P.op("dve", lambda e: e.memset(Sst[:], 0.0), writes=[("S", i) for i in range(NH)])
        P.op("dve", lambda e: e.memset(lstate[:], 0.0), writes=["lstate"])
        P.op("dve", lambda e: e.memset(lc[:], 0.0), writes=["lc"])
        P.op("dve", lambda e: e.memset(fc[:], 0.0), writes=["fc"])
        P.op("dve", lambda e: e.memset(smid[:], 0.5), writes=[("smid", c) for c in range(8)])
        P.op("dve", lambda e: e.memset(Bt[3][:], 0.0), writes=["B3"])
        P.op("dve", lambda e: e.memset(Bt[6][:], 0.0), writes=["B6"])
        P.op("dve", lambda e: e.tensor_tensor(out=dv[:, DV_SCR:DV_SCR + 16], in0=vecs[:, V_G0:V_G0 + 16],
                                              in1=vecs[:, V_G1:V_G1 + 16], op=ALU.subtract),
             reads=["vecs"], writes=["dvs"])
        P.op("act", lambda e: e.activation(out=dv[:, DV_LB:DV_LB + 16], in_=dv[:, DV_SCR:DV_SCR + 16], func=AF.Sigmoid),
             reads=["dvs"], writes=["dv_lb"])
        P.op("dve", lambda e: e.tensor_scalar_add(out=dv[:, DV_NOML:DV_NOML + 16], in0=dv[:, DV_LB:DV_LB + 16], scalar1=-1.0),
             reads=["dv_lb"], writes=["dv_noml"])
        P.op("act", lambda e: e.activation(out=dv[:, DV_LNOML:DV_LNOML + 16], in_=dv[:, DV_LB:DV_LB + 16], func=AF.Ln,
                                           scale=-1.0, bias=1.0),
             reads=["dv_lb"], writes=["dv_lnoml"])
        P.op("act", lambda e: e.activation(out=dv[:, DV_SCR:DV_SCR + 16], in_=vecs[:, V_LAM:V_LAM + 16], func=AF.Exp, scale=-1.0),
             reads=["vecs", "dvs"], writes=["dvs"])
        P.op("act", lambda e: e.activation(out=dv[:, DV_SCR:DV_SCR + 16], in_=dv[:, DV_SCR:DV_SCR + 16], func=AF.Ln, bias=1.0),
             reads=["dvs"], writes=["dvs"])
        P.op("dve", lambda e: e.tensor_scalar_mul(out=dv[:, DV_C8:DV_C8 + 16], in0=dv[:, DV_SCR:DV_SCR + 16], scalar1=-8.0),
             reads=["dvs"], writes=["dv_c8"])
        P.op("dve", lambda e: e.tensor_scalar_mul(out=dv[:, DV_C16:DV_C16 + 16], in0=dv[:, DV_SCR:DV_SCR + 16], scalar1=-16.0),
             reads=["dvs"], writes=["dv_c16"])
        P.op("dve", lambda e: e.tensor_scalar_mul(out=dv[:, DV_C8H:DV_C8H + 16], in0=dv[:, DV_SCR:DV_SCR + 16], scalar1=-4.0),
             reads=["dvs"], writes=["dv_c8h"])
        P.op("dve", lambda e: e.tensor_scalar_mul(out=dv[:, DV_BAH:DV_BAH + 16], in0=vecs[:, V_BA:V_BA + 16], scalar1=0.5),
             reads=["vecs"], writes=["dv_bah"])
        P.op("dve", lambda e: e.tensor_scalar_mul(out=dv[:, DV_BXH:DV_BXH + 16], in0=vecs[:, V_BX:V_BX + 16], scalar1=0.5),
             reads=["vecs"], writes=["dv_bxh"])
        P.op("dve", lambda e: e.tensor_scalar_mul(out=dv[:, DV_S1:DV_S1 + 16], in0=dv[:, DV_NOML:DV_NOML + 16], scalar1=0.5),
             reads=["dv_noml"], writes=["dv_s1"])
        P.op("dve", lambda e: e.tensor_scalar_add(out=dv[:, DV_B1:DV_B1 + 16], in0=dv[:, DV_S1:DV_S1 + 16], scalar1=1.0),
             reads=["dv_s1"], writes=["dv_b1"])
        P.op("act", lambda e: e.activation(out=dv[:, DV_LNH:DV_LNH + 16], in_=dv[:, DV_LB:DV_LB + 16], func=AF.Ln,
                                           scale=-0.5, bias=0.5),
             reads=["dv_lb"], writes=["dv_lnh"])
        CONST = ["vecs", "cstb", "cstf", "dv_lb", "dv_noml", "dv_lnoml", "dv_c8", "dv_c16"]

        sqc = [0]

        def rmsnorm_to_hn(wcol):
            for c in range(KC):
                s = sqc[0] % 2
                sqc[0] += 1
                P.op("act", lambda e, c=c, s=s: e.activation(out=SQ[s][:], in_=h[:, c, :], func=AF.Square),
                     reads=[("h", c)], writes=[("SQ", s)])
                P.op("pe", lambda e, c=c, s=s: e.matmul(pST[:], lhsT=ones, rhs=SQ[s][:], start=(c == 0), stop=(c == KC - 1)),
                     reads=[("SQ", s), "cstb"], writes=["pST"], signal=True)
                pe_filler(2)
            P.op("act", lambda e: e.activation(out=Ft[2][:, 0:T], in_=pST[:], func=AF.Ln, scale=1.0 / D, bias=EPS),
                 reads=["pST"], writes=["F2"])
            P.op("act", lambda e: e.activation(out=Ft[4][:, 0:T], in_=Ft[2][:, 0:T], func=AF.Exp, scale=-0.5),
                 reads=["F2"], writes=["F4"])
            for c in range(KC):
                P.op("dve", lambda e, c=c: e.scalar_tensor_tensor(out=hn[:, c, :], in0=h[:, c, :], scalar=vcol(wcol + c),
                                                                 in1=Ft[4][:, 0:T], op0=ALU.mult, op1=ALU.mult),
                     reads=[("h", c), "F4", "vecs"], writes=[("hn", c)])

        blocks = []

        bufs = []

        def assign_bufs():
            last_used = {0: -1, 1: -2, 2: -3, 3: -4}
            for i, b in enumerate(blocks):
                allowed = [0, 1, 2, 3] if b.get("ring4") else [0, 1, 2]
                bb = min(allowed, key=lambda x: last_used[x])
                bufs.append(bb)
                last_used[bb] = i

        def wres(buf):
            return [("wb", buf)] + (BRES if buf == 3 else [])

        def issue_wdma(i):
            b = blocks[i]
            buf = bufs[i]
            if "dma" in b:
                P.op("pool", lambda e, b=b, buf=buf: b["dma"](e, wb[buf]), writes=wres(buf), dma_sem=f"D_w{buf}")
                return
            nk = b["nk"]
            P.op("pool", lambda e, b=b, buf=buf, nk=nk: e.dma_start(
                out=wb[buf][:, 0:nk, :], in_=b["src"]), writes=wres(buf), dma_sem=f"D_w{buf}")

        bankc = [0]

        def run_blocks():
            assign_bufs()
            nxt = [0]
            for i, b in enumerate(blocks):
                if b.get("pre"):
                    b["pre"]()
                while nxt[0] < len(blocks) and nxt[0] - (3 if blocks[nxt[0]].get("ring4") else 2) <= i:
                    issue_wdma(nxt[0])
                    nxt[0] += 1
                buf = bufs[i]
                subs = b.get("subs") or [dict(k0=0, nk=b["nk"], rhs=b["rhs"], post=b["post"], nosig=b.get("nosig", False))]
                for si, sub in enumerate(subs):
                    bank = bankc[0] % NG
                    bankc[0] += 1
                    nk, k0 = sub["nk"], sub["k0"]
                    for k in range(nk):
                        rap, rres = sub["rhs"](k)
                        pre_r, pre_w = [], []
                        if k == nk - 1:
                            pre_w.append(("G", bankc[0] % NG))
                            if si == len(subs) - 1 and i + 1 < len(blocks):
                                pre_r.append(("wb", bufs[i + 1]))
                        P.op("pe", lambda e, buf=buf, bank=bank, k=k, rap=rap, nk=nk, k0=k0: e.matmul(
                            G[bank][:], lhsT=wb[buf][:, k0 + k, :], rhs=rap, start=(k == 0), stop=(k == nk - 1)),
                            reads=wres(buf) + [rres], writes=[("G", bank)], signal=(k == nk - 1 and not sub.get("nosig")),
                            pre_r=pre_r, pre_w=pre_w)
                        if b.get("fill") and k < nk - 1:
                            pe_filler(1)
                    sub["post"](bank)
                pipe.tick()

        def pe_filler(n, bank_off=0):
            bank = (bankc[0] + bank_off) % NG
            for _ in range(n):
                P.op("pe", lambda e, bank=bank: e.matmul(G[bank][:], lhsT=ident, rhs=smid[:, 0:4, :].rearrange("p a b -> p (a b)"),
                                                         start=True, stop=True),
                     reads=["cstb"] + [("smid", c) for c in range(4)], writes=[("G", bank)], signal=False)

        def rhs_hn(k):
            return hn[:, k, :], ("hn", k)

        def rhs_mix(k):
            return mix[:, k, :], ("mix", k)

        for t in range(ntiles):
            t0 = t * T

            def tile_start(t=t, t0=t0):
                for g8 in range(8):
                    P.op("sp", lambda e, g8=g8: e.dma_start(out=h[:, 4 * g8:4 * g8 + 4, :], in_=x_d[:, 4 * g8:4 * g8 + 4, t0:t0 + T]),
                         writes=[("h", c) for c in range(4 * g8, 4 * g8 + 4)], dma_sem=f"D_h{g8}")
                rmsnorm_to_hn(V_LN1)
                P.op("dve", lambda e: e.memset(Bt[3][64:128, :], 0.0), writes=["B3"])
                P.op("dve", lambda e: e.memset(Bt[6][0:64, :], 0.0), writes=["B6"])

            lru_sq = {}

            def mk_post_x(n, t=t):
                def post_x(bank):
                    l = n % 2
                    par = n % 2
                    P.op("pool", lambda e: e.dma_start(out=lw[l][:], in_=wlru_d[n].rearrange("p (a m) -> p a m", m=128)),
                         writes=[("lw", l)], dma_sem=f"D_lw{l}")
                    xs = Ft[0]
                    xb = GS[par]
                    P.op("dve", lambda e: e.tensor_copy(out=xs[:, 0:3], in_=lc[:, n, :]), reads=["lc"], writes=["F0"])
                    P.op("act", lambda e: e.activation(out=xs[:, 3:3 + T], in_=G[bank][:], func=AF.Copy),
                         reads=[("G", bank)], writes=["F0"])
                    P.op("dve", lambda e: e.tensor_copy(out=lc[:, n, :], in_=xs[:, T:T + 3]), reads=["F0"], writes=["lc"])
                    P.op("dve", lambda e: e.tensor_scalar(out=xb[:], in0=xs[:, 3:3 + T], scalar1=vcol(V_CW + 3 * 16 + n),
                                                          scalar2=vcol(V_CB + n), op0=ALU.mult, op1=ALU.add),
                         reads=["F0", "vecs"], writes=[("GS", par)])
                    for j in (2, 1, 0):
                        P.op("dve", lambda e, j=j: e.scalar_tensor_tensor(out=xb[:], in0=xs[:, j:j + T],
                                                                          scalar=vcol(V_CW + j * 16 + n), in1=xb[:],
                                                                          op0=ALU.mult, op1=ALU.add),
                             reads=["F0", ("GS", par), "vecs"], writes=[("GS", par)])
                    if n == 0:
                        P.op("act", lambda e: e.activation(out=Bt[par][:], in_=xb[:], func=AF.Copy),
                             reads=[("GS", par)], writes=[f"B{par}"])
                return post_x

            def mk_post_y(n, t=t):
                def post_y(bank):
                    l = n % 2
                    par = n % 2
                    P.op("act", lambda e: e.activation(out=Ft[7][:, 0:T], in_=G[bank][:], func=AF.Gelu_apprx_tanh),
                         reads=[("G", bank)], writes=["F7"])

                    def ssq_mm(nn):
                        s_ = lru_sq[nn]
                        P.op("pe", lambda e: e.matmul(pP[:].rearrange("p a b -> p (a b)"), lhsT=ones, rhs=SQ[s_][:],
                                                      start=(nn == 0), stop=(nn == NL - 1)),
                             reads=[("SQ", s_), "cstb"], writes=["pP"])

                    def fin():
                        ssq_mm(NL - 1)
                        P.op("act", lambda e: e.activation(out=Ft[3][:, 0:T], in_=pP[:].rearrange("p a b -> p (a b)"),
                                                           func=AF.Ln, scale=1.0 / 2048.0, bias=EPS),
                             reads=["pP"], writes=["F3"])
                        P.op("act", lambda e: e.activation(out=Ft[5][:, 0:T], in_=Ft[3][:, 0:T], func=AF.Exp, scale=-0.5),
                             reads=["F3"], writes=["F5"])
                        for nn in range(NL):
                            P.op("dve", lambda e, nn=nn: e.tensor_tensor(out=mix[:, NH + nn, :], in0=mix[:, NH + nn, :],
                                                                         in1=Ft[5][:, 0:T], op=ALU.mult),
                                 reads=[("mix", NH + nn), "F5"], writes=[("mix", NH + nn)])

                    def stage1():
                        if n > 0:
                            ssq_mm(n - 1)
                        P.op("pe", lambda e: e.matmul(pST[:], lhsT=lw[l][:, 0, :], rhs=Bt[par][:], start=True, stop=True),
                             reads=[("lw", l), f"B{par}"], writes=["pST"])
                        P.op("pe", lambda e: e.matmul(pOT[:], lhsT=lw[l][:, 1, :], rhs=Bt[par][:], start=True, stop=True),
                             reads=[("lw", l), f"B{par}"], writes=["pOT"])
                        thr, thi, a, a2 = Ft[2], Ft[3], Ft[4], Ft[5]
                        P.op("act", lambda e: e.activation(out=thr[:, 0:T], in_=pST[:], func=AF.Tanh, scale=0.5, bias=dcol(DV_BAH + n)),
                             reads=["pST", "dv_bah"], writes=["F2"])
                        P.op("act", lambda e: e.activation(out=thi[:, 0:T], in_=pOT[:], func=AF.Tanh, scale=0.5, bias=dcol(DV_BXH + n)),
                             reads=["pOT", "dv_bxh"], writes=["F3"])
                        P.op("act", lambda e: e.activation(out=a[:, 0:T], in_=thr[:, 0:T], func=AF.Exp, scale=dcol(DV_C8H + n),
                                                           bias=dcol(DV_C8H + n)),
                             reads=["F2", "dv_c8h"], writes=["F4"])
                        P.op("act", lambda e: e.activation(out=a2[:, 0:T], in_=thr[:, 0:T], func=AF.Exp, scale=dcol(DV_C8 + n),
                                                           bias=dcol(DV_C8 + n)),
                             reads=["F2", "dv_c8"], writes=["F5"])
                        P.op("act", lambda e: e.activation(out=a2[:, 0:T], in_=a2[:, 0:T], func=AF.Sqrt, scale=-1.0, bias=1.0),
                             reads=["F5"], writes=["F5"])
                        if n + 1 < NL:
                            P.op("act", lambda e: e.activation(out=Bt[1 - par][:], in_=GS[1 - par][:], func=AF.Copy),
                                 reads=[("GS", 1 - par)], writes=[f"B{1 - par}"])
                        if t == 0:
                            P.op("dve", lambda e: e.memset(a2[:, 0:1], 1.0), reads=["F5"], writes=["F5"])
                        u = GS[par]
                        P.op("dve", lambda e: e.scalar_tensor_tensor(out=u[:], in0=thi[:, 0:T], scalar=1.0, in1=u[:],
                                                                     op0=ALU.add, op1=ALU.mult),
                             reads=[("GS", par), "F3"], writes=[("GS", par)])
                        P.op("dve", lambda e: e.tensor_tensor(out=u[:], in0=u[:], in1=a2[:, 0:T], op=ALU.mult),
                             reads=[("GS", par), "F5"], writes=[("GS", par)])
                        hst = Ft[6]
                        P.op("dve", lambda e: e.tensor_tensor_scan(out=hst[:, 0:T], data0=a[:, 0:T], data1=u[:],
                                                                   initial=lstate[:, n:n + 1], op0=ALU.mult, op1=ALU.add),
                             reads=["F4", ("GS", par), "lstate"], writes=["F6"])
                        P.op("dve", lambda e: e.tensor_copy(out=lstate[:, n:n + 1], in_=hst[:, T - 1:T]),
                             reads=["F6"], writes=["lstate"])
                        o = Ft[2]
                        P.op("dve", lambda e: e.scalar_tensor_tensor(out=o[:, 0:T], in0=hst[:, 0:T], scalar=0.5, in1=Ft[7][:, 0:T],
                                                                     op0=ALU.mult, op1=ALU.mult),
                             reads=["F6", "F7", "F2"], writes=["F2"])
                        s_ = sqc[0] % 2
                        sqc[0] += 1
                        lru_sq[n] = s_
                        P.op("act", lambda e: e.activation(out=SQ[s_][:], in_=o[:, 0:T], func=AF.Square),
                             reads=["F2"], writes=[("SQ", s_)])
                        P.op("act", lambda e: e.activation(out=mix[:, NH + n, :], in_=o[:, 0:T], func=AF.Copy,
                                                           scale=vcol(V_LRUW + n)),
                             reads=["F2", "vecs"], writes=[("mix", NH + n)])
                        if n == NL - 1:
                            pipe.defer(1, fin)
                    pipe.defer(1, stage1)
                return post_y

            def lru_blk(which, n):
                off = 64 if which == 0 else 80
                return dict(src=win_d[off + n].rearrange("p (k m) -> p k m", m=128), nk=KC, rhs=rhs_hn,
                            post=(mk_post_x(n) if which == 0 else mk_post_y(n)), nosig=(which == 1))

            b0 = lru_blk(0, 0)
            b0["fill"] = True
            b0["pre"] = tile_start
            blocks.append(b0)
            for n in range(NL):
                if n + 1 < NL:
                    blocks.append(lru_blk(0, n + 1))
                blocks.append(lru_blk(1, n))

            for hh in range(NH):
                def post_q(bank, hh=hh):
                    P.op("act", lambda e: e.activation(out=Ft[0][:, 0:T], in_=G[bank][:], func=AF.Silu),
                         reads=[("G", bank)], writes=["F0"])

                def post_f(bank, hh=hh):
                    sgn, g, b, dd, E1, E2, eb = Ft[1], Ft[2], Ft[3], Ft[4], Ft[5], Ft[6], Ft[7]
                    P.op("act", lambda e: e.activation(out=sgn[:, 0:T], in_=G[bank][:], func=AF.Tanh, scale=-0.5),
                         reads=[("G", bank)], writes=["F1"])
                    P.op("act", lambda e: e.activation(out=g[:, 0:T], in_=sgn[:, 0:T], func=AF.Ln, scale=dcol(DV_S1 + hh),
                                                       bias=dcol(DV_B1 + hh)),
                         reads=["F1", "dv_s1", "dv_b1"], writes=["F2"])
                    P.op("dve", lambda e: e.tensor_tensor_scan(out=b[:, 0:T], data0=maskm, data1=g[:, 0:T], initial=0.0,
                                                               op0=ALU.mult, op1=ALU.add),
                         reads=["F2", "cstf"], writes=["F3"])
                    bv = b[:, 0:T].rearrange("p (c t) -> p c t", t=64)
                    P.op("dve", lambda e: e.tensor_tensor(out=dd[:, 0:T].rearrange("p (c t) -> p c t", t=64), in0=bv,
                                                          in1=bv[:, :, 31:32].to_broadcast([128, 8, 64]), op=ALU.subtract),
                         reads=["F3"], writes=["F4"])
                    P.op("act", lambda e: e.activation(out=E1[:, 0:T], in_=dd[:, 0:T], func=AF.Exp), reads=["F4"], writes=["F5"])
                    P.op("act", lambda e: e.activation(out=E2[:, 0:T], in_=dd[:, 0:T], func=AF.Exp, scale=-1.0,
                                                       bias=dcol(DV_LNH + hh)),
                         reads=["F4", "dv_lnh"], writes=["F6"])
                    P.op("act", lambda e: e.activation(out=eb[:, 0:T], in_=b[:, 0:T], func=AF.Exp), reads=["F3"], writes=["F7"])
                    P.op("dve", lambda e: e.tensor_tensor(out=Bt[0][:], in0=Ft[0][:, 0:T], in1=E1[:, 0:T], op=ALU.mult),
                         reads=["F0", "F5"], writes=["B0"])
                    P.op("dve", lambda e: e.scalar_tensor_tensor(out=Bt[1][:], in0=sgn[:, 0:T], scalar=1.0, in1=E2[:, 0:T],
                                                                 op0=ALU.add, op1=ALU.mult),
                         reads=["F1", "F6"], writes=["B1"])

                def post_i(bank, hh=hh):
                    P.op("act", lambda e: e.activation(out=Bt[2][:], in_=G[bank][:], func=AF.Copy),
                         reads=[("G", bank)], writes=["B2"])

                def pre_g(hh=hh):
                    ktv = Bt[3][:].rearrange("p (j k) -> p j k", k=128)
                    ktv2 = Bt[6][:].rearrange("p (j k) -> p j k", k=128)
                    for j in range(4):
                        P.op("pe", lambda e, j=j: e.transpose(out=pTR[:, 0, j, :], in_=Bt[1][:, j * 128:(j + 1) * 128], identity=ident),
                             reads=["B1", "cstb"], writes=["pTR"], signal=(j == 3))
                    P.op("act", lambda e: e.activation(out=ktv[0:64], in_=pTR[0:64, 0, :, :], func=AF.Copy), reads=["pTR"], writes=["B3"])
                    P.op("act", lambda e: e.activation(out=ktv2[64:128], in_=pTR[64:128, 0, :, :], func=AF.Copy), reads=["pTR"], writes=["B6"])
                    for j in range(4):
                        P.op("pe", lambda e, j=j: e.matmul(pST[:, j * 128:(j + 1) * 128], lhsT=Bt[1][:, j * 128:(j + 1) * 128],
                                                           rhs=Bt[0][:, j * 128:(j + 1) * 128], start=True, stop=True),
                             reads=["B1", "B0"], writes=["pST"], signal=(j == 3))
                    P.op("dve", lambda e: e.tensor_tensor(out=Bt[5][:], in0=pST[:], in1=mask4, op=ALU.mult),
                         reads=["pST", "cstf"], writes=["B5"])

                def post_g(bank, hh=hh):
                    gsb = hh % 2
                    E1, eb = Ft[5], Ft[7]
                    ktok, vtok, sTm = Bt[3], Bt[4], Bt[5]
                    ktv = ktok[:].rearrange("p (j k) -> p j k", k=128)
                    vtv = vtok[:].rearrange("p (j k) -> p j k", k=128)
                    ktv2 = Bt[6][:].rearrange("p (j k) -> p j k", k=128)
                    for j in range(4):
                        P.op("pe", lambda e, j=j: e.transpose(out=pTR[:, 1, j, :], in_=Bt[2][:, j * 128:(j + 1) * 128], identity=ident),
                             reads=["B2", "cstb"], writes=["pTR"], signal=(j == 3))
                    P.op("dve", lambda e: e.tensor_copy(out=vtv, in_=pTR[:, 1, :, :]), reads=["pTR"], writes=["B4"])
                    pbank = [pP[:].rearrange("p a b -> p (a b)"), pOT[:]]
                    pres = ["pP", "pOT"]
                    for c in range(8):
                        j, half = c // 2, c % 2
                        sl = c % 4
                        P.op("pe", lambda e, j=j, half=half, sl=sl, c=c: e.matmul(pbank[c // 4][:, sl * 128:(sl + 1) * 128],
                                                                                  lhsT=(ktv if half == 0 else ktv2)[:, j, :],
                                                                                  rhs=vtv[:, j, :], start=True, stop=True),
                             reads=["B3", "B6", "B4"], writes=[pres[c // 4]], signal=(sl == 3))
                    for c in range(8):
                        sl = c % 4
                        P.op("act", lambda e, c=c, sl=sl: e.activation(out=PT[sl][:], in_=pbank[c // 4][:, sl * 128:(sl + 1) * 128],
                                                                       func=AF.Copy, scale=E1[:, c * 64 + 63:c * 64 + 64]),
                             reads=[pres[c // 4], "F5"], writes=[("PT", sl)])
                        P.op("dve", lambda e, c=c: e.tensor_scalar_mul(out=smid[:, c, :], in0=Sst[:, hh, :],
                                                                       scalar1=eb[:, c * 64 + 31:c * 64 + 32]),
                             reads=[("S", hh), "F7"], writes=[("smid", c)])
                        P.op("dve", lambda e, c=c, sl=sl: e.scalar_tensor_tensor(out=Sst[:, hh, :], in0=Sst[:, hh, :],
                                                                                 scalar=eb[:, c * 64 + 63:c * 64 + 64],
                                                                                 in1=PT[sl][:], op0=ALU.mult, op1=ALU.add),
                             reads=[("S", hh), "F7", ("PT", sl)], writes=[("S", hh)])
                    P.op("act", lambda e: e.activation(out=GS[gsb][:], in_=G[bank][:], func=AF.Silu),
                         reads=[("G", bank)], writes=[("GS", gsb)])

                    def stage2a():
                        for j in range(4):
                            for half in range(2):
                                c = 2 * j + half
                                P.op("pe", lambda e, c=c, half=half: e.matmul(pOT[:, c * 64:(c + 1) * 64], lhsT=smid[:, c, :],
                                                                              rhs=Bt[0][:, c * 64:(c + 1) * 64],
                                                                              start=(half == 0), stop=False, skip_group_check=True),
                                     reads=[("smid", c), "B0"], writes=["pOT"], signal=False)
                            P.op("pe", lambda e, j=j: e.matmul(pOT[:, j * 128:(j + 1) * 128], lhsT=vtv[:, j, :],
                                                               rhs=sTm[:, j * 128:(j + 1) * 128], start=False, stop=True,
                                                               skip_group_check=True),
                                 reads=["B4", "B5"], writes=["pOT"], signal=(j == 3))
                        s_ = sqc[0] % 2
                        sqc[0] += 1
                        P.op("act", lambda e: e.activation(out=SQ[s_][:], in_=pOT[:], func=AF.Square), reads=["pOT"], writes=[("SQ", s_)])

                        def stage2b():
                            tb = 1 - gsb
                            P.op("pe", lambda e: e.matmul(pST[:], lhsT=ones, rhs=SQ[s_][:], start=True, stop=True),
                                 reads=[("SQ", s_), "cstb"], writes=["pST"])
                            P.op("act", lambda e: e.activation(out=GS[tb][:], in_=pST[:], func=AF.Ln, scale=1.0 / 128.0, bias=EPS),
                                 reads=["pST"], writes=[("GS", tb)])
                            P.op("act", lambda e: e.activation(out=GS[tb][:], in_=GS[tb][:], func=AF.Exp, scale=-0.5),
                                 reads=[("GS", tb)], writes=[("GS", tb)])
                            P.op("dve", lambda e: e.tensor_tensor(out=Ft[0][:, 0:T], in0=pOT[:], in1=GS[tb][:], op=ALU.mult),
                                 reads=["pOT", ("GS", tb)], writes=["F0"])
                            P.op("dve", lambda e: e.scalar_tensor_tensor(out=mix[:, hh, :], in0=Ft[0][:, 0:T], scalar=vcol(V_HGW + hh),
                                                                         in1=GS[gsb][:], op0=ALU.mult, op1=ALU.mult),
                                 reads=["F0", ("GS", gsb), "vecs"], writes=[("mix", hh)])
                        pipe.defer(1, stage2b)
                    pipe.defer(2, stage2a)

                for off, post in ((0, post_q), (16, post_f), (32, post_i), (48, post_g)):
                    blocks.append(dict(src=win_d[off + hh].rearrange("p (k m) -> p k m", m=128), nk=KC, rhs=rhs_hn, post=post,
                                       pre=(pre_g if off == 48 else None), nosig=(off != 16)))

            if small:
                continue
            wsq = {}

            def wout_ssq(mm):
                s_ = wsq[mm]
                P.op("pe", lambda e: e.matmul(pST[:], lhsT=ones, rhs=SQ[s_][:], start=(mm == 0), stop=(mm == KC - 1)),
                     reads=[("SQ", s_), "cstb"], writes=["pST"])

            for m in range(KC):
                def post_o(bank, m=m):
                    if m > 0:
                        wout_ssq(m - 1)
                    P.op("dve", lambda e: e.tensor_tensor(out=h[:, m, :], in0=h[:, m, :], in1=G[bank][:], op=ALU.add),
                         reads=[("h", m), ("G", bank)], writes=[("h", m)])
                    P.op("act", lambda e: e.activation(out=hn[:, m, :], in_=h[:, m, :], func=AF.Copy, scale=vcol(V_LN2 + m)),
                         reads=[("h", m), "vecs"], writes=[("hn", m)])
                    s_ = sqc[0] % 2
                    sqc[0] += 1
                    wsq[m] = s_
                    P.op("act", lambda e: e.activation(out=SQ[s_][:], in_=h[:, m, :], func=AF.Square),
                         reads=[("h", m)], writes=[("SQ", s_)])
                    if m == KC - 1:
                        wout_ssq(m)
                        P.op("act", lambda e: e.activation(out=rstd2[:], in_=pST[:], func=AF.Ln, scale=1.0 / D, bias=EPS),
                             reads=["pST"], writes=["rstd2"])
                        P.op("act", lambda e: e.activation(out=rstd2[:], in_=rstd2[:], func=AF.Exp, scale=-0.5),
                             reads=["rstd2"], writes=["rstd2"])
                blocks.append(dict(src=wout_d[m].rearrange("p (k m) -> p k m", m=128), nk=KC, rhs=rhs_mix,
                                   pre=(pipe.drain if m == 0 else None), post=post_o, nosig=(m > 0), ring4=(m >= 3)))

            gsz = [15, 15, 14, 14, 14, 14]
            gst = [sum(gsz[:i]) for i in range(len(gsz))]
            ngrp = len(gsz)
            j2g = {}
            for gi in range(ngrp):
                for jl_ in range(gsz[gi]):
                    j2g[gst[gi] + jl_] = (gi, jl_)

            def ffn_pre():
                pass

            def mk_up(j, which, t=t):
                g, jl = j2g[j]
                ab = (g % 2) * GRP + jl
                fs = (j % 2) * 4
                ci = j + which * NFF

                def post(bank):
                    xs = Ft[fs + which]
                    y = Ft[fs + 2 + which]
                    P.op("dve", lambda e: e.tensor_copy(out=xs[:, 0:2], in_=fc[:, ci, :]), reads=["fc"], writes=[f"F{fs + which}"])
                    P.op("dve", lambda e: e.tensor_tensor(out=xs[:, 2:2 + T], in0=G[bank][:], in1=rstd2[:], op=ALU.mult),
                         reads=[("G", bank), "rstd2"], writes=[f"F{fs + which}"])
                    P.op("dve", lambda e: e.tensor_copy(out=fc[:, ci, :], in_=xs[:, T:T + 2]), reads=[f"F{fs + which}"], writes=["fc"])
                    P.op("dve", lambda e: e.tensor_scalar(out=y[:, 0:T], in0=xs[:, 2:2 + T], scalar1=vcol(V_FCW + 2 * 172 + ci),
                                                          scalar2=vcol(V_FCB + ci), op0=ALU.mult, op1=ALU.add),
                         reads=[f"F{fs + which}", "vecs"], writes=[f"F{fs + 2 + which}"])
                    for jj in (1, 0):
                        P.op("dve", lambda e, jj=jj: e.scalar_tensor_tensor(out=y[:, 0:T], in0=xs[:, jj:jj + T],
                                                                            scalar=vcol(V_FCW + jj * 172 + ci), in1=y[:, 0:T],
                                                                            op0=ALU.mult, op1=ALU.add),
                             reads=[f"F{fs + which}", f"F{fs + 2 + which}", "vecs"], writes=[f"F{fs + 2 + which}"])
                    if which == 0:
                        P.op("act", lambda e: e.activation(out=y[:, 0:T], in_=y[:, 0:T], func=AF.Silu),
                             reads=[f"F{fs + 2}"], writes=[f"F{fs + 2}"])
                    else:
                        P.op("dve", lambda e: e.tensor_tensor(out=mix[:, ab, :], in0=Ft[fs + 2][:, 0:T], in1=y[:, 0:T], op=ALU.mult),
                             reads=[f"F{fs + 2}", f"F{fs + 3}"], writes=[("mix", ab)])
                return dict(src=(wupg_d if which == 0 else wupv_d)[j].rearrange("p (k m) -> p k m", m=128), nk=KC, rhs=rhs_hn, post=post,
                            nosig=(which == 0), ring4=True)

            fsq = {}

            def fin_ssq(mm):
                s_ = fsq[mm]
                P.op("pe", lambda e: e.matmul(pST[:], lhsT=ones, rhs=SQ[s_][:], start=(mm == 0), stop=(mm == KC - 1)),
                     reads=[("SQ", s_), "cstb"], writes=["pST"])

            def mk_dn2(g, m):
                k0 = gst[g]
                nk = gsz[g]
                base = (g % 2) * GRP

                def rhs(k):
                    return mix[:, base + k, :], ("mix", base + k)

                def mkpost(mm):
                    def post(bank):
                        last = (g == ngrp - 1)
                        if last and mm > 0:
                            fin_ssq(mm - 1)
                        P.op("dve", lambda e: e.tensor_tensor(out=h[:, mm, :], in0=h[:, mm, :], in1=G[bank][:], op=ALU.add),
                             reads=[("h", mm), ("G", bank)], writes=[("h", mm)])
                        if last:
                            s_ = sqc[0] % 2
                            sqc[0] += 1
                            fsq[mm] = s_
                            P.op("act", lambda e: e.activation(out=SQ[s_][:], in_=h[:, mm, :], func=AF.Square),
                                 reads=[("h", mm)], writes=[("SQ", s_)])
                    return post

                def dma(e, wbuf):
                    return e.dma_start(out=wbuf[:, 0:2 * nk, :].rearrange("p (a k) m -> p a k m", a=2),
                                       in_=wdn_d[m:m + 2][:, :, k0 * 128:(k0 + nk) * 128].rearrange("a p (k m) -> p a k m", m=128))
                return dict(dma=dma, ring4=True, subs=[dict(k0=0, nk=nk, rhs=rhs, post=mkpost(m), nosig=True),
                                           dict(k0=nk, nk=nk, rhs=rhs, post=mkpost(m + 1))])

            first = True
            for g in range(ngrp):
                js = list(range(gst[g], gst[g] + gsz[g]))
                for idx, j in enumerate(js):
                    bq = mk_up(j, 0)
                    if first:
                        bq["pre"] = ffn_pre
                        first = False
                    blocks.append(bq)
                    blocks.append(mk_up(j, 1))
                    if idx == 1 and g > 0:
                        for m in range(0, KC, 2):
                            blocks.append(mk_dn2(g - 1, m))
            for m in range(0, KC, 2):
                blocks.append(mk_dn2(ngrp - 1, m))

            def tile_end(t0=t0, fin_ssq=fin_ssq, t=t):
                fin_ssq(KC - 1)
                if t + 1 < ntiles:
                    pe_filler(48)
                P.op("act", lambda e: e.activation(out=Ft[2][:, 0:T], in_=pST[:], func=AF.Ln, scale=1.0 / D, bias=EPS),
                     reads=["pST"], writes=["F2"])
                P.op("act", lambda e: e.activation(out=Ft[4][:, 0:T], in_=Ft[2][:, 0:T], func=AF.Exp, scale=-0.5),
                     reads=["F2"], writes=["F4"])
                for c in range(KC):
                    P.op("dve", lambda e, c=c: e.scalar_tensor_tensor(out=h[:, c, :], in0=h[:, c, :], scalar=vcol(V_FIN + c),
                                                                     in1=Ft[4][:, 0:T], op0=ALU.mult, op1=ALU.mult),
                         reads=[("h", c), "F4", "vecs"], writes=[("h", c)])
                    if c % 4 == 3:
                        g8 = c // 4
                        P.op("sp", lambda e, g8=g8: e.dma_start(out=out_d[:, 4 * g8:4 * g8 + 4, t0:t0 + T], in_=h[:, 4 * g8:4 * g8 + 4, :]),
                             reads=[("h", cc) for cc in range(4 * g8, 4 * g8 + 4)], dma_sem=f"D_o{g8}")
            blocks[-1]["post_tile"] = tile_end

        for b in blocks:
            if "post_tile" in b:
                sub = b["subs"][-1]
                p0, p1 = sub["post"], b["post_tile"]
                sub["post"] = (lambda bank, p0=p0, p1=p1: (p0(bank), p1()))

        if maxblocks is not None:
            del blocks[maxblocks:]
        run_blocks()
        pipe.drain()
        P.final_waits("sp")
        sems = {n: st.enter_context(nc.semaphore(n)) for n in sorted(P.cnt.keys())}
        P.emit(nc, sems)
    return nc


def _colmajor(v, n):
    return np.ascontiguousarray(np.asarray(v, dtype=np.float32).reshape(n, 128).T)


def _wblocks(w, nblk):
    K = w.shape[0]
    kc = K // 128
    return np.ascontiguousarray(w.reshape(kc, 128, nblk, 128).transpose(2, 1, 0, 3)).reshape(nblk, 128, kc * 128)


def _prep_shared(inp):
    vec = np.zeros((128, NV), np.float32)
    vec[:, V_LN1:V_LN1 + 32] = _colmajor(inp["ln1_w"][0], 32)
    vec[:, V_LN2:V_LN2 + 32] = _colmajor(inp["ln2_w"][0], 32)
    vec[:, V_FIN:V_FIN + 32] = _colmajor(inp["final_norm_w"], 32)
    vec[:, V_G0:V_G0 + 16] = _colmajor(inp["lb_gamma"][0], 16)
    vec[:, V_G1:V_G1 + 16] = _colmajor(inp["lb_gamma"][1], 16)
    vec[:, V_HGW:V_HGW + 16] = _colmajor(inp["hg_norm_w"][0], 16)
    for j in range(4):
        vec[:, V_CW + j * 16:V_CW + (j + 1) * 16] = _colmajor(inp["lru_conv_w"][0, j], 16)
    vec[:, V_CB:V_CB + 16] = _colmajor(inp["lru_conv_b"][0], 16)
    vec[:, V_BA:V_BA + 16] = _colmajor(inp["lru_ba"][0], 16)
    vec[:, V_BX:V_BX + 16] = _colmajor(inp["lru_bx"][0], 16)
    vec[:, V_LAM:V_LAM + 16] = _colmajor(inp["lru_lambda"][0], 16)
    vec[:, V_LRUW:V_LRUW + 16] = _colmajor(inp["lru_norm_w"][0], 16)
    for j in range(3):
        vec[:, V_FCW + j * 172:V_FCW + (j + 1) * 172] = _colmajor(inp["ffn_conv_w"][0, j], 172)
    vec[:, V_FCB:V_FCB + 172] = _colmajor(inp["ffn_conv_b"][0], 172)

    cstb = np.zeros((128, 256), np.float32)
    cstb[:, 0:128] = np.eye(128, dtype=np.float32)
    cstb[:, 128:256] = 1.0
    cstb = cstb.astype(ml_dtypes.bfloat16)
    cstf = np.ones((128, 1024), np.float32)
    cstf[:, 0:512:64] = 0.0
    s = np.arange(128)[:, None]
    tt = np.arange(128)[None, :]
    pm = ((s // 64 == tt // 64) & (s <= tt)).astype(np.float32)
    cstf[:, 512:1024] = np.tile(pm, (1, 4))

    wa = np.asarray(inp["lru_wa"][0], np.float32)
    wx = np.asarray(inp["lru_wx"][0], np.float32)
    wlru = np.ascontiguousarray(np.stack([wa, wx], axis=2)).reshape(NL, 128, 256)
    shared = {
        "w_in": _wblocks(np.asarray(inp["w_in"][0], np.float32), 96),
        "w_out": _wblocks(np.asarray(inp["w_out"][0], np.float32), 32),
        "w_upg": _wblocks(np.asarray(inp["ffn_w_up"][0][:, :DFF], np.float32), NFF),
        "w_upv": _wblocks(np.asarray(inp["ffn_w_up"][0][:, DFF:], np.float32), NFF),
        "w_dn": _wblocks(np.asarray(inp["ffn_w_down"][0], np.float32), 32),
        "w_lru": wlru,
        "vecs": vec,
        "cstb": cstb,
        "cstf": cstf,
    }
    return shared


def kernel(**inputs):
    x = np.asarray(inputs["x"], np.float32)
    B = x.shape[0]
    shared = _prep_shared(inputs)
    nc = build_program()
    in_maps = []
    for b in range(B):
        xf = np.ascontiguousarray(x[b].T.reshape(KC, 128, SEQ).transpose(1, 0, 2))
        m = dict(shared)
        m["x"] = xf
        in_maps.append(m)
    res = run_bass_kernel_spmd(nc, in_maps, core_ids=list(range(B)))
    outs = []
    for b in range(B):
        o = np.asarray(res.results[b]["out"], np.float32)
        outs.append(np.ascontiguousarray(o.transpose(1, 0, 2).reshape(D, SEQ).T))
    return np.stack(outs, axis=0)
```

```python
import numpy as np
import ml_dtypes
from contextlib import ExitStack
import concourse.bass as bass
import concourse.mybir as mybir
from concourse.bass_utils import run_bass_kernel_spmd

F32 = mybir.dt.float32
BF16 = mybir.dt.bfloat16
AF = mybir.ActivationFunctionType
ALU = mybir.AluOpType

D = 4096
SEQ = 2048
T = 512
NT = SEQ // T
KC = D // 128
NH = 16
NL = 16
DFF = 11008
NFF = DFF // 128
GRP = 16
EPS = 1e-6
NB = 3
NG = 4

V_LN1, V_LN2, V_FIN, V_G0, V_G1, V_HGW = 0, 32, 64, 96, 112, 128
V_CW, V_CB, V_BA, V_BX, V_LAM, V_LRUW = 144, 208, 224, 240, 256, 272
V_FCW, V_FCB, NV = 288, 804, 976
DV_LB, DV_NOML, DV_LNOML, DV_C8, DV_C16, DV_SCR, DV_BAH, DV_BXH, DV_C8H, DV_S1, DV_B1, DV_LNH, NDV = 0, 16, 32, 48, 64, 80, 96, 112, 128, 144, 160, 176, 192

ENGS = ["pe", "act", "dve", "pool", "sp"]


class Prog:
    def __init__(self):
        self.ops = {e: [] for e in ENGS}
        self.cnt = {}
        self.seen = {e: {} for e in ENGS}
        self.lastw = {}
        self.readers = {}

    def op(self, eng, fn, reads=(), writes=(), signal=True, dma_sem=None, pre_r=(), pre_w=()):
        own = "S_" + eng
        deps = []
        for r in list(reads) + list(pre_r):
            if r in self.lastw:
                deps.append(self.lastw[r])
        for w in list(writes) + list(pre_w):
            if w in self.lastw:
                deps.append(self.lastw[w])
            deps += self.readers.get(w, [])
        waits = {}
        for (s, v) in deps:
            if eng == "pe" and s == own:
                continue
            if self.seen[eng].get(s, 0) < v:
                waits[s] = max(waits.get(s, 0), v)
        for s, v in waits.items():
            self.seen[eng][s] = v
        if dma_sem is not None:
            self.cnt[dma_sem] = self.cnt.get(dma_sem, 0) + 16
            tag = (dma_sem, self.cnt[dma_sem])
            inc = (dma_sem, 16)
        elif signal:
            self.cnt[own] = self.cnt.get(own, 0) + 1
            tag = (own, self.cnt[own])
            inc = (own, 1)
        else:
            tag = (own, self.cnt.get(own, 0) + 1)
            inc = None
        for r in reads:
            self.readers.setdefault(r, []).append(tag)
        for w in writes:
            self.lastw[w] = tag
            self.readers[w] = []
        self.ops[eng].append((waits, fn, inc))

    def final_waits(self, eng):
        waits = {}
        for s, v in self.cnt.items():
            if self.seen[eng].get(s, 0) < v:
                waits[s] = v
        self.ops[eng].append((waits, None, None))

    def emit(self, nc, sems):
        with nc.Block() as block:
            def run(engname, e):
                for (waits, fn, inc) in self.ops[engname]:
                    for s, v in waits.items():
                        e.wait_ge(sems[s], v)
                    if fn is None:
                        continue
                    ins = fn(e)
                    if inc is not None:
                        ins.then_inc(sems[inc[0]], inc[1])

            @block.tensor
            def _(e):
                run("pe", e)

            @block.scalar
            def _(e):
                run("act", e)

            @block.vector
            def _(e):
                run("dve", e)

            @block.gpsimd
            def _(e):
                run("pool", e)

            @block.sync
            def _(e):
                run("sp", e)


class Pipe:
    def __init__(self):
        self.blk = 0
        self.pending = []

    def defer(self, delay, fn):
        self.pending.append((self.blk + delay, fn))

    def flush(self):
        while True:
            due = [p for p in self.pending if p[0] <= self.blk]
            if not due:
                return
            p = due[0]
            self.pending.remove(p)
            p[1]()

    def tick(self):
        self.blk += 1
        self.flush()

    def drain(self):
        while self.pending:
            self.blk = max(self.blk, min(p[0] for p in self.pending))
            self.flush()


def build_program(ntiles=NT, maxblocks=None, small=False):
    nc = bass.Bass("TRN2", target_bir_lowering=False)
    x_d = nc.dram_tensor("x", [128, KC, SEQ], F32, kind="ExternalInput").ap()
    out_d = nc.dram_tensor("out", [128, KC, SEQ], F32, kind="ExternalOutput").ap()
    win_d = nc.dram_tensor("w_in", [96, 128, KC * 128], F32, kind="ExternalInput").ap()
    if not small:
        wout_d = nc.dram_tensor("w_out", [32, 128, KC * 128], F32, kind="ExternalInput").ap()
        wupg_d = nc.dram_tensor("w_upg", [NFF, 128, KC * 128], F32, kind="ExternalInput").ap()
        wupv_d = nc.dram_tensor("w_upv", [NFF, 128, KC * 128], F32, kind="ExternalInput").ap()
        wdn_d = nc.dram_tensor("w_dn", [32, 128, NFF * 128], F32, kind="ExternalInput").ap()
    wlru_d = nc.dram_tensor("w_lru", [NL, 128, 256], F32, kind="ExternalInput").ap()
    vecs_d = nc.dram_tensor("vecs", [128, NV], F32, kind="ExternalInput").ap()
    cstb_d = nc.dram_tensor("cstb", [128, 256], BF16, kind="ExternalInput").ap()
    cstf_d = nc.dram_tensor("cstf", [128, 1024], F32, kind="ExternalInput").ap()

    P = Prog()
    pipe = Pipe()
    with ExitStack() as st:
        def sb(name, shape, dt):
            return st.enter_context(nc.sbuf_tensor(name, shape, dt))

        def ps(name, shape, dt):
            return st.enter_context(nc.psum_tensor(name, shape, dt))

        h = sb("h", [128, KC, T], F32)
        hn = sb("hn", [128, KC, T], BF16)
        mix = sb("mix", [128, KC, T], BF16)
        wb = [sb(f"wb{i}", [128, KC, 128], BF16) for i in range(NB)]
        lw = [sb(f"lw{i}", [128, 2, 128], BF16) for i in range(2)]
        vecs = sb("vecs_sb", [128, NV], F32)
        dv = sb("dv", [128, NDV], F32)
        cstb = sb("cstb_sb", [128, 256], BF16)
        cstf = sb("cstf_sb", [128, 1024], F32)
        Sst = sb("Sst", [128, NH, 128], F32)
        lstate = sb("lstate", [128, NL], F32)
        lc = sb("lc", [128, NL, 3], F32)
        fc = sb("fc", [128, 2 * NFF, 2], F32)
        Ft = [sb(f"F{i}", [128, w_], F32) for i, w_ in enumerate([516, 514, 512, 512, 514, 514, 512, 512])]
        GS = [sb(f"GS{i}", [128, T], F32) for i in range(2)]
        wb3t = sb("wb3", [128, KC, 128], BF16)
        wb.append(wb3t)
        Bt = [wb3t[:, 4 * i:4 * i + 4, :].rearrange("p a b -> p (a b)") for i in range(7)]
        BRES = [f"B{i}" for i in range(7)]
        smid = sb("smid", [128, 8, 128], BF16)
        SQ = [sb(f"SQ{i}", [128, T], BF16) for i in range(2)]
        PT = [sb(f"PT{i}", [128, 128], F32) for i in range(4)]
        rstd2 = sb("rstd2", [128, T], F32)

        G = [ps(f"G{i}", [128, T], F32) for i in range(NG)]
        pST = ps("pST", [128, T], F32)
        pOT = ps("pOT", [128, T], F32)
        pTR = ps("pTR", [128, 2, 4, 128], BF16)
        pP = ps("pP", [128, 4, 128], F32)

        ident = cstb[:, 0:128]
        ones = cstb[:, 128:256]
        maskm = cstf[:, 0:512]
        mask4 = cstf[:, 512:1024]

        def vcol(off):
            return vecs[:, off:off + 1]

        def dcol(off):
            return dv[:, off:off + 1]

        P.op("sp", lambda e: e.dma_start(out=vecs[:], in_=vecs_d), writes=["vecs"], dma_sem="D_m0")
        P.op("sp", lambda e: e.dma_start(out=cstb[:], in_=cstb_d), writes=["cstb"], dma_sem="D_m1")
        P.op("sp", lambda e: e.dma_start(out=cstf[:], in_=cstf_d), writes=["cstf"], dma_sem="D_m2")
        P.op("dve", lambda e: e.memset(Sst[:], 0.0), writes=[("S", i) for i in range(NH)])
        P.op("dve", lambda e: e.memset(lstate[:], 0.0), writes=["lstate"])
        P.op("dve", lambda e: e.memset(lc[:], 0.0), writes=["lc"])
        P.op("dve", lambda e: e.memset(fc[:], 0.0), writes=["fc"])
        P.op("dve", lambda e: e.memset(smid[:], 0.5), writes=[("smid", c) for c in range(8)])
        P.op("dve", lambda e: e.memset(Bt[3][:], 0.0), writes=["B3"])
        P.op("dve", lambda e: e.memset(Bt[6][:], 0.0), writes=["B6"])
        P.op("dve", lambda e: e.tensor_tensor(out=dv[:, DV_SCR:DV_SCR + 16], in0=vecs[:, V_G0:V_G0 + 16],
                                              in1=vecs[:, V_G1:V_G1 + 16], op=ALU.subtract),
             reads=["vecs"], writes=["dvs"])
        P.op("act", lambda e: e.activation(out=dv[:, DV_LB:DV_LB + 16], in_=dv[:, DV_SCR:DV_SCR + 16], func=AF.Sigmoid),
             reads=["dvs"], writes=["dv_lb"])
        P.op("dve", lambda e: e.tensor_scalar_add(out=dv[:, DV_NOML:DV_NOML + 16], in0=dv[:, DV_LB:DV_LB + 16], scalar1=-1.0),
             reads=["dv_lb"], writes=["dv_noml"])
        P.op("act", lambda e: e.activation(out=dv[:, DV_LNOML:DV_LNOML + 16], in_=dv[:, DV_LB:DV_LB + 16], func=AF.Ln,
                                           scale=-1.0, bias=1.0),
             reads=["dv_lb"], writes=["dv_lnoml"])
        P.op("act", lambda e: e.activation(out=dv[:, DV_SCR:DV_SCR + 16], in_=vecs[:, V_LAM:V_LAM + 16], func=AF.Exp, scale=-1.0),
             reads=["vecs", "dvs"], writes=["dvs"])
        P.op("act", lambda e: e.activation(out=dv[:, DV_SCR:DV_SCR + 16], in_=dv[:, DV_SCR:DV_SCR + 16], func=AF.Ln, bias=1.0),
             reads=["dvs"], writes=["dvs"])
        P.op("dve", lambda e: e.tensor_scalar_mul(out=dv[:, DV_C8:DV_C8 + 16], in0=dv[:, DV_SCR:DV_SCR + 16], scalar1=-8.0),
             reads=["dvs"], writes=["dv_c8"])
        P.op("dve", lambda e: e.tensor_scalar_mul(out=dv[:, DV_C16:DV_C16 + 16], in0=dv[:, DV_SCR:DV_SCR + 16], scalar1=-16.0),
             reads=["dvs"], writes=["dv_c16"])
        P.op("dve", lambda e: e.tensor_scalar_mul(out=dv[:, DV_C8H:DV_C8H + 16], in0=dv[:, DV_SCR:DV_SCR + 16], scalar1=-4.0),
             reads=["dvs"], writes=["dv_c8h"])
        P.op("dve", lambda e: e.tensor_scalar_mul(out=dv[:, DV_BAH:DV_BAH + 16], in0=vecs[:, V_BA:V_BA + 16], scalar1=0.5),
             reads=["vecs"], writes=["dv_bah"])
        P.op("dve", lambda e: e.tensor_scalar_mul(out=dv[:, DV_BXH:DV_BXH + 16], in0=vecs[:, V_BX:V_BX + 16], scalar1=0.5),
             reads=["vecs"], writes=["dv_bxh"])
        P.op("dve", lambda e: e.tensor_scalar_mul(out=dv[:, DV_S1:DV_S1 + 16], in0=dv[:, DV_NOML:DV_NOML + 16], scalar1=0.5),
             reads=["dv_noml"], writes=["dv_s1"])
        P.op("dve", lambda e: e.tensor_scalar_add(out=dv[:, DV_B1:DV_B1 + 16], in0=dv[:, DV_S1:DV_S1 + 16], scalar1=1.0),
             reads=["dv_s1"], writes=["dv_b1"])
        P.op("act", lambda e: e.activation(out=dv[:, DV_LNH:DV_LNH + 16], in_=dv[:, DV_LB:DV_LB + 16], func=AF.Ln,
                                           scale=-0.5, bias=0.5),
             reads=["dv_lb"], writes=["dv_lnh"])
        CONST = ["vecs", "cstb", "cstf", "dv_lb", "dv_noml", "dv_lnoml", "dv_c8", "dv_c16"]

        for j in range(7):
            P.op("sp", lambda e, j=j: e.dma_start(out=h[:, 0:16, :], in_=x_d[:, 0:16, 0:T], cond=(e.partition_id() > j)),
                 writes=[("h", c) for c in range(16)], dma_sem="D_stag")

        sqc = [0]

        def rmsnorm_to_hn(wcol):
            for c in range(KC):
                s = sqc[0] % 2
                sqc[0] += 1
                P.op("act", lambda e, c=c, s=s: e.activation(out=SQ[s][:], in_=h[:, c, :], func=AF.Square),
                     reads=[("h", c)], writes=[("SQ", s)])
                P.op("pe", lambda e, c=c, s=s: e.matmul(pST[:], lhsT=ones, rhs=SQ[s][:], start=(c == 0), stop=(c == KC - 1)),
                     reads=[("SQ", s), "cstb"], writes=["pST"], signal=True)
                pe_filler(2)
            P.op("act", lambda e: e.activation(out=Ft[2][:, 0:T], in_=pST[:], func=AF.Ln, scale=1.0 / D, bias=EPS),
                 reads=["pST"], writes=["F2"])
            P.op("act", lambda e: e.activation(out=Ft[4][:, 0:T], in_=Ft[2][:, 0:T], func=AF.Exp, scale=-0.5),
                 reads=["F2"], writes=["F4"])
            for c in range(KC):
                P.op("dve", lambda e, c=c: e.scalar_tensor_tensor(out=hn[:, c, :], in0=h[:, c, :], scalar=vcol(wcol + c),
                                                                 in1=Ft[4][:, 0:T], op0=ALU.mult, op1=ALU.mult),
                     reads=[("h", c), "F4", "vecs"], writes=[("hn", c)])

        blocks = []

        bufs = []

        def assign_bufs():
            last_used = {0: -1, 1: -2, 2: -3, 3: -4}
            for i, b in enumerate(blocks):
                allowed = [0, 1, 2, 3] if b.get("ring4") else [0, 1, 2]
                bb = min(allowed, key=lambda x: last_used[x])
                bufs.append(bb)
                last_used[bb] = i

        def wres(buf):
            return [("wb", buf)] + (BRES if buf == 3 else [])

        def issue_wdma(i):
            b = blocks[i]
            buf = bufs[i]
            if "dma" in b:
                P.op("pool", lambda e, b=b, buf=buf: b["dma"](e, wb[buf]), writes=wres(buf), dma_sem=f"D_w{buf}")
                return
            nk = b["nk"]
            P.op("pool", lambda e, b=b, buf=buf, nk=nk: e.dma_start(
                out=wb[buf][:, 0:nk, :], in_=b["src"]), writes=wres(buf), dma_sem=f"D_w{buf}")

        bankc = [0]

        def run_blocks():
            assign_bufs()
            nxt = [0]
            for i, b in enumerate(blocks):
                if b.get("pre"):
                    b["pre"]()
                while nxt[0] < len(blocks) and nxt[0] - (3 if blocks[nxt[0]].get("ring4") else 2) <= i:
                    issue_wdma(nxt[0])
                    nxt[0] += 1
                buf = bufs[i]
                subs = b.get("subs") or [dict(k0=0, nk=b["nk"], rhs=b["rhs"], post=b["post"], nosig=b.get("nosig", False))]
                for si, sub in enumerate(subs):
                    bank = bankc[0] % NG
                    bankc[0] += 1
                    nk, k0 = sub["nk"], sub["k0"]
                    for k in range(nk):
                        rap, rres = sub["rhs"](k)
                        pre_r, pre_w = [], []
                        if k == nk - 1:
                            pre_w.append(("G", bankc[0] % NG))
                            if si == len(subs) - 1 and i + 1 < len(blocks):
                                pre_r.append(("wb", bufs[i + 1]))
                        P.op("pe", lambda e, buf=buf, bank=bank, k=k, rap=rap, nk=nk, k0=k0: e.matmul(
                            G[bank][:], lhsT=wb[buf][:, k0 + k, :], rhs=rap, start=(k == 0), stop=(k == nk - 1)),
                            reads=wres(buf) + [rres], writes=[("G", bank)], signal=(k == nk - 1 and not sub.get("nosig")),
                            pre_r=pre_r, pre_w=pre_w)
                        if b.get("fill") and k < nk - 1:
                            pe_filler(1)
                    sub["post"](bank)
                pipe.tick()

        def pe_filler(n, bank_off=0):
            bank = (bankc[0] + bank_off) % NG
            for _ in range(n):
                P.op("pe", lambda e, bank=bank: e.matmul(G[bank][:], lhsT=ident, rhs=smid[:, 0:4, :].rearrange("p a b -> p (a b)"),
                                                         start=True, stop=True),
                     reads=["cstb"] + [("smid", c) for c in range(4)], writes=[("G", bank)], signal=False)

        def rhs_hn(k):
            return hn[:, k, :], ("hn", k)

        def rhs_mix(k):
            return mix[:, k, :], ("mix", k)

        for t in range(ntiles):
            t0 = t * T

            def tile_start(t=t, t0=t0):
                for g8 in range(8):
                    P.op("sp", lambda e, g8=g8: e.dma_start(out=h[:, 4 * g8:4 * g8 + 4, :], in_=x_d[:, 4 * g8:4 * g8 + 4, t0:t0 + T]),
                         writes=[("h", c) for c in range(4 * g8, 4 * g8 + 4)], dma_sem=f"D_h{g8}")
                rmsnorm_to_hn(V_LN1)
                P.op("dve", lambda e: e.memset(Bt[3][64:128, :], 0.0), writes=["B3"])
                P.op("dve", lambda e: e.memset(Bt[6][0:64, :], 0.0), writes=["B6"])

            lru_sq = {}

            def mk_post_x(n, t=t):
                def post_x(bank):
                    l = n % 2
                    par = n % 2
                    P.op("pool", lambda e: e.dma_start(out=lw[l][:], in_=wlru_d[n].rearrange("p (a m) -> p a m", m=128)),
                         writes=[("lw", l)], dma_sem=f"D_lw{l}")
                    xs = Ft[0]
                    xb = GS[par]
                    P.op("dve", lambda e: e.tensor_copy(out=xs[:, 0:3], in_=lc[:, n, :]), reads=["lc"], writes=["F0"])
                    P.op("act", lambda e: e.activation(out=xs[:, 3:3 + T], in_=G[bank][:], func=AF.Copy),
                         reads=[("G", bank)], writes=["F0"])
                    P.op("dve", lambda e: e.tensor_copy(out=lc[:, n, :], in_=xs[:, T:T + 3]), reads=["F0"], writes=["lc"])
                    P.op("dve", lambda e: e.tensor_scalar(out=xb[:], in0=xs[:, 3:3 + T], scalar1=vcol(V_CW + 3 * 16 + n),
                                                          scalar2=vcol(V_CB + n), op0=ALU.mult, op1=ALU.add),
                         reads=["F0", "vecs"], writes=[("GS", par)])
                    for j in (2, 1, 0):
                        P.op("dve", lambda e, j=j: e.scalar_tensor_tensor(out=xb[:], in0=xs[:, j:j + T],
                                                                          scalar=vcol(V_CW + j * 16 + n), in1=xb[:],
                                                                          op0=ALU.mult, op1=ALU.add),
                             reads=["F0", ("GS", par), "vecs"], writes=[("GS", par)])
                    if n == 0:
                        P.op("act", lambda e: e.activation(out=Bt[par][:], in_=xb[:], func=AF.Copy),
                             reads=[("GS", par)], writes=[f"B{par}"])
                return post_x

            def mk_post_y(n, t=t):
                def post_y(bank):
                    l = n % 2
                    par = n % 2
                    P.op("act", lambda e: e.activation(out=Ft[7][:, 0:T], in_=G[bank][:], func=AF.Gelu_apprx_tanh),
                         reads=[("G", bank)], writes=["F7"])

                    def ssq_mm(nn):
                        s_ = lru_sq[nn]
                        P.op("pe", lambda e: e.matmul(pP[:].rearrange("p a b -> p (a b)"), lhsT=ones, rhs=SQ[s_][:],
                                                      start=(nn == 0), stop=(nn == NL - 1)),
                             reads=[("SQ", s_), "cstb"], writes=["pP"])

                    def fin():
                        ssq_mm(NL - 1)
                        P.op("act", lambda e: e.activation(out=Ft[3][:, 0:T], in_=pP[:].rearrange("p a b -> p (a b)"),
                                                           func=AF.Ln, scale=1.0 / 2048.0, bias=EPS),
                             reads=["pP"], writes=["F3"])
                        P.op("act", lambda e: e.activation(out=Ft[5][:, 0:T], in_=Ft[3][:, 0:T], func=AF.Exp, scale=-0.5),
                             reads=["F3"], writes=["F5"])
                        for nn in range(NL):
                            P.op("dve", lambda e, nn=nn: e.tensor_tensor(out=mix[:, NH + nn, :], in0=mix[:, NH + nn, :],
                                                                         in1=Ft[5][:, 0:T], op=ALU.mult),
                                 reads=[("mix", NH + nn), "F5"], writes=[("mix", NH + nn)])

                    def stage1():
                        if n > 0:
                            ssq_mm(n - 1)
                        P.op("pe", lambda e: e.matmul(pST[:], lhsT=lw[l][:, 0, :], rhs=Bt[par][:], start=True, stop=True),
                             reads=[("lw", l), f"B{par}"], writes=["pST"])
                        P.op("pe", lambda e: e.matmul(pOT[:], lhsT=lw[l][:, 1, :], rhs=Bt[par][:], start=True, stop=True),
                             reads=[("lw", l), f"B{par}"], writes=["pOT"])
                        thr, thi, a, a2 = Ft[2], Ft[3], Ft[4], Ft[5]
                        P.op("act", lambda e: e.activation(out=thr[:, 0:T], in_=pST[:], func=AF.Tanh, scale=0.5, bias=dcol(DV_BAH + n)),
                             reads=["pST", "dv_bah"], writes=["F2"])
                        P.op("act", lambda e: e.activation(out=thi[:, 0:T], in_=pOT[:], func=AF.Tanh, scale=0.5, bias=dcol(DV_BXH + n)),
                             reads=["pOT", "dv_bxh"], writes=["F3"])
                        P.op("act", lambda e: e.activation(out=a[:, 0:T], in_=thr[:, 0:T], func=AF.Exp, scale=dcol(DV_C8H + n),
                                                           bias=dcol(DV_C8H + n)),
                             reads=["F2", "dv_c8h"], writes=["F4"])
                        P.op("act", lambda e: e.activation(out=a2[:, 0:T], in_=thr[:, 0:T], func=AF.Exp, scale=dcol(DV_C8 + n),
                                                           bias=dcol(DV_C8 + n)),
                             reads=["F2", "dv_c8"], writes=["F5"])
                        P.op("act", lambda e: e.activation(out=a2[:, 0:T], in_=a2[:, 0:T], func=AF.Sqrt, scale=-1.0, bias=1.0),
                             reads=["F5"], writes=["F5"])
                        if n + 1 < NL:
                            P.op("act", lambda e: e.activation(out=Bt[1 - par][:], in_=GS[1 - par][:], func=AF.Copy),
                                 reads=[("GS", 1 - par)], writes=[f"B{1 - par}"])
                        if t == 0:
                            P.op("dve", lambda e: e.memset(a2[:, 0:1], 1.0), reads=["F5"], writes=["F5"])
                        u = GS[par]
                        P.op("dve", lambda e: e.scalar_tensor_tensor(out=u[:], in0=thi[:, 0:T], scalar=1.0, in1=u[:],
                                                                     op0=ALU.add, op1=ALU.mult),
                             reads=[("GS", par), "F3"], writes=[("GS", par)])
                        P.op("dve", lambda e: e.tensor_tensor(out=u[:], in0=u[:], in1=a2[:, 0:T], op=ALU.mult),
                             reads=[("GS", par), "F5"], writes=[("GS", par)])
                        hst = Ft[6]
                        P.op("dve", lambda e: e.tensor_tensor_scan(out=hst[:, 0:T], data0=a[:, 0:T], data1=u[:],
                                                                   initial=lstate[:, n:n + 1], op0=ALU.mult, op1=ALU.add),
                             reads=["F4", ("GS", par), "lstate"], writes=["F6"])
                        P.op("dve", lambda e: e.tensor_copy(out=lstate[:, n:n + 1], in_=hst[:, T - 1:T]),
                             reads=["F6"], writes=["lstate"])
                        o = Ft[2]
                        P.op("dve", lambda e: e.scalar_tensor_tensor(out=o[:, 0:T], in0=hst[:, 0:T], scalar=0.5, in1=Ft[7][:, 0:T],
                                                                     op0=ALU.mult, op1=ALU.mult),
                             reads=["F6", "F7", "F2"], writes=["F2"])
                        s_ = sqc[0] % 2
                        sqc[0] += 1
                        lru_sq[n] = s_
                        P.op("act", lambda e: e.activation(out=SQ[s_][:], in_=o[:, 0:T], func=AF.Square),
                             reads=["F2"], writes=[("SQ", s_)])
                        P.op("act", lambda e: e.activation(out=mix[:, NH + n, :], in_=o[:, 0:T], func=AF.Copy,
                                                           scale=vcol(V_LRUW + n)),
                             reads=["F2", "vecs"], writes=[("mix", NH + n)])
                        if n == NL - 1:
                            pipe.defer(1, fin)
                    pipe.defer(1, stage1)
                return post_y

            def lru_blk(which, n):
                off = 64 if which == 0 else 80
                return dict(src=win_d[off + n].rearrange("p (k m) -> p k m", m=128), nk=KC, rhs=rhs_hn,
                            post=(mk_post_x(n) if which == 0 else mk_post_y(n)), nosig=(which == 1))

            b0 = lru_blk(0, 0)
            b0["fill"] = True
            b0["pre"] = tile_start
            blocks.append(b0)
            for n in range(NL):
                if n + 1 < NL:
                    blocks.append(lru_blk(0, n + 1))
                blocks.append(lru_blk(1, n))

            for hh in range(NH):
                def post_q(bank, hh=hh):
                    P.op("act", lambda e: e.activation(out=Ft[0][:, 0:T], in_=G[bank][:], func=AF.Silu),
                         reads=[("G", bank)], writes=["F0"])

                def post_f(bank, hh=hh):
                    sgn, g, b, dd, E1, E2, eb = Ft[1], Ft[2], Ft[3], Ft[4], Ft[5], Ft[6], Ft[7]
                    P.op("act", lambda e: e.activation(out=sgn[:, 0:T], in_=G[bank][:], func=AF.Tanh, scale=-0.5),
                         reads=[("G", bank)], writes=["F1"])
                    P.op("act", lambda e: e.activation(out=g[:, 0:T], in_=sgn[:, 0:T], func=AF.Ln, scale=dcol(DV_S1 + hh),
                                                       bias=dcol(DV_B1 + hh)),
                         reads=["F1", "dv_s1", "dv_b1"], writes=["F2"])
                    P.op("dve", lambda e: e.tensor_tensor_scan(out=b[:, 0:T], data0=maskm, data1=g[:, 0:T], initial=0.0,
                                                               op0=ALU.mult, op1=ALU.add),
                         reads=["F2", "cstf"], writes=["F3"])
                    bv = b[:, 0:T].rearrange("p (c t) -> p c t", t=64)
                    P.op("dve", lambda e: e.tensor_tensor(out=dd[:, 0:T].rearrange("p (c t) -> p c t", t=64), in0=bv,
                                                          in1=bv[:, :, 31:32].to_broadcast([128, 8, 64]), op=ALU.subtract),
                         reads=["F3"], writes=["F4"])
                    P.op("act", lambda e: e.activation(out=E1[:, 0:T], in_=dd[:, 0:T], func=AF.Exp), reads=["F4"], writes=["F5"])
                    P.op("act", lambda e: e.activation(out=E2[:, 0:T], in_=dd[:, 0:T], func=AF.Exp, scale=-1.0,
                                                       bias=dcol(DV_LNH + hh)),
                         reads=["F4", "dv_lnh"], writes=["F6"])
                    P.op("act", lambda e: e.activation(out=eb[:, 0:T], in_=b[:, 0:T], func=AF.Exp), reads=["F3"], writes=["F7"])
                    P.op("dve", lambda e: e.tensor_tensor(out=Bt[0][:], in0=Ft[0][:, 0:T], in1=E1[:, 0:T], op=ALU.mult),
                         reads=["F0", "F5"], writes=["B0"])
                    P.op("dve", lambda e: e.scalar_tensor_tensor(out=Bt[1][:], in0=sgn[:, 0:T], scalar=1.0, in1=E2[:, 0:T],
                                                                 op0=ALU.add, op1=ALU.mult),
                         reads=["F1", "F6"], writes=["B1"])

                def post_i(bank, hh=hh):
                    P.op("act", lambda e: e.activation(out=Bt[2][:], in_=G[bank][:], func=AF.Copy),
                         reads=[("G", bank)], writes=["B2"])

                def pre_g(hh=hh):
                    ktv = Bt[3][:].rearrange("p (j k) -> p j k", k=128)
                    ktv2 = Bt[6][:].rearrange("p (j k) -> p j k", k=128)
                    for j in range(4):
                        P.op("pe", lambda e, j=j: e.transpose(out=pTR[:, 0, j, :], in_=Bt[1][:, j * 128:(j + 1) * 128], identity=ident),
                             reads=["B1", "cstb"], writes=["pTR"], signal=(j == 3))
                    P.op("act", lambda e: e.activation(out=ktv[0:64], in_=pTR[0:64, 0, :, :], func=AF.Copy), reads=["pTR"], writes=["B3"])
                    P.op("act", lambda e: e.activation(out=ktv2[64:128], in_=pTR[64:128, 0, :, :], func=AF.Copy), reads=["pTR"], writes=["B6"])
                    for j in range(4):
                        P.op("pe", lambda e, j=j: e.matmul(pST[:, j * 128:(j + 1) * 128], lhsT=Bt[1][:, j * 128:(j + 1) * 128],
                                                           rhs=Bt[0][:, j * 128:(j + 1) * 128], start=True, stop=True),
                             reads=["B1", "B0"], writes=["pST"], signal=(j == 3))
                    P.op("dve", lambda e: e.tensor_tensor(out=Bt[5][:], in0=pST[:], in1=mask4, op=ALU.mult),
                         reads=["pST", "cstf"], writes=["B5"])

                def post_g(bank, hh=hh):
                    gsb = hh % 2
                    E1, eb = Ft[5], Ft[7]
                    ktok, vtok, sTm = Bt[3], Bt[4], Bt[5]
                    ktv = ktok[:].rearrange("p (j k) -> p j k", k=128)
                    vtv = vtok[:].rearrange("p (j k) -> p j k", k=128)
                    ktv2 = Bt[6][:].rearrange("p (j k) -> p j k", k=128)
                    for j in range(4):
                        P.op("pe", lambda e, j=j: e.transpose(out=pTR[:, 1, j, :], in_=Bt[2][:, j * 128:(j + 1) * 128], identity=ident),
                             reads=["B2", "cstb"], writes=["pTR"], signal=(j == 3))
                    P.op("dve", lambda e: e.tensor_copy(out=vtv, in_=pTR[:, 1, :, :]), reads=["pTR"], writes=["B4"])
                    pbank = [pP[:].rearrange("p a b -> p (a b)"), pOT[:]]
                    pres = ["pP", "pOT"]
                    for c in range(8):
                        j, half = c // 2, c % 2
                        sl = c % 4
                        P.op("pe", lambda e, j=j, half=half, sl=sl, c=c: e.matmul(pbank[c // 4][:, sl * 128:(sl + 1) * 128],
                                                                                  lhsT=(ktv if half == 0 else ktv2)[:, j, :],
                                                                                  rhs=vtv[:, j, :], start=True, stop=True),
                             reads=["B3", "B6", "B4"], writes=[pres[c // 4]], signal=(sl == 3))
                    for c in range(8):
                        sl = c % 4
                        P.op("act", lambda e, c=c, sl=sl: e.activation(out=PT[sl][:], in_=pbank[c // 4][:, sl * 128:(sl + 1) * 128],
                                                                       func=AF.Copy, scale=E1[:, c * 64 + 63:c * 64 + 64]),
                             reads=[pres[c // 4], "F5"], writes=[("PT", sl)])
                        P.op("dve", lambda e, c=c: e.tensor_scalar_mul(out=smid[:, c, :], in0=Sst[:, hh, :],
                                                                       scalar1=eb[:, c * 64 + 31:c * 64 + 32]),
                             reads=[("S", hh), "F7"], writes=[("smid", c)])
                        P.op("dve", lambda e, c=c, sl=sl: e.scalar_tensor_tensor(out=Sst[:, hh, :], in0=Sst[:, hh, :],
                                                                                 scalar=eb[:, c * 64 + 63:c * 64 + 64],
                                                                                 in1=PT[sl][:], op0=ALU.mult, op1=ALU.add),
                             reads=[("S", hh), "F7", ("PT", sl)], writes=[("S", hh)])
                    P.op("act", lambda e: e.activation(out=GS[gsb][:], in_=G[bank][:], func=AF.Silu),
                         reads=[("G", bank)], writes=[("GS", gsb)])

                    def stage2a():
                        for j in range(4):
                            for half in range(2):
                                c = 2 * j + half
                                P.op("pe", lambda e, c=c, half=half: e.matmul(pOT[:, c * 64:(c + 1) * 64], lhsT=smid[:, c, :],
                                                                              rhs=Bt[0][:, c * 64:(c + 1) * 64],
                                                                              start=(half == 0), stop=False, skip_group_check=True),
                                     reads=[("smid", c), "B0"], writes=["pOT"], signal=False)
                            P.op("pe", lambda e, j=j: e.matmul(pOT[:, j * 128:(j + 1) * 128], lhsT=vtv[:, j, :],
                                                               rhs=sTm[:, j * 128:(j + 1) * 128], start=False, stop=True,
                                                               skip_group_check=True),
                                 reads=["B4", "B5"], writes=["pOT"], signal=(j == 3))
                        s_ = sqc[0] % 2
                        sqc[0] += 1
                        P.op("act", lambda e: e.activation(out=SQ[s_][:], in_=pOT[:], func=AF.Square), reads=["pOT"], writes=[("SQ", s_)])

                        def stage2b():
                            tb = 1 - gsb
                            P.op("pe", lambda e: e.matmul(pST[:], lhsT=ones, rhs=SQ[s_][:], start=True, stop=True),
                                 reads=[("SQ", s_), "cstb"], writes=["pST"])
                            P.op("act", lambda e: e.activation(out=GS[tb][:], in_=pST[:], func=AF.Ln, scale=1.0 / 128.0, bias=EPS),
                                 reads=["pST"], writes=[("GS", tb)])
                            P.op("act", lambda e: e.activation(out=GS[tb][:], in_=GS[tb][:], func=AF.Exp, scale=-0.5),
                                 reads=[("GS", tb)], writes=[("GS", tb)])
                            P.op("dve", lambda e: e.tensor_tensor(out=Ft[0][:, 0:T], in0=pOT[:], in1=GS[tb][:], op=ALU.mult),
                                 reads=["pOT", ("GS", tb)], writes=["F0"])
                            P.op("dve", lambda e: e.scalar_tensor_tensor(out=mix[:, hh, :], in0=Ft[0][:, 0:T], scalar=vcol(V_HGW + hh),
                                                                         in1=GS[gsb][:], op0=ALU.mult, op1=ALU.mult),
                                 reads=["F0", ("GS", gsb), "vecs"], writes=[("mix", hh)])
                        pipe.defer(1, stage2b)
                    pipe.defer(2, stage2a)

                for off, post in ((0, post_q), (16, post_f), (32, post_i), (48, post_g)):
                    blocks.append(dict(src=win_d[off + hh].rearrange("p (k m) -> p k m", m=128), nk=KC, rhs=rhs_hn, post=post,
                                       pre=(pre_g if off == 48 else None), nosig=(off != 16)))

            if small:
                continue
            wsq = {}

            def wout_ssq(mm):
                s_ = wsq[mm]
                P.op("pe", lambda e: e.matmul(pST[:], lhsT=ones, rhs=SQ[s_][:], start=(mm == 0), stop=(mm == KC - 1)),
                     reads=[("SQ", s_), "cstb"], writes=["pST"])

            for m in range(KC):
                def post_o(bank, m=m):
                    if m > 0:
                        wout_ssq(m - 1)
                    P.op("dve", lambda e: e.tensor_tensor(out=h[:, m, :], in0=h[:, m, :], in1=G[bank][:], op=ALU.add),
                         reads=[("h", m), ("G", bank)], writes=[("h", m)])
                    P.op("act", lambda e: e.activation(out=hn[:, m, :], in_=h[:, m, :], func=AF.Copy, scale=vcol(V_LN2 + m)),
                         reads=[("h", m), "vecs"], writes=[("hn", m)])
                    s_ = sqc[0] % 2
                    sqc[0] += 1
                    wsq[m] = s_
                    P.op("act", lambda e: e.activation(out=SQ[s_][:], in_=h[:, m, :], func=AF.Square),
                         reads=[("h", m)], writes=[("SQ", s_)])
                    if m == KC - 1:
                        wout_ssq(m)
                        P.op("act", lambda e: e.activation(out=rstd2[:], in_=pST[:], func=AF.Ln, scale=1.0 / D, bias=EPS),
                             reads=["pST"], writes=["rstd2"])
                        P.op("act", lambda e: e.activation(out=rstd2[:], in_=rstd2[:], func=AF.Exp, scale=-0.5),
                             reads=["rstd2"], writes=["rstd2"])
                blocks.append(dict(src=wout_d[m].rearrange("p (k m) -> p k m", m=128), nk=KC, rhs=rhs_mix,
                                   pre=(pipe.drain if m == 0 else None), post=post_o, nosig=(m > 0), ring4=(m >= 3)))

            gsz = [15, 15, 14, 14, 14, 14]
            gst = [sum(gsz[:i]) for i in range(len(gsz))]
            ngrp = len(gsz)
            j2g = {}
            for gi in range(ngrp):
                for jl_ in range(gsz[gi]):
                    j2g[gst[gi] + jl_] = (gi, jl_)

            def ffn_pre():
                pass

            def mk_up(j, which, t=t):
                g, jl = j2g[j]
                ab = (g % 2) * GRP + jl
                fs = (j % 2) * 4
                ci = j + which * NFF

                def post(bank):
                    xs = Ft[fs + which]
                    y = Ft[fs + 2 + which]
                    P.op("dve", lambda e: e.tensor_copy(out=xs[:, 0:2], in_=fc[:, ci, :]), reads=["fc"], writes=[f"F{fs + which}"])
                    P.op("dve", lambda e: e.tensor_tensor(out=xs[:, 2:2 + T], in0=G[bank][:], in1=rstd2[:], op=ALU.mult),
                         reads=[("G", bank), "rstd2"], writes=[f"F{fs + which}"])
                    P.op("dve", lambda e: e.tensor_copy(out=fc[:, ci, :], in_=xs[:, T:T + 2]), reads=[f"F{fs + which}"], writes=["fc"])
                    P.op("dve", lambda e: e.tensor_scalar(out=y[:, 0:T], in0=xs[:, 2:2 + T], scalar1=vcol(V_FCW + 2 * 172 + ci),
                                                          scalar2=vcol(V_FCB + ci), op0=ALU.mult, op1=ALU.add),
                         reads=[f"F{fs + which}", "vecs"], writes=[f"F{fs + 2 + which}"])
                    for jj in (1, 0):
                        P.op("dve", lambda e, jj=jj: e.scalar_tensor_tensor(out=y[:, 0:T], in0=xs[:, jj:jj + T],
                                                                            scalar=vcol(V_FCW + jj * 172 + ci), in1=y[:, 0:T],
                                                                            op0=ALU.mult, op1=ALU.add),
                             reads=[f"F{fs + which}", f"F{fs + 2 + which}", "vecs"], writes=[f"F{fs + 2 + which}"])
                    if which == 0:
                        P.op("act", lambda e: e.activation(out=y[:, 0:T], in_=y[:, 0:T], func=AF.Silu),
                             reads=[f"F{fs + 2}"], writes=[f"F{fs + 2}"])
                    else:
                        P.op("dve", lambda e: e.tensor_tensor(out=mix[:, ab, :], in0=Ft[fs + 2][:, 0:T], in1=y[:, 0:T], op=ALU.mult),
                             reads=[f"F{fs + 2}", f"F{fs + 3}"], writes=[("mix", ab)])
                return dict(src=(wupg_d if which == 0 else wupv_d)[j].rearrange("p (k m) -> p k m", m=128), nk=KC, rhs=rhs_hn, post=post,
                            nosig=(which == 0), ring4=True)

            fsq = {}

            def fin_ssq(mm):
                s_ = fsq[mm]
                P.op("pe", lambda e: e.matmul(pST[:], lhsT=ones, rhs=SQ[s_][:], start=(mm == 0), stop=(mm == KC - 1)),
                     reads=[("SQ", s_), "cstb"], writes=["pST"])

            def mk_dn2(g, m):
                k0 = gst[g]
                nk = gsz[g]
                base = (g % 2) * GRP

                def rhs(k):
                    return mix[:, base + k, :], ("mix", base + k)

                def mkpost(mm):
                    def post(bank):
                        last = (g == ngrp - 1)
                        if last and mm > 0:
                            fin_ssq(mm - 1)
                        P.op("dve", lambda e: e.tensor_tensor(out=h[:, mm, :], in0=h[:, mm, :], in1=G[bank][:], op=ALU.add),
                             reads=[("h", mm), ("G", bank)], writes=[("h", mm)])
                        if last:
                            s_ = sqc[0] % 2
                            sqc[0] += 1
                            fsq[mm] = s_
                            P.op("act", lambda e: e.activation(out=SQ[s_][:], in_=h[:, mm, :], func=AF.Square),
                                 reads=[("h", mm)], writes=[("SQ", s_)])
                    return post

                def dma(e, wbuf):
                    return e.dma_start(out=wbuf[:, 0:2 * nk, :].rearrange("p (a k) m -> p a k m", a=2),
                                       in_=wdn_d[m:m + 2][:, :, k0 * 128:(k0 + nk) * 128].rearrange("a p (k m) -> p a k m", m=128))
                return dict(dma=dma, ring4=True, subs=[dict(k0=0, nk=nk, rhs=rhs, post=mkpost(m), nosig=True),
                                           dict(k0=nk, nk=nk, rhs=rhs, post=mkpost(m + 1))])

            first = True
            for g in range(ngrp):
                js = list(range(gst[g], gst[g] + gsz[g]))
                for idx, j in enumerate(js):
                    bq = mk_up(j, 0)
                    if first:
                        bq["pre"] = ffn_pre
                        first = False
                    blocks.append(bq)
                    blocks.append(mk_up(j, 1))
                    if idx == 1 and g > 0:
                        for m in range(0, KC, 2):
                            blocks.append(mk_dn2(g - 1, m))
            for m in range(0, KC, 2):
                blocks.append(mk_dn2(ngrp - 1, m))

            def tile_end(t0=t0, fin_ssq=fin_ssq, t=t):
                fin_ssq(KC - 1)
                if t + 1 < ntiles:
                    pe_filler(48)
                P.op("act", lambda e: e.activation(out=Ft[2][:, 0:T], in_=pST[:], func=AF.Ln, scale=1.0 / D, bias=EPS),
                     reads=["pST"], writes=["F2"])
                P.op("act", lambda e: e.activation(out=Ft[4][:, 0:T], in_=Ft[2][:, 0:T], func=AF.Exp, scale=-0.5),
                     reads=["F2"], writes=["F4"])
                for c in range(KC):
                    P.op("dve", lambda e, c=c: e.scalar_tensor_tensor(out=h[:, c, :], in0=h[:, c, :], scalar=vcol(V_FIN + c),
                                                                     in1=Ft[4][:, 0:T], op0=ALU.mult, op1=ALU.mult),
                         reads=[("h", c), "F4", "vecs"], writes=[("h", c)])
                    if c % 4 == 3:
                        g8 = c // 4
                        P.op("sp", lambda e, g8=g8: e.dma_start(out=out_d[:, 4 * g8:4 * g8 + 4, t0:t0 + T], in_=h[:, 4 * g8:4 * g8 + 4, :]),
                             reads=[("h", cc) for cc in range(4 * g8, 4 * g8 + 4)], dma_sem=f"D_o{g8}")
            blocks[-1]["post_tile"] = tile_end

        for b in blocks:
            if "post_tile" in b:
                sub = b["subs"][-1]
                p0, p1 = sub["post"], b["post_tile"]
                sub["post"] = (lambda bank, p0=p0, p1=p1: (p0(bank), p1()))

        if maxblocks is not None:
            del blocks[maxblocks:]
        run_blocks()
        pipe.drain()
        P.final_waits("sp")
        sems = {n: st.enter_context(nc.semaphore(n)) for n in sorted(P.cnt.keys())}
        P.emit(nc, sems)
    return nc


def _colmajor(v, n):
    return np.ascontiguousarray(np.asarray(v, dtype=np.float32).reshape(n, 128).T)


def _wblocks(w, nblk):
    K = w.shape[0]
    kc = K // 128
    return np.ascontiguousarray(w.reshape(kc, 128, nblk, 128).transpose(2, 1, 0, 3)).reshape(nblk, 128, kc * 128)


def _prep_shared(inp):
    vec = np.zeros((128, NV), np.float32)
    vec[:, V_LN1:V_LN1 + 32] = _colmajor(inp["ln1_w"][0], 32)
    vec[:, V_LN2:V_LN2 + 32] = _colmajor(inp["ln2_w"][0], 32)
    vec[:, V_FIN:V_FIN + 32] = _colmajor(inp["final_norm_w"], 32)
    vec[:, V_G0:V_G0 + 16] = _colmajor(inp["lb_gamma"][0], 16)
    vec[:, V_G1:V_G1 + 16] = _colmajor(inp["lb_gamma"][1], 16)
    vec[:, V_HGW:V_HGW + 16] = _colmajor(inp["hg_norm_w"][0], 16)
    for j in range(4):
        vec[:, V_CW + j * 16:V_CW + (j + 1) * 16] = _colmajor(inp["lru_conv_w"][0, j], 16)
    vec[:, V_CB:V_CB + 16] = _colmajor(inp["lru_conv_b"][0], 16)
    vec[:, V_BA:V_BA + 16] = _colmajor(inp["lru_ba"][0], 16)
    vec[:, V_BX:V_BX + 16] = _colmajor(inp["lru_bx"][0], 16)
    vec[:, V_LAM:V_LAM + 16] = _colmajor(inp["lru_lambda"][0], 16)
    vec[:, V_LRUW:V_LRUW + 16] = _colmajor(inp["lru_norm_w"][0], 16)
    for j in range(3):
        vec[:, V_FCW + j * 172:V_FCW + (j + 1) * 172] = _colmajor(inp["ffn_conv_w"][0, j], 172)
    vec[:, V_FCB:V_FCB + 172] = _colmajor(inp["ffn_conv_b"][0], 172)

    cstb = np.zeros((128, 256), np.float32)
    cstb[:, 0:128] = np.eye(128, dtype=np.float32)
    cstb[:, 128:256] = 1.0
    cstb = cstb.astype(ml_dtypes.bfloat16)
    cstf = np.ones((128, 1024), np.float32)
    cstf[:, 0:512:64] = 0.0
    s = np.arange(128)[:, None]
    tt = np.arange(128)[None, :]
    pm = ((s // 64 == tt // 64) & (s <= tt)).astype(np.float32)
    cstf[:, 512:1024] = np.tile(pm, (1, 4))

    wa = np.asarray(inp["lru_wa"][0], np.float32)
    wx = np.asarray(inp["lru_wx"][0], np.float32)
    wlru = np.ascontiguousarray(np.stack([wa, wx], axis=2)).reshape(NL, 128, 256)
    shared = {
        "w_in": _wblocks(np.asarray(inp["w_in"][0], np.float32), 96),
        "w_out": _wblocks(np.asarray(inp["w_out"][0], np.float32), 32),
        "w_upg": _wblocks(np.asarray(inp["ffn_w_up"][0][:, :DFF], np.float32), NFF),
        "w_upv": _wblocks(np.asarray(inp["ffn_w_up"][0][:, DFF:], np.float32), NFF),
        "w_dn": _wblocks(np.asarray(inp["ffn_w_down"][0], np.float32), 32),
        "w_lru": wlru,
        "vecs": vec,
        "cstb": cstb,
        "cstf": cstf,
    }
    return shared


def kernel(**inputs):
    x = np.asarray(inputs["x"], np.float32)
    B = x.shape[0]
    shared = _prep_shared(inputs)
    nc = build_program()
    in_maps = []
    for b in range(B):
        xf = np.ascontiguousarray(x[b].T.reshape(KC, 128, SEQ).transpose(1, 0, 2))
        m = dict(shared)
        m["x"] = xf
        in_maps.append(m)
    res = run_bass_kernel_spmd(nc, in_maps, core_ids=list(range(B)))
    outs = []
    for b in range(B):
        o = np.asarray(res.results[b]["out"], np.float32)
        outs.append(np.ascontiguousarray(o.transpose(1, 0, 2).reshape(D, SEQ).T))
    return np.stack(outs, axis=0)
```
